# Optimizing a Trainium2 kernel written in Bass

```python
import math
import jax, jax.numpy as jnp
from jax import lax
import numpy as np

D_MODEL = 1024
BATCH = 2
SEQ = 16384
DEPTH = 4
DEC_BATCH = 8
DEC_SEQ = 8192
PAST_LEN = 128

N_META = 16
BLOCK = 128
WINDOW = 128
PAD = BLOCK - N_META
HEAD_DIM = 64
ATT_HEADS = 8
ATT_KV_HEADS = 2
RET_HEADS = 4
MLSTM_HEADS = 4
ATT_W = ATT_HEADS * HEAD_DIM
KV_W = ATT_KV_HEADS * HEAD_DIM
RET_W = RET_HEADS * HEAD_DIM
MLSTM_W = MLSTM_HEADS * HEAD_DIM
MIX_W = ATT_W + RET_W + MLSTM_W
N_GATES = 4 * MLSTM_HEADS
PROJ_SIZES = (ATT_W, KV_W, KV_W, RET_W, RET_W, RET_W, RET_W, MLSTM_W, MLSTM_W, MLSTM_W, MLSTM_W, N_GATES)
PROJ_W = ATT_W + 2 * KV_W + 4 * RET_W + 4 * MLSTM_W + N_GATES
D_FF = 2816
CONV_W = 3
EPS = 1e-6

kernel_name = 'hymba_style_bidir_hybrid_encoder'

f32 = jnp.float32


def rms_norm(x, w):
    xf = x.astype(f32)
    y = xf * lax.rsqrt(jnp.mean(xf * xf, axis=-1, keepdims=True) + EPS)
    return (y * w.astype(f32)).astype(x.dtype)


def head_norm(x, w):
    B, H, Lp, d = x.shape
    mu = jnp.mean(x, axis=-1, keepdims=True)
    xc = x - mu
    y = xc * lax.rsqrt(jnp.mean(xc * xc, axis=-1, keepdims=True) + EPS)
    return y.transpose(0, 2, 1, 3).reshape(B, Lp, H * d) * w.astype(f32)


def dwconv3(x, w):
    xp = jnp.pad(x, ((0, 0), (1, 1), (0, 0)))
    return xp[:, :-2] * w[0] + xp[:, 1:-1] * w[1] + xp[:, 2:] * w[2]


def _flip(a):
    return jnp.flip(a, axis=2)


def _neighbour_blocks(a, nb):
    B = a.shape[0]
    ab = a.reshape((B, nb, BLOCK) + a.shape[2:])
    z = jnp.zeros_like(ab[:, :1])
    ap = jnp.concatenate([z, ab, z], axis=1)
    return jnp.concatenate([ap[:, :-2], ap[:, 1:-1], ap[:, 2:]], axis=2)


def windowed_attention(q, k, v, sink):
    B, Lp, H, dh = q.shape
    Hkv = k.shape[2]
    G = H // Hkv
    nb = Lp // BLOCK
    scale = dh ** -0.5
    slopes = jnp.exp2(-8.0 * jnp.arange(1, H + 1, dtype=f32) / H).reshape(Hkv, G)
    qb = q.reshape(B, nb, BLOCK, Hkv, G, dh)
    kb = _neighbour_blocks(k, nb)
    vb = _neighbour_blocks(v, nb)
    qpos = jnp.arange(Lp).reshape(nb, BLOCK)
    kpos = (jnp.arange(nb)[:, None] - 1) * BLOCK + jnp.arange(3 * BLOCK)[None, :]
    dist = jnp.abs(qpos[:, :, None] - kpos[:, None, :])
    band = (dist <= WINDOW) & (kpos[:, None, :] >= BLOCK) & (kpos[:, None, :] < Lp)
    s_band = jnp.einsum('bnqkgd,bnskd->bnkgqs', qb, kb, preferred_element_type=f32) * scale
    s_band = s_band - slopes[:, :, None, None] * dist[:, None, None].astype(f32)
    s_band = jnp.where(band[:, None, None], s_band, -jnp.inf)
    km = k[:, PAD:BLOCK]
    vm = v[:, PAD:BLOCK]
    s_meta = jnp.einsum('bnqkgd,bmkd->bnkgqm', qb, km, preferred_element_type=f32) * scale
    sink_l = sink.astype(f32).reshape(Hkv, G)[:, :, None]
    mx = jnp.maximum(jnp.maximum(jnp.max(s_band, axis=-1), jnp.max(s_meta, axis=-1)), sink_l)
    p_band = jnp.exp(s_band - mx[..., None])
    p_meta = jnp.exp(s_meta - mx[..., None])
    denom = jnp.sum(p_band, axis=-1) + jnp.sum(p_meta, axis=-1) + jnp.exp(sink_l - mx)
    out = (jnp.einsum('bnkgqs,bnskd->bnqkgd', p_band.astype(v.dtype), vb, preferred_element_type=f32)
           + jnp.einsum('bnkgqm,bmkd->bnqkgd', p_meta.astype(v.dtype), vm, preferred_element_type=f32))
    out = out / jnp.moveaxis(denom, 4, 2)[..., None]
    return out.reshape(B, Lp, H * dh).astype(q.dtype)


def _retention_scan(q, k, v, log_gamma):
    B, H, Lp, d = q.shape
    nc = Lp // BLOCK
    q = q.reshape(B, H, nc, BLOCK, d)
    k = k.reshape(B, H, nc, BLOCK, d)
    v = v.reshape(B, H, nc, BLOCK, d)
    idx = jnp.arange(BLOCK, dtype=f32)
    diff = idx[:, None] - idx[None, :]
    decay = jnp.where(diff >= 0, jnp.exp(log_gamma[:, None, None] * jnp.maximum(diff, 0.0)), 0.0)
    scores = jnp.einsum('bhnqd,bhnsd->bhnqs', q, k) * decay[:, None]
    o = jnp.einsum('bhnqs,bhnse->bhnqe', scores, v)
    lg = log_gamma[:, None]
    k_end = k * jnp.exp(lg * (BLOCK - 1 - idx))[:, None, :, None]
    chunk_kv = jnp.einsum('bhnsd,bhnse->nbhde', k_end, v)
    gamma_blk = jnp.exp(log_gamma * BLOCK)[:, None, None]

    def step(R, kv):
        return R * gamma_blk + kv, R

    _, R_prev = lax.scan(step, jnp.zeros_like(chunk_kv[0]), chunk_kv)
    q_dec = q * jnp.exp(lg * (idx + 1.0))[:, None, :, None]
    o = o + jnp.einsum('bhnqd,nbhde->bhnqe', q_dec, R_prev)
    return o.reshape(B, H, Lp, d)


def _mlstm_scan(q, k, v, log_i, log_f):
    B, H, Lp, d = q.shape
    nc = Lp // BLOCK
    q = q.reshape(B, H, nc, BLOCK, d)
    k = k.reshape(B, H, nc, BLOCK, d)
    v = v.reshape(B, H, nc, BLOCK, d)
    li = log_i.reshape(B, H, nc, BLOCK)
    b = jnp.cumsum(log_f.reshape(B, H, nc, BLOCK), axis=-1)
    tril = jnp.tril(jnp.ones((BLOCK, BLOCK), dtype=bool))
    d_log = jnp.where(tril, b[..., :, None] - b[..., None, :] + li[..., None, :], -jnp.inf)
    b_last = b[..., -1]
    w_end = b_last[..., None] - b + li
    m_loc = jnp.max(w_end, axis=-1)
    e_end = jnp.exp(w_end - m_loc[..., None])
    kv_loc = jnp.einsum('bhnc,bhncd,bhnce->nbhde', e_end, k, v)
    n_loc = jnp.einsum('bhnc,bhncd->nbhd', e_end, k)

    def step(carry, inp):
        S, nv, m = carry
        kv_c, n_c, m_c, bl = inp
        m_new = jnp.maximum(bl + m, m_c)
        a = jnp.exp(bl + m - m_new)
        c = jnp.exp(m_c - m_new)
        S_new = a[..., None, None] * S + c[..., None, None] * kv_c
        n_new = a[..., None] * nv + c[..., None] * n_c
        return (S_new, n_new, m_new), (S, nv, m)

    init = (jnp.zeros((B, H, d, d), f32), jnp.zeros((B, H, d), f32), jnp.zeros((B, H), f32))
    xs = (kv_loc, n_loc, jnp.moveaxis(m_loc, 2, 0), jnp.moveaxis(b_last, 2, 0))
    _, (S_prev, n_prev, m_prev) = lax.scan(step, init, xs)
    inter = b + jnp.moveaxis(m_prev, 0, 2)[..., None]
    m_t = jnp.maximum(jnp.max(d_log, axis=-1), inter)
    a_inter = jnp.exp(inter - m_t)
    w = jnp.exp(d_log - m_t[..., None]) * jnp.einsum('bhnqd,bhnsd->bhnqs', q, k)
    num = jnp.einsum('bhnqs,bhnse->bhnqe', w, v) + a_inter[..., None] * jnp.einsum('bhnqd,nbhde->bhnqe', q, S_prev)
    den = jnp.sum(w, axis=-1) + a_inter * jnp.einsum('bhnqd,nbhd->bhnq', q, n_prev)
    h = num / jnp.maximum(jnp.abs(den), jnp.exp(-m_t))[..., None]
    return h.reshape(B, H, Lp, d)


def hybrid_mixer(h, w_in, q_norm_w, k_norm_w, sink, ret_decay_logit, ret_norm_w,
                 mlstm_conv_w, mlstm_gate_b, mlstm_norm_w, w_out):
    B, L, _ = h.shape
    Lp = L + PAD
    proj = jnp.pad(h @ w_in, ((0, 0), (PAD, 0), (0, 0)))
    splits = np.cumsum(PROJ_SIZES)[:-1].tolist()
    aq, ak, av, rq, rk, rv, rg, mq, mk, mv, mo, mgate = jnp.split(proj, splits, axis=-1)

    def heads(a, n):
        return a.reshape(B, Lp, n, HEAD_DIM).transpose(0, 2, 1, 3).astype(f32)

    aq = rms_norm(aq.reshape(B, Lp, ATT_HEADS, HEAD_DIM), q_norm_w)
    ak = rms_norm(ak.reshape(B, Lp, ATT_KV_HEADS, HEAD_DIM), k_norm_w)
    av = av.reshape(B, Lp, ATT_KV_HEADS, HEAD_DIM)
    att = windowed_attention(aq, ak, av, sink)

    rq_h = heads(rq, RET_HEADS)
    rk_h = heads(rk, RET_HEADS) * (HEAD_DIM ** -0.5)
    rv_h = heads(rv, RET_HEADS)
    log_gamma = jax.nn.log_sigmoid(ret_decay_logit.astype(f32))
    ro = (_retention_scan(rq_h, rk_h, rv_h, log_gamma[0])
          + _flip(_retention_scan(_flip(rq_h), _flip(rk_h), _flip(rv_h), log_gamma[1])))
    ret = (jax.nn.silu(rg.astype(f32)) * head_norm(ro, ret_norm_w)).astype(h.dtype)

    mqk = jax.nn.silu(dwconv3(jnp.concatenate([mq, mk], axis=-1), mlstm_conv_w))
    mq, mk = jnp.split(mqk, 2, axis=-1)
    not_pad = (jnp.arange(Lp) >= PAD)[None, :, None]
    mk = jnp.where(not_pad, mk, jnp.zeros_like(mk))
    g = (mgate.astype(f32) + mlstm_gate_b.astype(f32)).reshape(B, Lp, 4, MLSTM_HEADS).transpose(2, 0, 3, 1)
    mq_h = heads(mq, MLSTM_HEADS)
    mk_h = heads(mk, MLSTM_HEADS) * (HEAD_DIM ** -0.5)
    mv_h = heads(mv, MLSTM_HEADS)
    hm = (_mlstm_scan(mq_h, mk_h, mv_h, g[0], jax.nn.log_sigmoid(g[1]))
          + _flip(_mlstm_scan(_flip(mq_h), _flip(mk_h), _flip(mv_h), _flip(g[2]), _flip(jax.nn.log_sigmoid(g[3])))))
    ml = (jax.nn.sigmoid(mo.astype(f32)) * head_norm(hm, mlstm_norm_w)).astype(h.dtype)

    mixed = jnp.concatenate([att, ret, ml], axis=-1)[:, PAD:]
    return mixed @ w_out


def conv_ffn(h, w_up, conv_w, w_down):
    u = dwconv3(h @ w_up, conv_w)
    gate, val = jnp.split(u, 2, axis=-1)
    return (jax.nn.silu(gate) * val) @ w_down


def trunk(x, meta_tokens, norm1_w, w_in, attn_q_norm_w, attn_k_norm_w, attn_sink, ret_decay_logit,
          ret_norm_w, mlstm_conv_w, mlstm_gate_b, mlstm_norm_w, w_out, norm2_w, ffn_up, ffn_conv_w, ffn_down):
    B = x.shape[0]
    meta = jnp.broadcast_to(meta_tokens[None].astype(x.dtype), (B, N_META, x.shape[-1]))
    h = jnp.concatenate([meta, x], axis=1)
    for l in range(DEPTH):
        h = h + hybrid_mixer(rms_norm(h, norm1_w[l]), w_in[l], attn_q_norm_w[l], attn_k_norm_w[l],
                             attn_sink[l], ret_decay_logit[l], ret_norm_w[l], mlstm_conv_w[l],
                             mlstm_gate_b[l], mlstm_norm_w[l], w_out[l])
        h = h + conv_ffn(rms_norm(h, norm2_w[l]), ffn_up[l], ffn_conv_w[l], ffn_down[l])
    return h[:, N_META:]


def setup_inputs(seed: int = 0) -> dict:
    key = jax.random.key(seed)
    ks = jax.random.split(key, 20)

    def nrm(k, shape, s):
        return jax.random.normal(k, shape, f32) * s

    gamma0 = 1.0 - 2.0 ** (-5.0 - np.arange(RET_HEADS))
    logit0 = (np.log(gamma0) - np.log1p(-gamma0)).astype(np.float32)
    f_bias = np.linspace(3.0, 6.0, MLSTM_HEADS)
    zeros_h = np.zeros(MLSTM_HEADS)
    gate_base = np.concatenate([zeros_h, f_bias, zeros_h, f_bias]).astype(np.float32)
    return {
        'x_prompt': nrm(ks[0], (BATCH, SEQ, D_MODEL), 1.0),
        'x_sample': nrm(ks[1], (DEC_BATCH, DEC_SEQ, D_MODEL), 1.0),
        'meta_tokens': nrm(ks[2], (N_META, D_MODEL), 1.0),
        'norm1_w': 1.0 + nrm(ks[3], (DEPTH, D_MODEL), 0.02),
        'w_in': nrm(ks[4], (DEPTH, D_MODEL, PROJ_W), D_MODEL ** -0.5),
        'attn_q_norm_w': 1.0 + nrm(ks[5], (DEPTH, HEAD_DIM), 0.02),
        'attn_k_norm_w': 1.0 + nrm(ks[6], (DEPTH, HEAD_DIM), 0.02),
        'attn_sink': nrm(ks[7], (DEPTH, ATT_HEADS), 0.5),
        'ret_decay_logit': jnp.asarray(logit0)[None, None, :] + nrm(ks[8], (DEPTH, 2, RET_HEADS), 0.1),
        'ret_norm_w': 1.0 + nrm(ks[9], (DEPTH, RET_W), 0.02),
        'mlstm_conv_w': nrm(ks[10], (DEPTH, CONV_W, 2 * MLSTM_W), CONV_W ** -0.5),
        'mlstm_gate_b': jnp.asarray(gate_base)[None, :] + nrm(ks[11], (DEPTH, N_GATES), 0.1),
        'mlstm_norm_w': 1.0 + nrm(ks[12], (DEPTH, MLSTM_W), 0.02),
        'w_out': nrm(ks[13], (DEPTH, MIX_W, D_MODEL), MIX_W ** -0.5),
        'norm2_w': 1.0 + nrm(ks[14], (DEPTH, D_MODEL), 0.02),
        'ffn_up': nrm(ks[15], (DEPTH, D_MODEL, 2 * D_FF), D_MODEL ** -0.5),
        'ffn_conv_w': nrm(ks[16], (DEPTH, CONV_W, 2 * D_FF), CONV_W ** -0.5),
        'ffn_down': nrm(ks[17], (DEPTH, D_FF, D_MODEL), D_FF ** -0.5),
    }


def reference(x_prompt, x_sample, meta_tokens, norm1_w, w_in, attn_q_norm_w, attn_k_norm_w, attn_sink,
              ret_decay_logit, ret_norm_w, mlstm_conv_w, mlstm_gate_b, mlstm_norm_w, w_out, norm2_w,
              ffn_up, ffn_conv_w, ffn_down):
    y_prompt = trunk(x_prompt, meta_tokens, norm1_w, w_in, attn_q_norm_w, attn_k_norm_w, attn_sink,
                     ret_decay_logit, ret_norm_w, mlstm_conv_w, mlstm_gate_b, mlstm_norm_w, w_out,
                     norm2_w, ffn_up, ffn_conv_w, ffn_down)
    y_sample = trunk(x_sample, meta_tokens, norm1_w, w_in, attn_q_norm_w, attn_k_norm_w, attn_sink,
                     ret_decay_logit, ret_norm_w, mlstm_conv_w, mlstm_gate_b, mlstm_norm_w, w_out,
                     norm2_w, ffn_up, ffn_conv_w, ffn_down)
    return (y_prompt, y_sample)
```

```python
import numpy as np
import concourse.bass as bass
import concourse.mybir as mybir
from concourse.bass_utils import run_bass_kernel_spmd

F32 = mybir.dt.float32
BF16 = mybir.dt.bfloat16
AF = mybir.ActivationFunctionType
ALU = mybir.AluOpType
AX = mybir.AxisListType

D = 1024
PW = 2832
DFF = 2816
EPS = 1e-6
OFF = 1 << 16
import os
SES_SET = set(os.environ.get("SES", "act,dve,pool").split(","))

P64 = dict(AQT=(0, 1024), AKT=(1024, 1280), RQT=(1280, 1792), RKT=(1792, 2304), MQT=(2304, 2816),
           MKT=(2816, 3328), RF=(3328, 3584), SF=(3584, 3844), MKM=(3844, 3876))
P64W = 3876
P128 = dict(AV=(0, 130), RV=(130, 386), MVA=(386, 646))
P128W = 646
F128 = dict(RG=(0, 256), MO=(256, 512), GT=(512, 536))
F128W = 536
C_ID, C_TF, C_TB, C_DF, C_DB, C_DIST, C_RIDX, C_SEL, C_W = 0, 128, 256, 384, 512, 640, 1024, 1028, 1044


class T:
    def __init__(self, name, ap):
        self.name = name
        self.ap = ap
        self.w = None
        self.r = {}
        self.dsem = None
        self.dloc = 0

    def __getitem__(self, k):
        return self.ap[k]


CUR = [0]


class Dual:
    def __init__(self, a, b):
        self.s = (a, b)

    def cur(self):
        return self.s[CUR[0]]

    @property
    def ap(self):
        return self.cur().ap

    def __getitem__(self, k):
        return self.cur().ap[k]


def _res(ts):
    return [t.cur() if isinstance(t, Dual) else t for t in ts]


class Bld:
    def __init__(self, tables=None):
        self.nc = bass.Bass("TRN2", target_bir_lowering=False)
        nc = self.nc
        self.eng = {'pe': nc.tensor, 'act': nc.scalar, 'dve': nc.vector, 'pool': nc.gpsimd, 'sp': nc.sync}
        self.sem = {e: nc.alloc_semaphore('s_' + e) for e in self.eng}
        self.loc = {e: 0 for e in self.eng}
        self.tiles = {}
        self.waited = {e: {} for e in self.eng}
        self.common = [0, 0]
        self.nwait = 0
        self.rec = None

    def sb(self, name, shape, dt=F32, dma=False, reg=None):
        reg = reg or self.common
        n = 1
        for x in shape[1:]:
            n *= x
        nbytes = n * (4 if dt == F32 else 2)
        off = reg[0]
        reg[0] = (off + nbytes + 63) // 64 * 64
        assert reg[0] <= reg[1], (name, reg)
        t = T(name, self.nc.alloc_sbuf_tensor_at(name, list(shape), dt, offset=off).ap())
        self.tiles[name] = t
        if dma:
            self.mkdma(t)
        return t

    def mkdma(self, t):
        t.dsem = self.nc.alloc_semaphore('d_' + t.name)

    def ps(self, name, shape=(128, 512)):
        t = T(name, self.nc.alloc_psum_tensor(name, list(shape), F32).ap())
        self.tiles[name] = t
        return t

    def _wait(self, e, k, pos):
        if k == e and e not in SES_SET:
            return
        if self.waited[e].get(k, -1) >= pos:
            return
        self.waited[e][k] = pos
        sem = self.tiles[k[1]].dsem if isinstance(k, tuple) else self.sem[k]
        self.eng[e].wait_ge(sem, pos)
        self.nwait += 1

    def _deps(self, e, r, w):
        for t in r:
            if t.w is not None:
                self._wait(e, *t.w)
        for t in w:
            if t.w is not None:
                self._wait(e, *t.w)
            for k, pos in list(t.r.items()):
                self._wait(e, k, pos)

    def sb2(self, name, shape, dt=F32, dma=False, reg=None, reg2=None):
        return Dual(self.sb(name + "_0", shape, dt, dma, reg), self.sb(name + "_1", shape, dt, dma, reg2 or reg))

    def play(self, items):
        for it in items:
            CUR[0] = it[1]
            if it[0] == 'op':
                self.op(*it[2:])
            else:
                self.dma(*it[2:])
        CUR[0] = 0

    def op(self, e, fn, r=(), w=()):
        r, w = _res(r), _res(w)
        if self.rec is not None:
            self.rec.append(('op', CUR[0], e, fn, r, w))
            return None
        self._deps(e, r, w)
        ins = fn(self.eng[e])
        ins.then_inc(self.sem[e], 1)
        self.loc[e] += 1
        pos = self.loc[e]
        for t in r:
            if t not in w:
                t.r[e] = pos
        for t in w:
            t.w = (e, pos)
            t.r = {}
        return ins

    def dma(self, out, in_, tw=(), tr=(), q='sp', slow=False):
        tw = _res(tw)
        tr = _res(tr)
        if self.rec is not None:
            self.rec.append(('dma', CUR[0], out, in_, tw, tr, q, slow))
            return
        self._deps(q, tr, tw)
        if slow:
            ins = self.eng[q].dma_start(out=out, in_=in_, allow_slow_non_contiguous=True)
        else:
            ins = self.eng[q].dma_start(out=out, in_=in_)
        tt = (tw + tr)[0]
        ins.then_inc(tt.dsem, 16)
        tt.dloc += 16
        pos = tt.dloc
        key = ('D', tt.name)
        for t in tw:
            t.w = (key, pos)
            t.r = {}
        for t in tr:
            t.r[key] = pos

    def seg_end(self):
        nc = self.nc
        used = []
        for e in self.eng:
            if self.loc[e]:
                if e != 'sp':
                    self.eng[e].wait_ge(self.sem[e], self.loc[e])
                used.append(self.sem[e])
        for t in self.tiles.values():
            if t.dsem is not None and t.dloc:
                nc.sync.wait_ge(t.dsem, t.dloc)
                used.append(t.dsem)
        if used:
            nc.all_engine_barrier()
            nums = [sm.num for sm in list(self.sem.values()) + [t.dsem for t in self.tiles.values() if t.dsem is not None]]
            nc.sync.sem_clear(range(min(nums), max(nums) + 1))
            nc.all_engine_barrier()
        for e in self.eng:
            self.loc[e] = 0
        for t in self.tiles.values():
            t.w = None
            t.r = {}
            t.dloc = 0
        self.waited = {e: {} for e in self.eng}

    def flush(self):
        self.seg_end()

    def loop(self, n, body):
        self.seg_end()
        with self.nc.Fori(0, n) as i:
            body(i)
            self.seg_end()


def bc(ap, shape):
    return ap.broadcast_to(list(shape))


def build(NB, depth):
    NBT = NB + 2
    NBA = NB + 6
    NI = (NBT + 2) // 2
    NJ = NBT // 2
    assert NBT % 2 == 0
    B = Bld()
    nc = B.nc
    dt_ = nc.dram_tensor

    SB = 128 * D
    LP = 16896
    LW1 = 128 * PW
    xin = dt_("xin", [NBA, 128, D], F32, kind="ExternalInput").ap()
    rowinfo = dt_("rowinfo", [NBA, 128, 4], F32, kind="ExternalInput").ap()
    rmrow = dt_("rmrow", [NBA, 1, 128], F32, kind="ExternalInput").ap()
    consts = dt_("consts", [128, C_W], F32, kind="ExternalInput").ap()
    psmall = dt_("psmall", [depth, LP], F32, kind="ExternalInput").ap()
    pn1 = dt_("pn1", [depth, LP], F32, kind="ExternalInput").ap()
    pn2 = dt_("pn2", [depth, LP], F32, kind="ExternalInput").ap()
    pcw = [dt_("pcw%d" % k, [depth, LP], F32, kind="ExternalInput").ap() for k in range(3)]
    pfc = [dt_("pfc%d" % k, [depth, LP], F32, kind="ExternalInput").ap() for k in range(3)]
    w_in_k = [dt_("w_in_k%d" % k, [depth, LW1], F32, kind="ExternalInput").ap() for k in range(8)]
    ffn_up_k = [[dt_("ffn_up_k%d_%d" % (k, hf), [depth, LW1], F32, kind="ExternalInput").ap() for hf in range(2)]
                for k in range(8)]
    w_out_k = [dt_("w_out_k%d" % k, [depth, SB], F32, kind="ExternalInput").ap() for k in range(8)]
    ffn_down_k = [dt_("ffn_down_k%d" % k, [depth, SB], F32, kind="ExternalInput").ap() for k in range(22)]
    Hh = dt_("hout", [NBA, 128, D], F32, kind="ExternalOutput")
    H = Hh.ap()
    Hf = H.rearrange("n p d -> n (p d)")

    def scratch(name, dt):
        return dt_(name, [NBA, SB], dt).ap()
    S64a, SAK, S64b, S64c = (scratch(n, BF16) for n in ("s64a", "sak", "s64b", "s64c"))
    SAV, S128b, SMV, SO64 = (scratch(n, BF16) for n in ("sav", "s128b", "smv", "so64"))
    SF128, SCMT, SROW, SB64, SMB, RI, RM = (scratch(n, F32) for n in ("sf128", "scmt", "srow", "sb64", "smb", "ri", "rm"))

    def bv(tb, rows, cols):
        return tb[0:rows * cols].rearrange("(p c) -> p c", c=cols)

    def bvb(tb, cols, parts):
        return bc(tb[0:cols].unsqueeze(0), [parts, cols])

    def pairs(T, off, n):
        v = T[off:off + 2 * n]
        if len(T.shape) == 2:
            return v.rearrange("(n u) s -> n u s", u=2)
        return v.rearrange("(n u) p d -> n u p d", u=2)

    BASE = 16512
    W1, W2, WST, GA, CM, TOP = (BASE + x for x in (0, 90112, 135168, 146496, 158784, 196608))
    B.common = [CM, TOP]
    rW1, rW2, rWS = [W1, W2], [W2, WST], [WST, GA]
    SH2 = W1 + 45312
    rS2 = [SH2, SH2 + 8768]
    rP = [SH2 + 8768, GA]
    rF1 = [W1, SH2]
    rF2 = [SH2 + 8768, W2]
    rG = [WST, CM]
    cst = B.sb("cst", [128, C_W], F32, dma=True)
    ident = cst[:, C_ID:C_ID + 128]
    triF = cst[:, C_TF:C_TF + 128]
    triB = cst[:, C_TB:C_TB + 128]
    ones = B.sb("ones", [128, 128])
    sel16 = B.sb("sel16", [128, 16], BF16)
    par = B.sb("par", [128, 1200], F32, dma=True)
    n1w = B.sb("n1w", [128, 8], F32, dma=True)
    n2w = B.sb("n2w", [128, 8], F32, dma=True)
    cw = B.sb("cw", [64, 3, 8], F32, dma=True)
    cwf = B.sb("cwf", [128, 3, 44], F32, dma=True)
    der = B.sb("der", [128, 64])
    DT = B.sb("DT", [128, 4, 128])
    dtmp = B.sb("dtmp", [128, 2, 128])
    wreg = B.sb("wreg", [128, 8 * 2 * DFF], BF16, reg=rW1)
    wdn = B.sb("wdn", [128, 22 * D], BF16, reg=rW2)
    wstage = B.sb("wstage", [128, PW], F32, dma=True, reg=rWS)
    Win = wreg.ap[:, 0:8 * PW].rearrange("p (k c) -> p k c", k=8)
    Wup = wreg.ap.rearrange("p (k c) -> p k c", k=8)
    Wdn = wdn.ap.rearrange("p (k c) -> p k c", k=22)
    Wout = wdn.ap[:, 0:8 * D].rearrange("p (k c) -> p k c", k=8)

    hA = B.sb2("hA", [128, D], F32, dma=True, reg=None, reg2=rS2)
    rinf = B.sb2("rinf", [128, 4], F32, dma=True, reg=None, reg2=rS2)
    rinfB = B.sb2("rinfB", [128, 4], F32, dma=True, reg=None, reg2=rS2)
    xn = B.sb2("xn", [128, D], F32, reg=None, reg2=rS2)
    junk = xn
    st8 = B.sb2("st8", [128, 8], reg=None, reg2=rS2)
    win = B.sb("win", [128, 8, 513], BF16)
    sm8 = B.sb2("sm8", [128, 64])
    Rf = B.sb("Rf", [64, 256])
    Sf = B.sb("Sf", [64, 260])
    mF = B.sb("mF", [128, 4])
    Rb = B.sb("Rb", [64, 256])
    Sb = B.sb("Sb", [64, 260])
    mB = B.sb("mB", [128, 4])
    metaK = B.sb("metaK", [64, 32])
    metaV = B.sb("metaV", [16, 130])
    PP = [B.ps("pp%d" % i) for i in range(5)]
    X = [B.ps("x%d" % i) for i in range(3)]
    Kp = [Dual(PP[j], (PP[4], X[0], X[1], X[2])[j]) for j in range(4)]
    p64 = B.sb2("p64", [64, P64W], BF16, dma=True, reg=rP)
    p128 = B.sb2("p128", [128, P128W], BF16, dma=True, reg=rP)
    f128 = B.sb2("f128", [128, F128W], F32, dma=True, reg=rP)

    def V64(n):
        a, b = P64[n]
        return p64.ap[:, a:b]

    def V128(n):
        a, b = P128[n]
        return p128.ap[:, a:b]

    def VF(n):
        a, b = F128[n]
        return f128.ap[:, a:b]
    sq = B.sb2("sq", [128, 640], reg=rP)
    hs = B.sb2("hs", [128, 16], reg=rP)
    qn = B.sb2("qn", [128, 640], reg=rP)
    vun = B.sb2("vun", [128, 2, 65], BF16, reg=rP)
    rqk = B.sb2("rqk", [128, 512], reg=rP)
    rkw = B.sb2("rkw", [128, 2, 256], BF16, reg=rP)
    mtmp = B.sb("mtmp", [64, 130], reg=rP)
    mvb = B.sb2("mvb", [16, 130], BF16, dma=True, reg=rP)
    rmr = B.sb2("rmr", [64, 128], F32, dma=True, reg=rP)
    cacc = B.sb2("cacc", [64, 8, 128], reg=rP)
    mkf = B.sb2("mkf", [64, 4, 128], reg=rP)
    mktok = B.sb2("mktok", [128, 256], reg=rP)
    mke = B.sb2("mke", [128, 2, 256], BF16, reg=rP)
    G = B.sb2("G", [128, 16], reg=rP)
    gl = B.sb2("gl", [128, 8], reg=rP)
    aT = B.sb2("aT", [4, 256], reg=rP)
    cmT = B.sb2("cmT", [4, 256], F32, dma=True, reg=rP)
    dg = B.sb2("dg", [4, 8], reg=rP)
    tot = B.sb2("tot", [128, 16], reg=rP)
    eend = B.sb2("eend", [128, 8], reg=rP)
    sc4 = B.sb("sc4", [128, 16], reg=rP)
    b64 = B.sb2("b64", [64, 516], F32, dma=True, reg=rP)
    srow = B.sb2("srow", [1, 16], F32, dma=True, reg=rP)
    stmp = B.sb("stmp", [64, 260], reg=rP)
    vmn = B.sb2("vmn", [16, 130], reg=rP)
    kvf = B.sb2("kvf", [64, 256], reg=rP)
    kvnf = B.sb2("kvnf", [64, 260], reg=rP)
    scb = B.sb("scb", [128, 16], F32, dma=True, reg=rP)
    b64B = B.sb("b64B", [64, 2, 516], F32, dma=True, reg=rP)
    scbB = B.sb("scbB", [128, 2, 12], F32, dma=True, reg=rP)
    rinfB2 = B.sb("rinfB2", [128, 2, 4], F32, dma=True, reg=rP)
    o64B = B.sb("o64B", [64, 2, 516], BF16, dma=True, reg=rP)
    mbrowB = B.sb("mbrowB", [1, 2, 4], F32, dma=True, reg=rP)
    o64 = B.sb("o64", [64, 516], BF16, dma=True, reg=rP)
    mbrow = B.sb("mbrow", [1, 4], F32, dma=True, reg=rP)
    kn = B.sb("kn", [64, 4, 256], BF16, dma=True, reg=rF1)
    vn = B.sb("vn", [128, 4, 130], BF16, dma=True, reg=rF1)
    q64 = B.sb2("q64", [64, P64W], BF16, dma=True, reg=rF1, reg2=rF2)
    q128 = B.sb2("q128", [128, P128W], BF16, dma=True, reg=rF1, reg2=rF2)
    g128 = B.sb2("g128", [128, F128W], F32, dma=True, reg=rF1, reg2=rF2)
    mv16 = B.sb2("mv16", [16, 130], BF16, dma=True, reg=rF1, reg2=rF2)
    cmb = B.sb2("cmb", [128, 1024], F32, dma=True, reg=rF1, reg2=rF2)
    mpv = B.sb2("mpv", [128, 8], F32, dma=True, reg=rF1, reg2=rF2)
    r64 = B.sb2("r64", [64, 516], BF16, dma=True, reg=rF1, reg2=rF2)
    mixed = B.sb2("mixed", [128, D], reg=rF1, reg2=rF2)
    stmpF = B.sb2("stmpF", [128, 512], reg=rF1, reg2=rF2)
    ptb = [B.sb2("ptb%d" % i, [128, 512], BF16, reg=rF1, reg2=rF2) for i in range(3)]
    pmb = B.sb2("pmb", [16, 512], BF16, reg=rF1, reg2=rF2)
    wtr = B.sb2("wtr", [128, 512], BF16, reg=rF1, reg2=rF2)
    wtm = B.sb2("wtm", [128, 1024], BF16, reg=rF1, reg2=rF2)
    oacc = B.sb2("oacc", [128, 520], reg=rF1, reg2=rF2)
    obuf = B.sb2("obuf", [128, 520], reg=rF1, reg2=rF2)
    mixT = B.sb2("mixT", [128, 8, 128], BF16, reg=rF1, reg2=rF1)
    K = [Dual(PP[j], (PP[4], X[0], X[1], X[2])[j]) for j in range(4)]
    rF3 = [W2 + 16384, WST]
    Eb = B.sb("Eb", [128, 6, 512], BF16, reg=rF3)
    actT = B.sb("actT", [128, 22, 256], BF16, reg=rG)
    cg = B.sb("cg", [128, 2, 256], reg=rG)
    sg = B.sb("sg", [128, 256], reg=rG)

    def E(e, fn, r=(), w=()):
        return B.op(e, fn, r, w)

    B.dma(cst.ap, consts, tw=[cst])
    E('dve', lambda e: e.memset(ones.ap, 1.0), w=[ones])
    E('dve', lambda e: e.tensor_copy(out=sel16.ap, in_=cst[:, C_SEL:C_SEL + 16]), r=[cst], w=[sel16])
    for c0 in range(0, NBA, 8):
        c1 = min(NBA, c0 + 8)
        B.dma(H[c0:c1], xin[c0:c1], tr=[cst])
    for c0 in range(NBA):
        B.dma(bv(RI[c0], 128, 4), rowinfo[c0], tr=[cst])
        B.dma(bv(RM[c0], 1, 128), rmrow[c0], tr=[cst])
        if c0 % 16 == 15:
            B.seg_end()
    B.flush()

    def load_weight(dst3, chunks, l, ncols, scale_t=None):
        wt = wdn if (dst3 is Wdn or dst3 is Wout) else wreg
        it = 0
        for k, pieces in enumerate(chunks):
            for (tk, c0, c1) in pieces:
                w_ = c1 - c0
                B.dma(wstage.ap[:, 0:w_], tk[l][0:128 * w_].rearrange("(p c) -> p c", c=w_), tw=[wstage])
                eng = 'dve' if it % 2 == 0 else 'pool'
                it += 1
                if scale_t is not None:
                    E(eng, lambda e, k=k, c0=c0, c1=c1, w_=w_: e.tensor_scalar(
                        out=dst3[:, k, c0:c1], in0=wstage.ap[:, 0:w_], scalar1=scale_t[:, k:k + 1], scalar2=None,
                        op0=ALU.mult), r=[wstage, scale_t], w=[wt])
                else:
                    E(eng, lambda e, k=k, c0=c0, c1=c1, w_=w_: e.tensor_copy(out=dst3[:, k, c0:c1], in_=wstage.ap[:, 0:w_]),
                      r=[wstage], w=[wt])

    def stageA(hsrc, rsrc, wc):
        B.dma(hA.ap, hsrc, tw=[hA])
        B.dma(rinf.ap, bv(rsrc, 128, 4), tw=[rinf])
        E('act', lambda e: e.activation(out=junk.ap, in_=hA.ap, func=AF.Square, accum_out=st8[:, 0:1]),
          r=[hA], w=[junk, st8])
        E('dve', lambda e: e.tensor_scalar(out=st8[:, 1:2], in0=st8[:, 0:1], scalar1=1.0 / D, scalar2=EPS,
                                           op0=ALU.mult, op1=ALU.add), r=[st8], w=[st8])
        E('act', lambda e: e.activation(out=st8[:, 3:4], in_=st8[:, 1:2], func=AF.Sqrt), r=[st8], w=[st8])
        E('dve', lambda e: e.reciprocal(out=st8[:, 4:5], in_=st8[:, 3:4]), r=[st8], w=[st8])
        E('dve', lambda e: e.tensor_tensor(out=st8[:, 2:3], in0=st8[:, 4:5], in1=rinf[:, 0:1], op=ALU.mult), r=[st8, rinf], w=[st8])
        E('dve', lambda e: e.tensor_scalar(out=xn.ap, in0=hA.ap, scalar1=st8[:, 2:3], scalar2=None, op0=ALU.mult),
          r=[hA, st8], w=[xn])
        for half in range(2):
            for kk in range(4):
                k = half * 4 + kk
                E('pe', lambda e, k=k, kk=kk, half=half: e.transpose(Kp[2 + half][:, kk * 128:(kk + 1) * 128],
                                                                      xn[:, k * 128:(k + 1) * 128], ident),
                  r=[xn, cst], w=[Kp[2 + half]])
            E('act', lambda e, half=half: e.activation(
                out=win[:, half * 4:half * 4 + 4, wc:wc + 128],
                in_=Kp[2 + half].ap.rearrange("p (k c) -> p k c", k=4), func=AF.Copy),
              r=[Kp[2 + half]], w=[win])

    def shift_win(n=128):
        E('pool', lambda e: e.tensor_copy(out=win[:, :, 0:1], in_=win[:, :, n:n + 1]), r=[win], w=[win])
        E('pool', lambda e: e.tensor_copy(out=win[:, :, 1:n + 1], in_=win[:, :, n + 1:2 * n + 1]), r=[win], w=[win])

    def headnorm(src_ap, dst_ap, w_ap, gate_ap, rl, wl):
        E('dve', lambda e: e.tensor_reduce(out=sm8[:, 0:4], in_=src_ap, axis=AX.X, op=ALU.add), r=rl, w=[sm8])
        E('dve', lambda e: e.tensor_scalar(out=sm8[:, 4:8], in0=sm8[:, 0:4], scalar1=-1.0 / 64, scalar2=None,
                                           op0=ALU.mult), r=[sm8], w=[sm8])
        E('dve', lambda e: e.tensor_tensor(out=src_ap, in0=src_ap, in1=bc(sm8[:, 4:8].unsqueeze(2), [128, 4, 64]),
                                           op=ALU.add), r=rl + [sm8], w=rl)
        j3 = junk[:, 0:256].rearrange("p (h d) -> p h d", h=4)
        E('act', lambda e: e.activation(out=j3, in_=src_ap, func=AF.Square), r=rl, w=[junk])
        E('dve', lambda e: e.tensor_reduce(out=sm8[:, 8:12], in_=j3, axis=AX.X, op=ALU.add), r=[junk], w=[sm8])
        E('dve', lambda e: e.tensor_scalar(out=sm8[:, 12:16], in0=sm8[:, 8:12], scalar1=1.0 / 64, scalar2=EPS,
                                           op0=ALU.mult, op1=ALU.add), r=[sm8], w=[sm8])
        E('act', lambda e: e.activation(out=sm8[:, 12:16], in_=sm8[:, 12:16], func=AF.Sqrt), r=[sm8], w=[sm8])
        E('dve', lambda e: e.reciprocal(out=sm8[:, 16:20], in_=sm8[:, 12:16]), r=[sm8], w=[sm8])
        E('dve', lambda e: e.tensor_tensor(out=src_ap, in0=src_ap, in1=bc(sm8[:, 16:20].unsqueeze(2), [128, 4, 64]),
                                           op=ALU.mult), r=rl + [sm8], w=rl)
        E('dve', lambda e: e.tensor_tensor(out=src_ap, in0=src_ap, in1=w_ap, op=ALU.mult), r=rl + [par], w=rl)
        E('dve', lambda e: e.tensor_tensor(out=dst_ap, in0=src_ap, in1=gate_ap, op=ALU.mult), r=rl + [g128], w=wl)

    def layer(l):
        B.dma(par[:, 0:672], bc(psmall[l][0:672].unsqueeze(0), [128, 672]), tw=[par])
        B.dma(n1w.ap, pn1[l][0:D].rearrange("(k p) -> p k", p=128), tw=[n1w], slow=True)
        B.dma(n2w.ap, pn2[l][0:D].rearrange("(k p) -> p k", p=128), tw=[n2w], slow=True)
        for k in range(3):
            B.dma(cw[:, k, :], pcw[k][l][0:512].rearrange("(c p) -> p c", p=64), tw=[cw], slow=True)
            B.dma(cwf[:, k, :], pfc[k][l][0:2 * DFF].rearrange("(c p) -> p c", p=128), tw=[cwf], slow=True)
        E('dve', lambda e: e.tensor_scalar(out=par[:, 0:64], in0=par[:, 0:64], scalar1=0.125, scalar2=None,
                                           op0=ALU.mult), r=[par], w=[par])
        E('act', lambda e: e.activation(out=der[:, 32:40], in_=par[:, 128:136], func=AF.Exp), r=[par], w=[der])
        E('act', lambda e: e.activation(out=der[:, 0:8], in_=par[:, 136:144], func=AF.Exp, scale=-1.0), r=[par], w=[der])
        E('act', lambda e: e.activation(out=der[:, 0:8], in_=der[:, 0:8], func=AF.Ln, bias=1.0), r=[der], w=[der])
        E('dve', lambda e: e.tensor_scalar(out=der[:, 0:8], in0=der[:, 0:8], scalar1=-1.0, scalar2=None, op0=ALU.mult),
          r=[der], w=[der])
        E('act', lambda e: e.activation(out=der[:, 8:16], in_=der[:, 0:8], func=AF.Exp, scale=128.0), r=[der], w=[der])
        for h in range(4):
            for (dst, ridx, lgc) in ((16 + h, 0, h), (20 + h, 1, 4 + h), (24 + h, 2, h), (28 + h, 3, 4 + h)):
                E('act', lambda e, dst=dst, ridx=ridx, lgc=lgc: e.activation(
                    out=der[:, dst:dst + 1], in_=cst[:, C_RIDX + ridx:C_RIDX + ridx + 1], func=AF.Exp,
                    scale=der[:, lgc:lgc + 1]), r=[der, cst], w=[der])
            E('act', lambda e, h=h: e.activation(out=dtmp[:, 0, :], in_=cst[:, C_DF:C_DF + 128], func=AF.Exp,
                                                 scale=der[:, h:h + 1]), r=[der, cst], w=[dtmp])
            E('act', lambda e, h=h: e.activation(out=dtmp[:, 1, :], in_=cst[:, C_DB:C_DB + 128], func=AF.Exp,
                                                 scale=der[:, 4 + h:5 + h]), r=[der, cst], w=[dtmp])
            E('dve', lambda e: e.tensor_tensor(out=dtmp[:, 0, :], in0=dtmp[:, 0, :], in1=triF, op=ALU.mult),
              r=[dtmp, cst], w=[dtmp])
            E('dve', lambda e: e.tensor_tensor(out=dtmp[:, 1, :], in0=dtmp[:, 1, :], in1=triB, op=ALU.mult),
              r=[dtmp, cst], w=[dtmp])
            E('dve', lambda e, h=h: e.tensor_tensor(out=DT[:, h, :], in0=dtmp[:, 0, :], in1=dtmp[:, 1, :], op=ALU.add),
              r=[dtmp], w=[DT])
        load_weight(Win, [[(w_in_k[k], 0, PW)] for k in range(8)], l, PW, n1w)
        for t_ in (Rf, Sf, mF, Rb, Sb, mB, metaK, metaV):
            E('dve', lambda e, t_=t_: e.memset(t_.ap, 0.0), w=[t_])
        E('pool', lambda e: e.memset(win.ap, 0.0), w=[win])
        for c_ in range(2):
            CUR[0] = c_
            E('dve', lambda e: e.memset(vun.ap, 1.0), w=[vun])
            E('dve', lambda e: e.memset(p128.ap, 1.0), w=[p128])
        CUR[0] = 0
        stageA(H[0], RI[0], 257)
        shift_win(256)

        def stepP_main(ix, u):
            wo = 128 * u
            stageA(ix(H, 1), ix(RI, 1), 129 + wo)
            B.dma(rmr.ap, bvb(ix(RM, 0), 128, 64), tw=[rmr])
            B.dma(rinfB.ap, bv(ix(RI, 0), 128, 4), tw=[rinfB])
            rB = rinfB
            def proj(groups):
                for (bk, c0, c1, o0) in groups:
                    for k in range(8):
                        E('pe', lambda e, bk=bk, c0=c0, c1=c1, o0=o0, k=k: e.matmul(
                            Kp[bk][:, o0:o0 + c1 - c0], win[:, k, 1 + wo:129 + wo], Win[:, k, c0:c1], start=(k == 0), stop=(k == 7)),
                          r=[win, wreg], w=[Kp[bk]])
            proj([(0, 0, 512, 0), (1, 512, 1024, 0), (2, 1024, 1536, 0), (3, 1536, 1792, 0), (3, 2816, 2832, 256)])
            E('act', lambda e: e.activation(out=sq[:, 0:512], in_=Kp[0].ap, func=AF.Square), r=[Kp[0]], w=[sq])
            E('act', lambda e: e.activation(out=sq[:, 512:640], in_=Kp[1][:, 0:128], func=AF.Square), r=[Kp[1]], w=[sq])
            E('dve', lambda e: e.tensor_reduce(out=hs[:, 0:10], in_=sq.ap.rearrange("p (h d) -> p h d", d=64),
                                               axis=AX.X, op=ALU.add), r=[sq], w=[hs])
            E('dve', lambda e: e.tensor_scalar(out=hs[:, 0:10], in0=hs[:, 0:10], scalar1=1.0 / 64, scalar2=EPS,
                                               op0=ALU.mult, op1=ALU.add), r=[hs], w=[hs])
            E('act', lambda e: e.activation(out=hs[:, 0:10], in_=hs[:, 0:10], func=AF.Sqrt), r=[hs], w=[hs])
            E('dve', lambda e: e.reciprocal(out=hs[:, 0:10], in_=hs[:, 0:10]), r=[hs], w=[hs])
            E('dve', lambda e: e.tensor_tensor(out=qn[:, 0:512].rearrange("p (h d) -> p h d", d=64),
                                               in0=Kp[0].ap.rearrange("p (h d) -> p h d", d=64),
                                               in1=bc(hs[:, 0:8].unsqueeze(2), [128, 8, 64]), op=ALU.mult),
              r=[Kp[0], hs], w=[qn])
            E('dve', lambda e: e.tensor_tensor(out=qn[:, 512:640].rearrange("p (h d) -> p h d", d=64),
                                               in0=Kp[1][:, 0:128].rearrange("p (h d) -> p h d", d=64),
                                               in1=bc(hs[:, 8:10].unsqueeze(2), [128, 2, 64]), op=ALU.mult),
              r=[Kp[1], hs], w=[qn])
            E('pool', lambda e: e.tensor_tensor(out=qn[:, 0:512].rearrange("p (h d) -> p h d", d=64),
                                                in0=qn[:, 0:512].rearrange("p (h d) -> p h d", d=64),
                                                in1=bc(par[:, 0:64].unsqueeze(1), [128, 8, 64]), op=ALU.mult),
              r=[qn, par], w=[qn])
            E('pool', lambda e: e.tensor_tensor(out=qn[:, 512:640].rearrange("p (h d) -> p h d", d=64),
                                                in0=qn[:, 512:640].rearrange("p (h d) -> p h d", d=64),
                                                in1=bc(par[:, 64:128].unsqueeze(1), [128, 2, 64]), op=ALU.mult),
              r=[qn, par], w=[qn])
            E('act', lambda e: e.activation(out=vun[:, :, 0:64], in_=Kp[1][:, 128:256].rearrange("p (h d) -> p h d", d=64),
                                            func=AF.Copy), r=[Kp[1]], w=[vun])
            E('dve', lambda e: e.tensor_scalar(out=V128('AV'), in0=vun.ap.rearrange("p h d -> p (h d)"),
                                               scalar1=rB[:, 1:2], scalar2=None, op0=ALU.mult), r=[vun, rB], w=[p128])
            E('act', lambda e: e.activation(out=rqk[:, 0:256], in_=Kp[1][:, 256:512], func=AF.Copy), r=[Kp[1]], w=[rqk])
            E('act', lambda e: e.activation(out=rqk[:, 256:512], in_=Kp[2][:, 0:256], func=AF.Copy, scale=0.125),
              r=[Kp[2]], w=[rqk])
            E('act', lambda e: e.activation(out=V128('RV'), in_=Kp[2][:, 256:512], func=AF.Copy), r=[Kp[2]], w=[p128])
            E('act', lambda e: e.activation(out=VF('RG'), in_=Kp[3][:, 0:256], func=AF.Silu), r=[Kp[3]], w=[f128])
            E('dve', lambda e: e.tensor_tensor(out=G.ap, in0=Kp[3][:, 256:272], in1=par[:, 656:672], op=ALU.add),
              r=[Kp[3], par], w=[G])
            for d_ in range(2):
                E('dve', lambda e, d_=d_: e.tensor_tensor(
                    out=rkw[:, d_, :].rearrange("p (h d) -> p h d", d=64),
                    in0=rqk[:, 256:512].rearrange("p (h d) -> p h d", d=64),
                    in1=bc(der[:, 24 + 4 * d_:28 + 4 * d_].unsqueeze(2), [128, 4, 64]), op=ALU.mult),
                  r=[rqk, der], w=[rkw])
            proj([(0, 2304, 2816, 0)])
            E('act', lambda e: e.activation(out=VF('MO'), in_=Kp[0][:, 256:512], func=AF.Sigmoid), r=[Kp[0]], w=[f128])
            E('act', lambda e: e.activation(out=V128('MVA').rearrange("p (h d) -> p h d", d=65)[:, :, 0:64],
                                            in_=Kp[0][:, 0:256].rearrange("p (h d) -> p h d", d=64), func=AF.Copy),
              r=[Kp[0]], w=[p128])
            for hc in range(8):
                xb, xo = Kp[1 + hc // 3], (hc % 3) * 130
                for k in range(8):
                    E('pe', lambda e, hc=hc, xb=xb, xo=xo, k=k: e.matmul(
                        xb[0:64, xo:xo + 130], Win[:, k, 1792 + hc * 64:1792 + (hc + 1) * 64], win[:, k, wo:wo + 130],
                        start=(k == 0), stop=(k == 7)), r=[win, wreg], w=[xb])
            for hc in range(8):
                xb, xo = Kp[1 + hc // 3], (hc % 3) * 130
                E('act', lambda e, hc=hc, xb=xb, xo=xo: e.activation(out=cacc[:, hc, :], in_=xb[0:64, xo:xo + 128],
                                                                      func=AF.Copy, scale=cw[:, 0, hc:hc + 1]),
                  r=[xb, cw], w=[cacc])
                for tap in (1, 2):
                    E('dve', lambda e, hc=hc, xb=xb, xo=xo, tap=tap: e.scalar_tensor_tensor(
                        out=cacc[:, hc, :], in0=xb[0:64, xo + tap:xo + tap + 128], scalar=cw[:, tap, hc:hc + 1],
                        in1=cacc[:, hc, :], op0=ALU.mult, op1=ALU.add), r=[xb, cw, cacc], w=[cacc])
            E('act', lambda e: e.activation(out=V64('MQT').rearrange("p (h t) -> p h t", h=4), in_=cacc[:, 0:4, :],
                                            func=AF.Silu), r=[cacc], w=[p64])
            E('act', lambda e: e.activation(out=mkf.ap, in_=cacc[:, 4:8, :], func=AF.Silu), r=[cacc], w=[mkf])
            E('dve', lambda e: e.scalar_tensor_tensor(out=mkf.ap, in0=mkf.ap, scalar=0.125,
                                                      in1=bc(rmr.ap.unsqueeze(1), [64, 4, 128]),
                                                      op0=ALU.mult, op1=ALU.mult), r=[mkf, rmr], w=[mkf])
            E('pool', lambda e: e.tensor_copy(out=V64('MKT').rearrange("p (h t) -> p h t", h=4), in_=mkf.ap),
              r=[mkf], w=[p64])
            for h in range(8):
                xb = Kp[h // 4]
                E('pe', lambda e, h=h, xb=xb: e.transpose(xb[0:64, (h % 4) * 128:(h % 4 + 1) * 128],
                                                          qn[:, h * 64:(h + 1) * 64], ident), r=[qn, cst], w=[xb])
            E('act', lambda e: e.activation(out=V64('AQT')[:, 0:512], in_=Kp[0][0:64, :], func=AF.Copy), r=[Kp[0]], w=[p64])
            E('dve', lambda e: e.tensor_copy(out=V64('AQT')[:, 512:1024], in_=Kp[1][0:64, :]), r=[Kp[1]], w=[p64])
            for h in range(2):
                E('pe', lambda e, h=h: e.transpose(Kp[2][0:64, h * 128:(h + 1) * 128], qn[:, 512 + h * 64:512 + (h + 1) * 64],
                                                   ident), r=[qn, cst], w=[Kp[2]])
            E('act', lambda e: e.activation(out=V64('AKT'), in_=Kp[2][0:64, 0:256], func=AF.Copy), r=[Kp[2]], w=[p64])
            for h in range(4):
                E('pe', lambda e, h=h: e.transpose(Kp[0][0:64, h * 128:(h + 1) * 128], rqk[:, h * 64:(h + 1) * 64], ident),
                  r=[rqk, cst], w=[Kp[0]])
            E('act', lambda e: e.activation(out=V64('RQT'), in_=Kp[0][0:64, :], func=AF.Copy), r=[Kp[0]], w=[p64])
            for h in range(4):
                E('pe', lambda e, h=h: e.transpose(Kp[1][0:64, h * 128:(h + 1) * 128],
                                                   rqk[:, 256 + h * 64:256 + (h + 1) * 64], ident),
                  r=[rqk, cst], w=[Kp[1]])
            E('dve', lambda e: e.tensor_copy(out=V64('RKT'), in_=Kp[1][0:64, :]), r=[Kp[1]], w=[p64])
            for h in range(4):
                E('pe', lambda e, h=h: e.transpose(Kp[2][:, 256 + h * 64:256 + (h + 1) * 64], mkf[:, h, :], ident[0:64, 0:64]),
                  r=[mkf, cst], w=[Kp[2]])
            E('act', lambda e: e.activation(out=mktok.ap, in_=Kp[2][:, 256:512], func=AF.Copy), r=[Kp[2]], w=[mktok])
            E('pe', lambda e: e.matmul(Kp[3][0:16, 0:130], sel16.ap, vun.ap.rearrange("p h d -> p (h d)"), start=True, stop=True),
              r=[sel16, vun], w=[Kp[3]])
            E('act', lambda e: e.activation(out=vmn.ap, in_=Kp[3][0:16, 0:130], func=AF.Copy), r=[Kp[3]], w=[vmn])
            for d_ in range(2):
                for h in range(4):
                    E('pe', lambda e, d_=d_, h=h: e.matmul(
                        Kp[0][0:64, d_ * 256 + h * 64:d_ * 256 + (h + 1) * 64], rkw[:, d_, h * 64:(h + 1) * 64],
                        V128('RV')[:, h * 64:(h + 1) * 64], start=True, stop=True), r=[rkw, p128], w=[Kp[0]])
            E('act', lambda e: e.activation(out=kvf.ap, in_=Kp[0][0:64, 0:256], func=AF.Copy), r=[Kp[0]], w=[kvf])
            E('act', lambda e: e.activation(out=b64[:, 0:256], in_=Kp[0][0:64, 256:512], func=AF.Copy), r=[Kp[0]], w=[b64])
            G4 = G.ap.rearrange("p (d k h) -> p d k h", d=2, k=2)
            gl3 = gl.ap.rearrange("p (d h) -> p d h", d=2)
            E('act', lambda e: e.activation(out=gl3, in_=G4[:, :, 1, :], func=AF.Exp, scale=-1.0), r=[G], w=[gl])
            E('act', lambda e: e.activation(out=gl.ap, in_=gl.ap, func=AF.Ln, bias=1.0), r=[gl], w=[gl])
            E('pe', lambda e: e.matmul(Kp[1][:, 0:4], triF, gl[:, 0:4], start=True, stop=True), r=[cst, gl], w=[Kp[1]])
            E('pe', lambda e: e.matmul(Kp[1][:, 4:8], triB, gl[:, 4:8], start=True, stop=True), r=[cst, gl], w=[Kp[1]])
            E('pe', lambda e: e.matmul(Kp[1][:, 8:16], ones.ap, gl.ap, start=True, stop=True), r=[ones, gl], w=[Kp[1]])
            GT = VF('GT')
            E('dve', lambda e: e.tensor_scalar(out=GT[:, 8:16], in0=Kp[1][:, 0:8], scalar1=-1.0, scalar2=None, op0=ALU.mult),
              r=[Kp[1]], w=[f128])
            E('dve', lambda e: e.tensor_tensor(out=GT[:, 0:8].rearrange("p (d h) -> p d h", d=2), in0=Kp[1][:, 0:8].rearrange("p (d h) -> p d h", d=2),
                                               in1=G4[:, :, 0, :], op=ALU.add), r=[Kp[1], G], w=[f128])
            E('dve', lambda e: e.tensor_scalar(out=tot[:, 8:16], in0=Kp[1][:, 8:16], scalar1=-1.0, scalar2=None, op0=ALU.mult),
              r=[Kp[1]], w=[tot])
            E('pe', lambda e: e.matmul(Kp[2][0:4, 0:128], GT[:, 0:4], ident, start=True, stop=True), r=[f128, cst], w=[Kp[2]])
            E('pe', lambda e: e.matmul(Kp[2][0:4, 128:256], GT[:, 4:8], ident, start=True, stop=True), r=[f128, cst], w=[Kp[2]])
            E('dve', lambda e: e.tensor_copy(out=aT.ap, in_=Kp[2][0:4, 0:256]), r=[Kp[2]], w=[aT])
            E('dve', lambda e: e.tensor_tensor_scan(out=cmT[:, 0:128], data0=ones[0:4, :], data1=aT[:, 0:128],
                                                    initial=-3.0e38, op0=ALU.mult, op1=ALU.max), r=[ones, aT], w=[cmT])
            pst = cmT.ap.ap[0][0]
            rev_o = bass.AP(cmT.ap.tensor, cmT.ap.offset + 255, [[pst, 4], [-1, 128]])
            pst2 = aT.ap.ap[0][0]
            rev_i = bass.AP(aT.ap.tensor, aT.ap.offset + 255, [[pst2, 4], [-1, 128]])
            E('dve', lambda e: e.tensor_tensor_scan(out=rev_o, data0=ones[0:4, :], data1=rev_i,
                                                    initial=-3.0e38, op0=ALU.mult, op1=ALU.max), r=[ones, aT], w=[cmT])
            for d_ in range(2):
                col = 127 if d_ == 0 else 128
                E('dve', lambda e, d_=d_, col=col: e.tensor_scalar(out=dg[:, d_ * 4:d_ * 4 + 4], in0=ident[0:4, 0:4],
                                                                    scalar1=cmT[:, col:col + 1], scalar2=None, op0=ALU.mult),
                  r=[cst, cmT], w=[dg])
            E('pe', lambda e: e.matmul(Kp[1][:, 16:24], ones[0:4, :], dg.ap, start=True, stop=True), r=[ones, dg], w=[Kp[1]])
            E('dve', lambda e: e.tensor_copy(out=tot[:, 0:8], in_=Kp[1][:, 16:24]), r=[Kp[1]], w=[tot])
            E('pe', lambda e: e.matmul(Kp[1][:, 24:28], cmT[:, 0:128], ident[0:4, 0:4], start=True, stop=True), r=[cmT, cst], w=[Kp[1]])
            E('pe', lambda e: e.matmul(Kp[1][:, 28:32], cmT[:, 128:256], ident[0:4, 0:4], start=True, stop=True), r=[cmT, cst], w=[Kp[1]])
            E('dve', lambda e: e.tensor_copy(out=GT[:, 16:24], in_=Kp[1][:, 24:32]), r=[Kp[1]], w=[f128])
            E('dve', lambda e: e.tensor_tensor(out=eend.ap, in0=GT[:, 0:8], in1=tot[:, 0:8], op=ALU.subtract), r=[f128, tot], w=[eend])
            E('act', lambda e: e.activation(out=eend.ap, in_=eend.ap, func=AF.Exp), r=[eend], w=[eend])
            for d_ in range(2):
                E('dve', lambda e, d_=d_: e.tensor_tensor(
                    out=mke[:, d_, :].rearrange("p (h d) -> p h d", d=64), in0=mktok.ap.rearrange("p (h d) -> p h d", d=64),
                    in1=bc(eend[:, d_ * 4:d_ * 4 + 4].unsqueeze(2), [128, 4, 64]), op=ALU.mult), r=[mktok, eend], w=[mke])
            for d_ in range(2):
                for h in range(4):
                    dst = Kp[1][0:64, 64 + h * 65:64 + (h + 1) * 65] if d_ == 0 else Kp[2][0:64, h * 65:(h + 1) * 65]
                    E('pe', lambda e, d_=d_, h=h, dst=dst: e.matmul(dst, mke[:, d_, h * 64:(h + 1) * 64],
                                                                     V128('MVA')[:, h * 65:(h + 1) * 65], start=True, stop=True),
                      r=[mke, p128], w=[Kp[1] if d_ == 0 else Kp[2]])
            E('act', lambda e: e.activation(out=b64[:, 256:516], in_=Kp[2][0:64, 0:260], func=AF.Copy), r=[Kp[2]], w=[b64])
            E('act', lambda e: e.activation(out=kvnf.ap, in_=Kp[1][0:64, 64:324], func=AF.Copy), r=[Kp[1]], w=[kvnf])
            B.dma(bv(ix(SAV, 0), 128, 130), p128[:, 0:130], tr=[p128])
            B.dma(bv(ix(S128b, 0), 128, 516), p128[:, 130:646], tr=[p128])
            B.dma(bv(ix(SF128, 0), 128, F128W), f128.ap, tr=[f128])
            B.dma(bv(ix(SCMT, 0), 4, 256), cmT.ap, tr=[cmT])
            B.dma(bv(ix(SB64, 0), 64, 516), b64.ap, tr=[b64])

        def stepP_state(ix, u):
            rB = rinfB
            E('dve', lambda e: e.tensor_tensor(out=mtmp[0:16, :], in0=vmn.ap, in1=metaV.ap, op=ALU.subtract),
              r=[vmn, metaV], w=[mtmp])
            E('dve', lambda e: e.scalar_tensor_tensor(out=metaV.ap, in0=mtmp[0:16, :], scalar=rB[0:16, 2:3], in1=metaV.ap,
                                                      op0=ALU.mult, op1=ALU.add), r=[mtmp, rB, metaV], w=[metaV])
            E('dve', lambda e: e.tensor_copy(out=mvb.ap, in_=metaV.ap), r=[metaV], w=[mvb])
            akm = V64('AKT').rearrange("p (h t) -> p h t", h=2)[:, :, 112:128]
            mk3 = metaK.ap.rearrange("p (h t) -> p h t", h=2)
            mt3 = mtmp[:, 0:32].rearrange("p (h t) -> p h t", h=2)
            E('dve', lambda e: e.tensor_tensor(out=mt3, in0=akm, in1=mk3, op=ALU.subtract), r=[p64, metaK], w=[mtmp])
            E('dve', lambda e: e.scalar_tensor_tensor(out=metaK.ap, in0=mtmp[:, 0:32], scalar=rB[0:64, 2:3], in1=metaK.ap,
                                                      op0=ALU.mult, op1=ALU.add), r=[mtmp, rB, metaK], w=[metaK])
            E('dve', lambda e: e.tensor_copy(out=V64('MKM'), in_=metaK.ap), r=[metaK], w=[p64])
            E('dve', lambda e: e.tensor_scalar(out=Rf.ap, in0=Rf.ap, scalar1=rB[0:64, 3:4], scalar2=None, op0=ALU.mult),
              r=[Rf, rB], w=[Rf])
            E('act', lambda e: e.activation(out=V64('RF'), in_=Rf.ap, func=AF.Copy), r=[Rf], w=[p64])
            E('dve', lambda e: e.tensor_tensor(out=Rf.ap.rearrange("p (h d) -> p h d", d=64),
                                               in0=Rf.ap.rearrange("p (h d) -> p h d", d=64),
                                               in1=bc(der[0:64, 8:12].unsqueeze(2), [64, 4, 64]), op=ALU.mult),
              r=[Rf, der], w=[Rf])
            E('dve', lambda e: e.tensor_tensor(out=Rf.ap, in0=Rf.ap, in1=kvf.ap, op=ALU.add), r=[Rf, kvf], w=[Rf])
            E('dve', lambda e: e.tensor_scalar(out=Sf.ap, in0=Sf.ap, scalar1=rB[0:64, 3:4], scalar2=None, op0=ALU.mult),
              r=[Sf, rB], w=[Sf])
            E('dve', lambda e: e.tensor_scalar(out=mF.ap, in0=mF.ap, scalar1=rB[:, 3:4], scalar2=None, op0=ALU.mult),
              r=[mF, rB], w=[mF])
            E('act', lambda e: e.activation(out=V64('SF'), in_=Sf.ap, func=AF.Copy), r=[Sf], w=[p64])
            E('dve', lambda e: e.tensor_copy(out=srow[0:1, 0:4], in_=mF[0:1, :]), r=[mF], w=[srow])
            E('dve', lambda e: e.tensor_copy(out=srow[0:1, 4:8], in_=tot[0:1, 4:8]), r=[tot], w=[srow])
            E('dve', lambda e: e.tensor_copy(out=srow[0:1, 8:12], in_=tot[0:1, 12:16]), r=[tot], w=[srow])
            mlstm_update(mF, Sf, tot[:, 0:4], tot[:, 8:12], kvnf.ap, kvnf, [tot])
            B.dma(bv(ix(S64a, 0), 64, 1024), p64[:, 0:1024], tr=[p64])
            B.dma(bv(ix(SAK, 0), 64, 256), p64[:, 1024:1280], tr=[p64])
            B.dma(bv(ix(S64b, 0), 64, 1536), p64[:, 1280:2816], tr=[p64])
            B.dma(bv(ix(S64c, 0), 64, 1060), p64[:, 2816:3876], tr=[p64])
            B.dma(bv(ix(SMV, 0), 16, 130), mvb.ap, tr=[mvb])
            B.dma(bv(ix(SROW, 0), 1, 12), srow[0:1, 0:12], tr=[srow])

        def mlstm_update(m_t, S_t, totc, blc, kvn_ap, kvn_tile, extra_r):
            E('dve', lambda e: e.tensor_tensor(out=sc4[:, 0:4], in0=m_t.ap, in1=totc, op=ALU.max), r=[m_t] + extra_r, w=[sc4])
            E('dve', lambda e: e.tensor_tensor(out=sc4[:, 4:8], in0=m_t.ap, in1=sc4[:, 0:4], op=ALU.subtract), r=[m_t, sc4], w=[sc4])
            E('dve', lambda e: e.tensor_tensor(out=sc4[:, 8:12], in0=totc, in1=sc4[:, 0:4], op=ALU.subtract), r=extra_r + [sc4], w=[sc4])
            E('act', lambda e: e.activation(out=sc4[:, 4:12], in_=sc4[:, 4:12], func=AF.Exp), r=[sc4], w=[sc4])
            E('dve', lambda e: e.tensor_tensor(out=S_t.ap.rearrange("p (h d) -> p h d", d=65),
                                               in0=S_t.ap.rearrange("p (h d) -> p h d", d=65),
                                               in1=bc(sc4[0:64, 4:8].unsqueeze(2), [64, 4, 65]), op=ALU.mult), r=[S_t, sc4], w=[S_t])
            E('dve', lambda e: e.tensor_tensor(out=stmp.ap.rearrange("p (h d) -> p h d", d=65),
                                               in0=kvn_ap.rearrange("p (h d) -> p h d", d=65),
                                               in1=bc(sc4[0:64, 8:12].unsqueeze(2), [64, 4, 65]), op=ALU.mult),
              r=[kvn_tile, sc4], w=[stmp])
            E('dve', lambda e: e.tensor_tensor(out=S_t.ap, in0=S_t.ap, in1=stmp.ap, op=ALU.add), r=[S_t, stmp], w=[S_t])
            E('dve', lambda e: e.tensor_tensor(out=m_t.ap, in0=blc, in1=sc4[:, 0:4], op=ALU.add), r=extra_r + [sc4], w=[m_t])

        def bodyP(i):
            streams = []
            ixs = [lambda T, off, u=u: pairs(T, u + off, NI)[i][0] for u in range(2)]
            for u in range(2):
                CUR[0] = u
                B.rec = []
                stepP_main(ixs[u], u)
                streams.append(B.rec)
                B.rec = None
            CUR[0] = 0
            merged = []
            for a_, b_ in zip(*streams):
                merged.append(a_)
                merged.append(b_)
            B.play(merged)
            for u in range(2):
                CUR[0] = u
                stepP_state(ixs[u], u)
            CUR[0] = 0
            shift_win(256)

        B.loop(NI, bodyP)

        def bodyB(i):
            def two(T):
                return pairs(T, 0, NI)[(NI - 1) - i]
            B.dma(b64B.ap, two(SB64)[:, 0:64 * 516].rearrange("u (p c) -> p u c", c=516), tw=[b64B])
            B.dma(scbB.ap, bc(two(SROW)[:, 0:12].unsqueeze(0), [128, 2, 12]), tw=[scbB])
            B.dma(rinfB2.ap, two(RI)[:, 0:128 * 4].rearrange("u (p c) -> p u c", c=4), tw=[rinfB2])
            for idx in (1, 0):
                E('act', lambda e, idx=idx: e.activation(out=o64B[:, idx, 0:256], in_=Rb.ap, func=AF.Copy), r=[Rb], w=[o64B])
                E('act', lambda e, idx=idx: e.activation(out=o64B[:, idx, 256:516], in_=Sb.ap, func=AF.Copy), r=[Sb], w=[o64B])
                E('dve', lambda e, idx=idx: e.tensor_copy(out=mbrowB[0:1, idx, :], in_=mB[0:1, :]), r=[mB], w=[mbrowB])
                E('dve', lambda e: e.tensor_tensor(out=Rb.ap.rearrange("p (h d) -> p h d", d=64),
                                                   in0=Rb.ap.rearrange("p (h d) -> p h d", d=64),
                                                   in1=bc(der[0:64, 12:16].unsqueeze(2), [64, 4, 64]), op=ALU.mult), r=[Rb, der], w=[Rb])
                E('dve', lambda e, idx=idx: e.tensor_tensor(out=Rb.ap, in0=Rb.ap, in1=b64B[:, idx, 0:256], op=ALU.add), r=[Rb, b64B], w=[Rb])
                mlstm_update(mB, Sb, scbB[:, idx, 4:8], scbB[:, idx, 8:12], b64B[:, idx, 256:516], b64B, [scbB])
                E('dve', lambda e, idx=idx: e.tensor_scalar(out=Rb.ap, in0=Rb.ap, scalar1=rinfB2[0:64, idx, 3:4], scalar2=None, op0=ALU.mult), r=[Rb, rinfB2], w=[Rb])
                E('dve', lambda e, idx=idx: e.tensor_scalar(out=Sb.ap, in0=Sb.ap, scalar1=rinfB2[0:64, idx, 3:4], scalar2=None, op0=ALU.mult), r=[Sb, rinfB2], w=[Sb])
                E('dve', lambda e, idx=idx: e.tensor_scalar(out=mB.ap, in0=mB.ap, scalar1=rinfB2[:, idx, 3:4], scalar2=None, op0=ALU.mult), r=[mB, rinfB2], w=[mB])
            B.dma(two(SO64)[:, 0:64 * 516].rearrange("u (p c) -> p u c", c=516), o64B.ap, tr=[o64B])
            B.dma(two(SMB)[:, 0:4].unsqueeze(0), mbrowB.ap, tr=[mbrowB])

        B.loop(NI, bodyB)

        load_weight(Wout, [[(w_out_k[k], 0, D)] for k in range(8)], l, D, None)

        def Q64(n):
            a, b = P64[n]
            return q64.ap[:, a:b]

        def Q128(n):
            a, b = P128[n]
            return q128.ap[:, a:b]

        def QF(n):
            a, b = F128[n]
            return g128.ap[:, a:b]

        def stepF(ix, u):
            B.dma(hA.ap, ix(H, 1), tw=[hA])
            B.dma(q64[:, 0:1024], bv(ix(S64a, 1), 64, 1024), tw=[q64])
            B.dma(q64[:, 1280:2816], bv(ix(S64b, 1), 64, 1536), tw=[q64])
            B.dma(q64[:, 2816:3876], bv(ix(S64c, 1), 64, 1060), tw=[q64])
            B.dma(q128[:, 130:646], bv(ix(S128b, 1), 128, 516), tw=[q128])
            B.dma(g128.ap, bv(ix(SF128, 1), 128, F128W), tw=[g128])
            B.dma(mv16.ap, bv(ix(SMV, 1), 16, 130), tw=[mv16])
            B.dma(cmb.ap, bvb(ix(SCMT, 1), 1024, 128), tw=[cmb])
            B.dma(mpv[:, 0:4], bvb(ix(SROW, 1), 4, 128), tw=[mpv])
            B.dma(mpv[:, 4:8], bvb(ix(SMB, 1), 4, 128), tw=[mpv])
            B.dma(r64.ap, bv(ix(SO64, 1), 64, 516), tw=[r64])
            B.dma(rinf.ap, bv(ix(RI, 1), 128, 4), tw=[rinf])
            E('dve', lambda e: e.tensor_scalar(out=rinf[:, 2:3], in0=rinf[:, 2:3], scalar1=-30000.0, scalar2=None, op0=ALU.mult),
              r=[rinf], w=[rinf])
            KM = [K[1], K[2], K[3], K[0]]
            KW = [K[3], K[0]]
            aqt = Q64('AQT').rearrange("p (h t) -> p h t", h=8)
            for kvh in range(2):
                ksrc = [kn[:, u + nb_, kvh * 128:(kvh + 1) * 128] for nb_ in range(3)]
                ktile = [kn, kn, kn]
                vsrc = [vn[:, u + nb_, kvh * 65:(kvh + 1) * 65] for nb_ in range(3)]
                vtile = [vn, vn, vn]
                qrhs = aqt[:, 4 * kvh:4 * kvh + 4, :]
                for nb in range(3):
                    E('pe', lambda e, nb=nb, ksrc=ksrc, qrhs=qrhs: e.matmul(K[nb].ap, ksrc[nb], qrhs, start=True, stop=True), r=[ktile[nb], q64], w=[K[nb]])
                E('pe', lambda e, kvh=kvh, qrhs=qrhs: e.matmul(K[3][0:16, :], Q64('MKM')[:, kvh * 16:(kvh + 1) * 16], qrhs, start=True, stop=True),
                  r=[q64], w=[K[3]])
                for nb in range(3):
                    if nb == 0:
                        E('act', lambda e, nb=nb: e.activation(out=stmpF.ap, in_=K[nb].ap, func=AF.Exp, bias=rinf[:, 2:3]),
                          r=[K[nb], rinf], w=[stmpF])
                    else:
                        E('act', lambda e, nb=nb: e.activation(out=stmpF.ap, in_=K[nb].ap, func=AF.Exp), r=[K[nb]], w=[stmpF])
                    E('pool', lambda e, nb=nb, kvh=kvh: e.tensor_tensor(out=ptb[nb].ap, in0=stmpF.ap, in1=Eb[:, kvh * 3 + nb, :],
                                                                        op=ALU.mult), r=[stmpF, Eb], w=[ptb[nb]])
                E('act', lambda e: e.activation(out=pmb.ap, in_=K[3][0:16, :], func=AF.Exp), r=[K[3]], w=[pmb])
                for g in range(4):
                    for nb in range(3):
                        E('pe', lambda e, g=g, nb=nb, vsrc=vsrc: e.matmul(K[0][:, g * 65:(g + 1) * 65], ptb[nb][:, g * 128:(g + 1) * 128],
                                                               vsrc[nb], start=(nb == 0), stop=False), r=[ptb[nb], vtile[nb]], w=[K[0]])
                    E('pe', lambda e, g=g, kvh=kvh: e.matmul(K[0][:, g * 65:(g + 1) * 65], pmb[:, g * 128:(g + 1) * 128],
                                                    mv16[:, kvh * 65:(kvh + 1) * 65], start=False, stop=True), r=[pmb, mv16], w=[K[0]])
                o3 = K[0][:, 0:260].rearrange("p (h d) -> p h d", d=65)
                E('dve', lambda e, o3=o3, kvh=kvh: e.tensor_tensor(out=sm8[:, 32:36], in0=o3[:, :, 64], in1=der[:, 32 + 4 * kvh:36 + 4 * kvh], op=ALU.add),
                  r=[K[0], der], w=[sm8])
                E('dve', lambda e: e.reciprocal(out=sm8[:, 36:40], in_=sm8[:, 32:36]), r=[sm8], w=[sm8])
                E('dve', lambda e, o3=o3, kvh=kvh: e.tensor_tensor(out=mixed[:, kvh * 256:(kvh + 1) * 256].rearrange("p (h d) -> p h d", d=64),
                                                   in0=o3[:, :, 0:64], in1=bc(sm8[:, 36:40].unsqueeze(2), [128, 4, 64]), op=ALU.mult),
                  r=[K[0], sm8], w=[mixed])
            rqt = Q64('RQT').rearrange("p (h t) -> p h t", h=4)
            rkt = Q64('RKT').rearrange("p (h t) -> p h t", h=4)
            for h in range(4):
                E('pe', lambda e, h=h: e.matmul(K[1][:, h * 128:(h + 1) * 128], rkt[:, h, :], rqt[:, h, :], start=True, stop=True),
                  r=[q64], w=[K[1]])
            E('dve', lambda e: e.tensor_tensor(out=wtr.ap, in0=K[1].ap, in1=DT.ap.rearrange("p h t -> p (h t)"), op=ALU.mult),
              r=[K[1], DT], w=[wtr])
            for h in range(4):
                E('pe', lambda e, h=h: e.matmul(K[2][:, h * 64:(h + 1) * 64], wtr[:, h * 128:(h + 1) * 128],
                                                Q128('RV')[:, h * 64:(h + 1) * 64], start=True, stop=True), r=[wtr, q128], w=[K[2]])
            for h in range(4):
                E('pe', lambda e, h=h: e.matmul(K[2][:, 256 + h * 64:256 + (h + 1) * 64], rqt[:, h, :],
                                                Q64('RF')[:, h * 64:(h + 1) * 64], start=True, stop=True), r=[q64], w=[K[2]])
            for h in range(4):
                E('pe', lambda e, h=h: e.matmul(K[3][:, h * 64:(h + 1) * 64], rqt[:, h, :],
                                                r64[:, h * 64:(h + 1) * 64], start=True, stop=True), r=[q64, r64], w=[K[3]])
            ro = oacc[:, 0:256].rearrange("p (h d) -> p h d", d=64)
            E('dve', lambda e: e.tensor_tensor(out=ro, in0=K[2][:, 256:512].rearrange("p (h d) -> p h d", d=64),
                                               in1=bc(der[:, 16:20].unsqueeze(2), [128, 4, 64]), op=ALU.mult), r=[K[2], der], w=[oacc])
            E('dve', lambda e: e.tensor_tensor(out=oacc[:, 0:256], in0=oacc[:, 0:256], in1=K[2][:, 0:256], op=ALU.add), r=[oacc, K[2]], w=[oacc])
            rb3 = obuf[:, 0:256].rearrange("p (h d) -> p h d", d=64)
            E('dve', lambda e: e.tensor_tensor(out=rb3, in0=K[3][:, 0:256].rearrange("p (h d) -> p h d", d=64),
                                               in1=bc(der[:, 20:24].unsqueeze(2), [128, 4, 64]), op=ALU.mult), r=[K[3], der], w=[obuf])
            E('dve', lambda e: e.tensor_tensor(out=oacc[:, 0:256], in0=oacc[:, 0:256], in1=obuf[:, 0:256], op=ALU.add), r=[oacc, obuf], w=[oacc])
            headnorm(ro, mixed[:, 512:768].rearrange("p (h d) -> p h d", d=64),
                     par[:, 144:400].rearrange("p (h d) -> p h d", d=64), QF('RG').rearrange("p (h d) -> p h d", d=64),
                     [oacc], [mixed])
            mqt = Q64('MQT').rearrange("p (h t) -> p h t", h=4)
            mkt = Q64('MKT').rearrange("p (h t) -> p h t", h=4)
            GT = QF('GT')
            for h in range(4):
                E('pe', lambda e, h=h: e.matmul(K[0][:, h * 128:(h + 1) * 128], mkt[:, h, :], mqt[:, h, :], start=True, stop=True),
                  r=[q64], w=[K[0]])
            cm4 = cmb.ap.rearrange("p (h d t) -> p h d t", h=4, d=2)
            mp3 = mpv.ap.rearrange("p (d h) -> p h d", d=2)
            E('dve', lambda e: e.tensor_tensor(out=cm4, in0=cm4, in1=bc(mp3.unsqueeze(3), [128, 4, 2, 128]), op=ALU.max),
              r=[cmb, mpv], w=[cmb])
            for h in range(4):
                for d_ in range(2):
                    E('dve', lambda e, h=h, d_=d_: e.tensor_scalar(out=cm4[:, h, d_, :], in0=cm4[:, h, d_, :],
                                                                    scalar1=GT[:, d_ * 4 + h:d_ * 4 + h + 1], scalar2=0.0,
                                                                    op0=ALU.subtract, op1=ALU.max), r=[cmb, g128], w=[cmb])
            E('act', lambda e: e.activation(out=cmb.ap, in_=cmb.ap, func=AF.Exp, scale=-1.0), r=[cmb], w=[cmb])
            msk = cst[:, C_TF:C_TF + 256].rearrange("p (d t) -> p d t", d=2)
            E('pool', lambda e: e.tensor_tensor(out=cm4, in0=cm4, in1=bc(msk.unsqueeze(1), [128, 4, 2, 128]), op=ALU.mult),
              r=[cmb, cst], w=[cmb])
            E('dve', lambda e: e.tensor_tensor(out=wtm.ap.rearrange("p (h d t) -> p h d t", h=4, d=2), in0=cm4,
                                               in1=bc(K[0].ap.rearrange("p (h t) -> p h t", h=4).unsqueeze(2), [128, 4, 2, 128]),
                                               op=ALU.mult), r=[cmb, K[0]], w=[wtm])
            wt4 = wtm.ap.rearrange("p (h d t) -> p h d t", h=4, d=2)
            for d_ in range(2):
                for h in range(4):
                    E('pe', lambda e, d_=d_, h=h: e.matmul(KM[d_][:, h * 65:(h + 1) * 65], wt4[:, h, d_, :],
                                                           Q128('MVA')[:, h * 65:(h + 1) * 65], start=True, stop=True),
                      r=[wtm, q128], w=[KM[d_]])
                for h in range(4):
                    st_src = Q64('SF')[:, h * 65:(h + 1) * 65] if d_ == 0 else r64[:, 256 + h * 65:256 + (h + 1) * 65]
                    E('pe', lambda e, d_=d_, h=h, st_src=st_src: e.matmul(KM[2 + d_][:, h * 65:(h + 1) * 65], mqt[:, h, :], st_src,
                                                                           start=True, stop=True), r=[q64, r64], w=[KM[2 + d_]])
            E('dve', lambda e: e.tensor_tensor(out=sm8[:, 40:48], in0=GT[:, 16:24], in1=mpv.ap, op=ALU.max), r=[g128, mpv], w=[sm8])
            E('dve', lambda e: e.tensor_tensor(out=sm8[:, 48:56], in0=mpv.ap, in1=sm8[:, 40:48], op=ALU.subtract), r=[mpv, sm8], w=[sm8])
            E('dve', lambda e: e.scalar_tensor_tensor(out=sm8[:, 56:64], in0=GT[:, 8:16], scalar=-1.0, in1=sm8[:, 40:48],
                                                      op0=ALU.mult, op1=ALU.subtract), r=[g128, sm8], w=[sm8])
            E('act', lambda e: e.activation(out=sm8[:, 48:64], in_=sm8[:, 48:64], func=AF.Exp), r=[sm8], w=[sm8])
            for d_ in range(2):
                nd = obuf[:, 0:260].rearrange("p (h d) -> p h d", d=65)
                E('dve', lambda e, d_=d_: e.tensor_tensor(out=nd, in0=KM[2 + d_][:, 0:260].rearrange("p (h d) -> p h d", d=65),
                                                          in1=bc(sm8[:, 48 + 4 * d_:52 + 4 * d_].unsqueeze(2), [128, 4, 65]), op=ALU.mult),
                  r=[KM[2 + d_], sm8], w=[obuf])
                E('dve', lambda e, d_=d_: e.tensor_tensor(out=obuf[:, 0:260], in0=obuf[:, 0:260], in1=KM[d_][:, 0:260], op=ALU.add),
                  r=[obuf, KM[d_]], w=[obuf])
                E('dve', lambda e, d_=d_: e.scalar_tensor_tensor(out=sm8[:, 20:24], in0=nd[:, :, 64], scalar=-1.0, in1=nd[:, :, 64],
                                                                 op0=ALU.mult, op1=ALU.max), r=[obuf], w=[sm8])
                E('dve', lambda e, d_=d_: e.tensor_tensor(out=sm8[:, 24:28], in0=sm8[:, 20:24], in1=sm8[:, 56 + 4 * d_:60 + 4 * d_], op=ALU.max),
                  r=[sm8], w=[sm8])
                E('dve', lambda e: e.reciprocal(out=sm8[:, 28:32], in_=sm8[:, 24:28]), r=[sm8], w=[sm8])
                hm = oacc[:, 260:516].rearrange("p (h d) -> p h d", d=64)
                if d_ == 0:
                    E('dve', lambda e: e.tensor_tensor(out=hm, in0=nd[:, :, 0:64], in1=bc(sm8[:, 28:32].unsqueeze(2), [128, 4, 64]), op=ALU.mult),
                      r=[obuf, sm8], w=[oacc])
                else:
                    E('dve', lambda e: e.tensor_tensor(out=nd[:, :, 0:64], in0=nd[:, :, 0:64], in1=bc(sm8[:, 28:32].unsqueeze(2), [128, 4, 64]), op=ALU.mult),
                      r=[obuf, sm8], w=[obuf])
                    E('dve', lambda e: e.tensor_tensor(out=hm, in0=hm, in1=nd[:, :, 0:64], op=ALU.add), r=[oacc, obuf], w=[oacc])
            hm = oacc[:, 260:516].rearrange("p (h d) -> p h d", d=64)
            headnorm(hm, mixed[:, 768:1024].rearrange("p (h d) -> p h d", d=64),
                     par[:, 400:656].rearrange("p (h d) -> p h d", d=64), QF('MO').rearrange("p (h d) -> p h d", d=64),
                     [oacc], [mixed])
            for half in range(2):
                for kk in range(4):
                    k = half * 4 + kk
                    E('pe', lambda e, k=k, kk=kk, half=half: e.transpose(K[1 + half][:, kk * 128:(kk + 1) * 128],
                                                                          mixed[:, k * 128:(k + 1) * 128], ident), r=[mixed, cst], w=[K[1 + half]])
                E('act', lambda e, half=half: e.activation(out=mixT[:, half * 4:half * 4 + 4, :],
                                                           in_=K[1 + half].ap.rearrange("p (k c) -> p k c", k=4), func=AF.Copy),
                  r=[K[1 + half]], w=[mixT])
            for n in range(2):
                for k in range(8):
                    E('pe', lambda e, n=n, k=k: e.matmul(KW[n].ap, mixT[:, k, :], Wout[:, k, n * 512:(n + 1) * 512],
                                                         start=(k == 0), stop=(k == 7)), r=[mixT, wdn], w=[KW[n]])
                E('dve', lambda e, n=n: e.tensor_tensor(out=hA[:, n * 512:(n + 1) * 512], in0=hA[:, n * 512:(n + 1) * 512],
                                                        in1=KW[n].ap, op=ALU.add), r=[hA, KW[n]], w=[hA])
            B.dma(ix(H, 1), hA.ap, tr=[hA])

        for kvh in range(2):
            for nb in range(3):
                for g in range(4):
                    slope = float(2.0 ** (-8.0 * (4 * kvh + g + 1) / 8.0))
                    E('act', lambda e, kvh=kvh, nb=nb, g=g, slope=slope: e.activation(
                        out=Eb[:, kvh * 3 + nb, g * 128:(g + 1) * 128], in_=cst[:, C_DIST + nb * 128:C_DIST + (nb + 1) * 128],
                        func=AF.Exp, scale=-slope), r=[cst], w=[Eb])
        B.dma(kn[:, 2, :], bv(SAK[0], 64, 256), tw=[kn])
        B.dma(kn[:, 3, :], bv(SAK[1], 64, 256), tw=[kn])
        B.dma(vn[:, 2, :], bv(SAV[0], 128, 130), tw=[vn])
        B.dma(vn[:, 3, :], bv(SAV[1], 128, 130), tw=[vn])

        def bodyF(i):
            for sl in range(2):
                E('pool', lambda e, sl=sl: e.tensor_copy(out=kn[:, sl, :], in_=kn[:, sl + 2, :]), r=[kn], w=[kn])
                E('pool', lambda e, sl=sl: e.tensor_copy(out=vn[:, sl, :], in_=vn[:, sl + 2, :]), r=[vn], w=[vn])
            for sl in range(2):
                B.dma(kn[:, 2 + sl, :], bv(pairs(SAK, 2 + sl, NJ)[i][0], 64, 256), tw=[kn])
                B.dma(vn[:, 2 + sl, :], bv(pairs(SAV, 2 + sl, NJ)[i][0], 128, 130), tw=[vn])
            streams = []
            for u in range(2):
                CUR[0] = u
                B.rec = []
                stepF(lambda T, off, u=u: pairs(T, u + off, NJ)[i][0], u)
                streams.append(B.rec)
                B.rec = None
            CUR[0] = 0
            merged = []
            if os.environ.get("MERGE", "alt") == "seq":
                merged = streams[0] + streams[1]
            else:
                for a_, b_ in zip(*streams):
                    merged.append(a_)
                    merged.append(b_)
            B.play(merged)

        B.loop(NJ, bodyF)

        load_weight(Wup, [[(ffn_up_k[k][0], 0, DFF), (ffn_up_k[k][1], DFF, 2 * DFF)] for k in range(8)], l, 2 * DFF, n2w)
        load_weight(Wdn, [[(ffn_down_k[k], 0, D)] for k in range(22)], l, D, None)
        E('pool', lambda e: e.memset(win.ap, 0.0), w=[win])
        stageA(H[0], RI[0], 257)
        shift_win(256)

        def bodyG(i):
            cur = lambda T, u: pairs(T, 0, NJ)[i][u]
            nxt = lambda T, u: pairs(T, 1, NJ)[i][u]
            stageA(nxt(H, 0), nxt(RI, 0), 129)
            stageA(nxt(H, 1), nxt(RI, 1), 257)
            for cc in range(22):
                banks = (PP[2 * (cc % 2)], PP[2 * (cc % 2) + 1])
                for gv in range(2):
                    for k in range(8):
                        E('pe', lambda e, cc=cc, gv=gv, k=k: e.matmul(
                            banks[gv][:, 0:258], Wup[:, k, gv * DFF + cc * 128:gv * DFF + (cc + 1) * 128],
                            win[:, k, 0:258], start=(k == 0), stop=(k == 7)), r=[wreg, win], w=[banks[gv]])
                for gv in range(2):
                    ch = gv * 22 + cc
                    xb = banks[gv]
                    E('act', lambda e, gv=gv, ch=ch, xb=xb: e.activation(out=cg[:, gv, :], in_=xb[:, 0:256],
                                                                          func=AF.Copy, scale=cwf[:, 0, ch:ch + 1]), r=[xb, cwf], w=[cg])
                    for tap in (1, 2):
                        E('dve', lambda e, gv=gv, ch=ch, xb=xb, tap=tap: e.scalar_tensor_tensor(
                            out=cg[:, gv, :], in0=xb[:, tap:tap + 256], scalar=cwf[:, tap, ch:ch + 1],
                            in1=cg[:, gv, :], op0=ALU.mult, op1=ALU.add), r=[xb, cwf, cg], w=[cg])
                E('act', lambda e: e.activation(out=sg.ap, in_=cg[:, 0, :], func=AF.Silu), r=[cg], w=[sg])
                E('pool', lambda e, cc=cc: e.tensor_tensor(out=actT[:, cc, :], in0=sg.ap, in1=cg[:, 1, :], op=ALU.mult),
                  r=[sg, cg], w=[actT])
            for u in range(2):
                dbank = (X[0], X[1]) if u == 0 else (X[2], PP[4])
                for n in range(2):
                    for cc in range(22):
                        E('pe', lambda e, n=n, cc=cc, u=u: e.matmul(dbank[n].ap, actT[:, cc, u * 128:(u + 1) * 128],
                                                                    Wdn[:, cc, n * 512:(n + 1) * 512],
                                                                    start=(cc == 0), stop=(cc == 21)), r=[actT, wdn], w=[dbank[n]])
                B.dma(hA.ap, cur(H, u), tw=[hA])
                for n in range(2):
                    E('dve', lambda e, n=n: e.tensor_tensor(out=hA[:, n * 512:(n + 1) * 512], in0=hA[:, n * 512:(n + 1) * 512],
                                                            in1=dbank[n].ap, op=ALU.add), r=[hA, dbank[n]], w=[hA])
                B.dma(cur(H, u), hA.ap, tr=[hA])
            shift_win(256)

        B.loop(NJ, bodyG)

    B.seg_end()
    with nc.Fori(0, depth) as l:
        layer(l)
        B.seg_end()
    B.seg_end()
    return B


def make_consts():
    c = np.zeros((128, C_W), np.float32)
    s = np.arange(128)[:, None].astype(np.float64)
    t = np.arange(128)[None, :].astype(np.float64)
    c[:, C_ID:C_ID + 128] = np.eye(128)
    c[:, C_TF:C_TF + 128] = (s <= t)
    c[:, C_TB:C_TB + 128] = (s >= t)
    c[:, C_DF:C_DF + 128] = np.maximum(t - s, 0)
    c[:, C_DB:C_DB + 128] = np.maximum(s - t, 0)
    for nb in range(3):
        kpos = (nb - 1) * 128 + s
        dist = np.abs(t - kpos)
        c[:, C_DIST + nb * 128:C_DIST + (nb + 1) * 128] = np.where(dist <= 128, dist, 1.0e6)
    sv = np.arange(128)
    c[:, C_RIDX + 0] = sv + 1
    c[:, C_RIDX + 1] = 128 - sv
    c[:, C_RIDX + 2] = 127 - sv
    c[:, C_RIDX + 3] = sv
    for m in range(16):
        c[112 + m, C_SEL + m] = 1.0
    return c


def layout_core(seqs, meta, NB):
    NBA = NB + 6
    xin = np.zeros((NBA, 128, D), np.float32)
    ri = np.zeros((NBA, 128, 4), np.float32)
    ri[:, :, 2] = 1.0
    rowmap = []
    b = 1
    for x in seqs:
        nb = x.shape[0] // 128
        xin[b, 112:128] = meta
        ri[b, 112:128, 0] = 1.0
        ri[b, :, 2] = 1.0
        xin[b + 1:b + 1 + nb] = x.reshape(nb, 128, D)
        ri[b + 1:b + 1 + nb, :, 0] = 1.0
        ri[b + 1:b + 1 + nb, :, 1] = 1.0
        ri[b + 1:b + 1 + nb, :, 2] = 0.0
        rowmap.append((b + 1, nb))
        b += 1 + nb
    assert b <= NB + 1
    ri[:, :, 3] = 1.0 - ri[:, :, 2]
    rmrow = np.ascontiguousarray(ri[:, :, 0].reshape(NBA, 1, 128))
    return xin, ri, rmrow, rowmap


_CACHE = {}


def get_program(NB, depth):
    key = (NB, depth)
    if key not in _CACHE:
        _CACHE[key] = build(NB, depth).nc
    return _CACHE[key]


def kernel(x_prompt, x_sample, meta_tokens, norm1_w, w_in, attn_q_norm_w, attn_k_norm_w, attn_sink,
           ret_decay_logit, ret_norm_w, mlstm_conv_w, mlstm_gate_b, mlstm_norm_w, w_out, norm2_w,
           ffn_up, ffn_conv_w, ffn_down):
    f = lambda a: np.ascontiguousarray(np.asarray(a, dtype=np.float32))
    x_prompt, x_sample, meta = f(x_prompt), f(x_sample), f(meta_tokens)
    depth = int(np.asarray(norm1_w).shape[0])
    Bp, Sp, _ = x_prompt.shape
    Bs, Ss, _ = x_sample.shape
    assert Sp == 2 * Ss and Bs % 2 == 0
    NB = 2 * (1 + Ss // 128)
    cores = [[x_prompt[i]] for i in range(Bp)] + [[x_sample[2 * i], x_sample[2 * i + 1]] for i in range(Bs // 2)]
    ncore = len(cores)
    assert ncore <= 8
    nc = get_program(NB, depth)
    consts = make_consts()
    LP = 16896
    LW1 = 128 * PW
    SBk = 128 * D

    def padrow(a, L):
        a = f(a).reshape(depth, -1)
        o = np.zeros((depth, L), np.float32)
        o[:, :a.shape[1]] = a
        return o
    shared = {"consts": consts}
    shared["psmall"] = padrow(np.concatenate([f(attn_q_norm_w), f(attn_k_norm_w), f(attn_sink),
                                              f(ret_decay_logit).reshape(depth, 8), f(ret_norm_w), f(mlstm_norm_w),
                                              f(mlstm_gate_b)], axis=1), LP)
    shared["pn1"] = padrow(norm1_w, LP)
    shared["pn2"] = padrow(norm2_w, LP)
    mc, fc = f(mlstm_conv_w), f(ffn_conv_w)
    for k in range(3):
        shared["pcw%d" % k] = padrow(mc[:, k], LP)
        shared["pfc%d" % k] = padrow(fc[:, k], LP)
    wi, wo, fu, fd = f(w_in), f(w_out), f(ffn_up), f(ffn_down)
    for k in range(8):
        shared["w_in_k%d" % k] = padrow(wi[:, k * 128:(k + 1) * 128, :], LW1)
        shared["w_out_k%d" % k] = padrow(wo[:, k * 128:(k + 1) * 128, :], SBk)
        for hf in range(2):
            shared["ffn_up_k%d_%d" % (k, hf)] = padrow(fu[:, k * 128:(k + 1) * 128, hf * DFF:(hf + 1) * DFF], LW1)
    for k in range(22):
        shared["ffn_down_k%d" % k] = padrow(fd[:, k * 128:(k + 1) * 128, :], SBk)
    in_maps, maps = [], []
    for seqs in cores:
        xin, ri, rmrow, rowmap = layout_core(seqs, meta, NB)
        m = dict(shared)
        m.update({"xin": xin, "rowinfo": ri, "rmrow": rmrow})
        in_maps.append(m)
        maps.append(rowmap)
    while len(in_maps) < 8:
        in_maps.append(in_maps[-1])
    res = run_bass_kernel_spmd(nc, in_maps, core_ids=list(range(len(in_maps))))
    y_p = np.zeros_like(x_prompt)
    y_s = np.zeros_like(x_sample)
    for ci in range(ncore):
        h = res.results[ci]["hout"]
        if ci < Bp:
            b0, nb = maps[ci][0]
            y_p[ci] = h[b0:b0 + nb].reshape(nb * 128, D)
        else:
            for j, (b0, nb) in enumerate(maps[ci]):
                y_s[2 * (ci - Bp) + j] = h[b0:b0 + nb].reshape(nb * 128, D)
    return (y_p, y_s)
```

```python
import numpy as np
import concourse.bass as bass
import concourse.mybir as mybir
from concourse.bass_utils import run_bass_kernel_spmd

F32 = mybir.dt.float32
BF16 = mybir.dt.bfloat16
AF = mybir.ActivationFunctionType
ALU = mybir.AluOpType
AX = mybir.AxisListType

D = 1024
PW = 2832
DFF = 2816
EPS = 1e-6
OFF = 1 << 16
import os
SES_SET = set(os.environ.get("SES", "act,dve,pool").split(","))

P64 = dict(AQT=(0, 1024), AKT=(1024, 1280), RQT=(1280, 1792), RKT=(1792, 2304), MQT=(2304, 2816),
           MKT=(2816, 3328), RF=(3328, 3584), SF=(3584, 3844), MKM=(3844, 3876))
P64W = 3876
P128 = dict(AV=(0, 130), RV=(130, 386), MVA=(386, 646))
P128W = 646
F128 = dict(RG=(0, 256), MO=(256, 512), GT=(512, 536))
F128W = 537
C_ID, C_TF, C_TB, C_DF, C_DB, C_DIST, C_RIDX, C_SEL, C_W = 0, 128, 256, 384, 512, 640, 1024, 1028, 1044


class T:
    def __init__(self, name, ap):
        self.name = name
        self.ap = ap
        self.w = None
        self.r = {}
        self.dsem = None
        self.dloc = 0

    def __getitem__(self, k):
        return self.ap[k]


CUR = [0]


class Dual:
    def __init__(self, a, b):
        self.s = (a, b)

    def cur(self):
        return self.s[CUR[0]]

    @property
    def ap(self):
        return self.cur().ap

    def __getitem__(self, k):
        return self.cur().ap[k]


def _res(ts):
    return [t.cur() if isinstance(t, Dual) else t for t in ts]


class Bld:
    def __init__(self, tables=None):
        self.nc = bass.Bass("TRN2", target_bir_lowering=False)
        nc = self.nc
        self.eng = {'pe': nc.tensor, 'act': nc.scalar, 'dve': nc.vector, 'pool': nc.gpsimd, 'sp': nc.sync}
        self.sem = {e: nc.alloc_semaphore('s_' + e) for e in self.eng}
        self.loc = {e: 0 for e in self.eng}
        self.tiles = {}
        self.waited = {e: {} for e in self.eng}
        self.common = [0, 0]
        self.nwait = 0
        self.rec = None

    def sb(self, name, shape, dt=F32, dma=False, reg=None):
        reg = reg or self.common
        n = 1
        for x in shape[1:]:
            n *= x
        nbytes = n * (4 if dt == F32 else 2)
        off = reg[0]
        reg[0] = (off + nbytes + 63) // 64 * 64
        assert reg[0] <= reg[1], (name, reg)
        t = T(name, self.nc.alloc_sbuf_tensor_at(name, list(shape), dt, offset=off).ap())
        self.tiles[name] = t
        if dma:
            self.mkdma(t)
        return t

    def mkdma(self, t):
        t.dsem = self.nc.alloc_semaphore('d_' + t.name)

    def ps(self, name, shape=(128, 512)):
        t = T(name, self.nc.alloc_psum_tensor(name, list(shape), F32).ap())
        self.tiles[name] = t
        return t

    def _wait(self, e, k, pos):
        if k == e and e not in SES_SET:
            return
        if self.waited[e].get(k, -1) >= pos:
            return
        self.waited[e][k] = pos
        sem = self.tiles[k[1]].dsem if isinstance(k, tuple) else self.sem[k]
        self.eng[e].wait_ge(sem, pos)
        self.nwait += 1

    def _deps(self, e, r, w):
        for t in r:
            if t.w is not None:
                self._wait(e, *t.w)
        for t in w:
            if t.w is not None:
                self._wait(e, *t.w)
            for k, pos in list(t.r.items()):
                self._wait(e, k, pos)

    def sb2(self, name, shape, dt=F32, dma=False, reg=None, reg2=None):
        return Dual(self.sb(name + "_0", shape, dt, dma, reg), self.sb(name + "_1", shape, dt, dma, reg2 or reg))

    def play(self, items):
        for it in items:
            CUR[0] = it[1]
            if it[0] == 'op':
                self.op(*it[2:])
            else:
                self.dma(*it[2:])
        CUR[0] = 0

    def op(self, e, fn, r=(), w=()):
        r, w = _res(r), _res(w)
        if self.rec is not None:
            self.rec.append(('op', CUR[0], e, fn, r, w))
            return None
        self._deps(e, r, w)
        ins = fn(self.eng[e])
        ins.then_inc(self.sem[e], 1)
        self.loc[e] += 1
        pos = self.loc[e]
        for t in r:
            if t not in w:
                t.r[e] = pos
        for t in w:
            t.w = (e, pos)
            t.r = {}
        return ins

    def dma(self, out, in_, tw=(), tr=(), q='sp', slow=False):
        tw = _res(tw)
        tr = _res(tr)
        if self.rec is not None:
            self.rec.append(('dma', CUR[0], out, in_, tw, tr, q, slow))
            return
        self._deps(q, tr, tw)
        if slow:
            ins = self.eng[q].dma_start(out=out, in_=in_, allow_slow_non_contiguous=True)
        else:
            ins = self.eng[q].dma_start(out=out, in_=in_)
        tt = (tw + tr)[0]
        ins.then_inc(tt.dsem, 16)
        tt.dloc += 16
        pos = tt.dloc
        key = ('D', tt.name)
        for t in tw:
            t.w = (key, pos)
            t.r = {}
        for t in tr:
            t.r[key] = pos

    def seg_end(self):
        nc = self.nc
        used = []
        for e in self.eng:
            if self.loc[e]:
                if e != 'sp':
                    self.eng[e].wait_ge(self.sem[e], self.loc[e])
                used.append(self.sem[e])
        for t in self.tiles.values():
            if t.dsem is not None and t.dloc:
                nc.sync.wait_ge(t.dsem, t.dloc)
                used.append(t.dsem)
        if used:
            nc.all_engine_barrier()
            nums = [sm.num for sm in list(self.sem.values()) + [t.dsem for t in self.tiles.values() if t.dsem is not None]]
            nc.sync.sem_clear(range(min(nums), max(nums) + 1))
            nc.all_engine_barrier()
        for e in self.eng:
            self.loc[e] = 0
        for t in self.tiles.values():
            t.w = None
            t.r = {}
            t.dloc = 0
        self.waited = {e: {} for e in self.eng}

    def flush(self):
        self.seg_end()

    def loop(self, n, body):
        self.seg_end()
        with self.nc.Fori(0, n) as i:
            body(i)
            self.seg_end()


def bc(ap, shape):
    return ap.broadcast_to(list(shape))


def build(NB, depth):
    NBT = NB + 2
    NBA = NB + 6
    NI = (NBT + 2) // 2
    NJ = NBT // 2
    assert NBT % 2 == 0
    B = Bld()
    nc = B.nc
    dt_ = nc.dram_tensor

    SB = 128 * D
    LP = 16896
    LW1 = 128 * PW
    xin = dt_("xin", [NBA, 128, D], F32, kind="ExternalInput").ap()
    rowinfo = dt_("rowinfo", [NBA, 128, 4], F32, kind="ExternalInput").ap()
    rmrow = dt_("rmrow", [NBA, 1, 128], F32, kind="ExternalInput").ap()
    consts = dt_("consts", [128, C_W], F32, kind="ExternalInput").ap()
    psmall = dt_("psmall", [depth, LP], F32, kind="ExternalInput").ap()
    pn1 = dt_("pn1", [depth, LP], F32, kind="ExternalInput").ap()
    pn2 = dt_("pn2", [depth, LP], F32, kind="ExternalInput").ap()
    pcw = [dt_("pcw%d" % k, [depth, LP], F32, kind="ExternalInput").ap() for k in range(3)]
    pfc = [dt_("pfc%d" % k, [depth, LP], F32, kind="ExternalInput").ap() for k in range(3)]
    w_in_k = [dt_("w_in_k%d" % k, [depth, LW1], F32, kind="ExternalInput").ap() for k in range(8)]
    ffn_up_k = [[dt_("ffn_up_k%d_%d" % (k, hf), [depth, LW1], F32, kind="ExternalInput").ap() for hf in range(2)]
                for k in range(8)]
    w_out_k = [dt_("w_out_k%d" % k, [depth, SB], F32, kind="ExternalInput").ap() for k in range(8)]
    ffn_down_k = [dt_("ffn_down_k%d" % k, [depth, SB], F32, kind="ExternalInput").ap() for k in range(22)]
    Hh = dt_("hout", [NBA, 128, D], F32, kind="ExternalOutput")
    H = Hh.ap()
    Hf = H.rearrange("n p d -> n (p d)")

    def scratch(name, dt):
        return dt_(name, [NBA, SB], dt).ap()
    S64a, SAK, S64b, S64c = (scratch(n, BF16) for n in ("s64a", "sak", "s64b", "s64c"))
    SAV, S128b, SMV, SO64 = (scratch(n, BF16) for n in ("sav", "s128b", "smv", "so64"))
    SF128, SCMT, SROW, SB64, SMB, RI, RM = (scratch(n, F32) for n in ("sf128", "scmt", "srow", "sb64", "smb", "ri", "rm"))

    def bv(tb, rows, cols):
        return tb[0:rows * cols].rearrange("(p c) -> p c", c=cols)

    def bvb(tb, cols, parts):
        return bc(tb[0:cols].unsqueeze(0), [parts, cols])

    def pairs(T, off, n):
        v = T[off:off + 2 * n]
        if len(T.shape) == 2:
            return v.rearrange("(n u) s -> n u s", u=2)
        return v.rearrange("(n u) p d -> n u p d", u=2)

    BASE = 16512
    W1, W2, WST, GA, CM, TOP = (BASE + x for x in (0, 90112, 135168, 146496, 158784, 196608))
    B.common = [CM, TOP]
    rW1, rW2, rWS = [W1, W2], [W2, WST], [WST, GA]
    SH2 = W1 + 45312
    rS2 = [SH2, SH2 + 8768]
    rP = [SH2 + 8768, GA]
    rF1 = [W1, SH2]
    rF2 = [SH2 + 8768, W2]
    rG = [WST, CM]
    cst = B.sb("cst", [128, C_W], F32, dma=True)
    ident = cst[:, C_ID:C_ID + 128]
    triF = cst[:, C_TF:C_TF + 128]
    triB = cst[:, C_TB:C_TB + 128]
    ones = B.sb("ones", [128, 128])
    sel16 = B.sb("sel16", [128, 16], BF16)
    par = B.sb("par", [128, 1200], F32, dma=True)
    n1w = B.sb("n1w", [128, 8], F32, dma=True)
    n2w = B.sb("n2w", [128, 8], F32, dma=True)
    cw = B.sb("cw", [64, 3, 8], F32, dma=True)
    cwf = B.sb("cwf", [128, 3, 44], F32, dma=True)
    der = B.sb("der", [128, 64])
    DT = B.sb("DT", [128, 4, 128])
    dtmp = B.sb("dtmp", [128, 2, 128])
    wreg = B.sb("wreg", [128, 8 * 2 * DFF], BF16, reg=rW1)
    wdn = B.sb("wdn", [128, 22 * D], BF16, reg=rW2)
    wstage = B.sb("wstage", [128, PW], F32, dma=True, reg=rWS)
    Win = wreg.ap[:, 0:8 * PW].rearrange("p (k c) -> p k c", k=8)
    Wup = wreg.ap.rearrange("p (k c) -> p k c", k=8)
    Wdn = wdn.ap.rearrange("p (k c) -> p k c", k=22)
    Wout = wdn.ap[:, 0:8 * D].rearrange("p (k c) -> p k c", k=8)

    hA = B.sb2("hA", [128, D], F32, dma=True, reg=None, reg2=rS2)
    rinf = B.sb2("rinf", [128, 4], F32, dma=True, reg=None, reg2=rS2)
    rinfB = B.sb2("rinfB", [128, 4], F32, dma=True, reg=None, reg2=rS2)
    xn = B.sb2("xn", [128, D], F32, reg=None, reg2=rS2)
    junk = xn
    st8 = B.sb2("st8", [128, 8], reg=None, reg2=rS2)
    win = B.sb("win", [128, 8, 513], BF16)
    sm8 = B.sb2("sm8", [128, 64])
    Rf = B.sb("Rf", [64, 256])
    Sf = B.sb("Sf", [64, 260])
    mF = B.sb("mF", [128, 4])
    Rb = B.sb("Rb", [64, 256])
    Sb = B.sb("Sb", [64, 260])
    mB = B.sb("mB", [128, 4])
    metaK = B.sb("metaK", [64, 32])
    metaV = B.sb("metaV", [16, 130])
    PP = [B.ps("pp%d" % i) for i in range(5)]
    X = [B.ps("x%d" % i) for i in range(3)]
    Kp = [Dual(PP[j], (PP[4], X[0], X[1], X[2])[j]) for j in range(4)]
    p64 = B.sb2("p64", [64, P64W], BF16, dma=True, reg=rP)
    p128 = B.sb2("p128", [128, P128W], BF16, dma=True, reg=rP)
    f128 = B.sb2("f128", [128, F128W], F32, dma=True, reg=rP)

    def V64(n):
        a, b = P64[n]
        return p64.ap[:, a:b]

    def V128(n):
        a, b = P128[n]
        return p128.ap[:, a:b]

    def VF(n):
        a, b = F128[n]
        return f128.ap[:, a:b]
    sq = B.sb2("sq", [128, 640], reg=rP)
    hs = B.sb2("hs", [128, 16], reg=rP)
    qn = B.sb2("qn", [128, 640], reg=rP)
    vun = B.sb2("vun", [128, 2, 65], BF16, reg=rP)
    rqk = B.sb2("rqk", [128, 512], reg=rP)
    rkw = B.sb2("rkw", [128, 2, 256], BF16, reg=rP)
    mtmp = B.sb("mtmp", [64, 130], reg=rP)
    mvb = B.sb2("mvb", [16, 130], BF16, dma=True, reg=rP)
    rmr = B.sb2("rmr", [64, 128], F32, dma=True, reg=rP)
    cacc = B.sb2("cacc", [64, 8, 128], reg=rP)
    mkf = B.sb2("mkf", [64, 4, 128], reg=rP)
    mktok = B.sb2("mktok", [128, 256], reg=rP)
    mke = B.sb2("mke", [128, 2, 256], BF16, reg=rP)
    G = B.sb2("G", [128, 16], reg=rP)
    gl = B.sb2("gl", [128, 8], reg=rP)
    aT = B.sb2("aT", [4, 256], reg=rP)
    cmT = B.sb2("cmT", [4, 256], F32, dma=True, reg=rP)
    dg = B.sb2("dg", [4, 8], reg=rP)
    tot = B.sb2("tot", [128, 16], reg=rP)
    eend = B.sb2("eend", [128, 8], reg=rP)
    sc4 = B.sb("sc4", [128, 16], reg=rP)
    b64 = B.sb2("b64", [64, 516], F32, dma=True, reg=rP)
    srow = B.sb2("srow", [1, 16], F32, dma=True, reg=rP)
    stmp = B.sb("stmp", [64, 260], reg=rP)
    vmn = B.sb2("vmn", [16, 130], reg=rP)
    kvf = B.sb2("kvf", [64, 256], reg=rP)
    kvnf = B.sb2("kvnf", [64, 260], reg=rP)
    scb = B.sb("scb", [128, 16], F32, dma=True, reg=rP)
    b64B = B.sb("b64B", [64, 2, 516], F32, dma=True, reg=rP)
    scbB = B.sb("scbB", [128, 2, 12], F32, dma=True, reg=rP)
    rinfB2 = B.sb("rinfB2", [128, 2, 4], F32, dma=True, reg=rP)
    o64B = B.sb("o64B", [64, 2, 516], BF16, dma=True, reg=rP)
    mbrowB = B.sb("mbrowB", [1, 2, 4], F32, dma=True, reg=rP)
    o64 = B.sb("o64", [64, 516], BF16, dma=True, reg=rP)
    mbrow = B.sb("mbrow", [1, 4], F32, dma=True, reg=rP)
    kn = B.sb("kn", [64, 4, 256], BF16, dma=True, reg=rF1)
    vn = B.sb("vn", [128, 4, 130], BF16, dma=True, reg=rF1)
    q64 = B.sb2("q64", [64, P64W], BF16, dma=True, reg=rF1, reg2=rF2)
    q128 = B.sb2("q128", [128, P128W], BF16, dma=True, reg=rF1, reg2=rF2)
    g128 = B.sb2("g128", [128, F128W], F32, dma=True, reg=rF1, reg2=rF2)
    mv16 = B.sb2("mv16", [16, 130], BF16, dma=True, reg=rF1, reg2=rF2)
    cmb = B.sb2("cmb", [128, 1024], F32, dma=True, reg=rF1, reg2=rF2)
    mpv = B.sb2("mpv", [128, 8], F32, dma=True, reg=rF1, reg2=rF2)
    r64 = B.sb2("r64", [64, 516], BF16, dma=True, reg=rF1, reg2=rF2)
    mixed = B.sb2("mixed", [128, D], reg=rF1, reg2=rF2)
    stmpF = B.sb2("stmpF", [128, 512], reg=rF1, reg2=rF2)
    ptb = [B.sb2("ptb%d" % i, [128, 512], BF16, reg=rF1, reg2=rF2) for i in range(3)]
    pmb = B.sb2("pmb", [16, 512], BF16, reg=rF1, reg2=rF2)
    wtr = B.sb2("wtr", [128, 512], BF16, reg=rF1, reg2=rF2)
    wtm = B.sb2("wtm", [128, 1024], BF16, reg=rF1, reg2=rF2)
    oacc = B.sb2("oacc", [128, 520], reg=rF1, reg2=rF2)
    obuf = B.sb2("obuf", [128, 520], reg=rF1, reg2=rF2)
    mixT = B.sb2("mixT", [128, 8, 128], BF16, reg=rF1, reg2=rF1)
    K = [Dual(PP[j], (PP[4], X[0], X[1], X[2])[j]) for j in range(4)]
    rF3 = [W2 + 16384, WST]
    Eb = B.sb("Eb", [128, 6, 512], BF16, reg=rF3)
    actT = B.sb("actT", [128, 22, 256], BF16, reg=rG)
    cg = B.sb("cg", [128, 2, 256], reg=rG)
    sg = B.sb("sg", [128, 256], reg=rG)

    def E(e, fn, r=(), w=()):
        return B.op(e, fn, r, w)

    B.dma(cst.ap, consts, tw=[cst])
    E('dve', lambda e: e.memset(ones.ap, 1.0), w=[ones])
    E('dve', lambda e: e.tensor_copy(out=sel16.ap, in_=cst[:, C_SEL:C_SEL + 16]), r=[cst], w=[sel16])
    for c0 in range(0, NBA, 8):
        c1 = min(NBA, c0 + 8)
        B.dma(H[c0:c1], xin[c0:c1], tr=[cst])
    for c0 in range(NBA):
        B.dma(bv(RI[c0], 128, 4), rowinfo[c0], tr=[cst])
        B.dma(bv(RM[c0], 1, 128), rmrow[c0], tr=[cst])
        if c0 % 16 == 15:
            B.seg_end()
    B.flush()

    def load_weight(dst3, chunks, l, ncols, scale_t=None):
        wt = wdn if (dst3 is Wdn or dst3 is Wout) else wreg
        it = 0
        for k, pieces in enumerate(chunks):
            for (tk, c0, c1) in pieces:
                w_ = c1 - c0
                B.dma(wstage.ap[:, 0:w_], tk[l][0:128 * w_].rearrange("(p c) -> p c", c=w_), tw=[wstage])
                eng = 'dve' if it % 2 == 0 else 'pool'
                it += 1
                if scale_t is not None:
                    E(eng, lambda e, k=k, c0=c0, c1=c1, w_=w_: e.tensor_scalar(
                        out=dst3[:, k, c0:c1], in0=wstage.ap[:, 0:w_], scalar1=scale_t[:, k:k + 1], scalar2=None,
                        op0=ALU.mult), r=[wstage, scale_t], w=[wt])
                else:
                    E(eng, lambda e, k=k, c0=c0, c1=c1, w_=w_: e.tensor_copy(out=dst3[:, k, c0:c1], in_=wstage.ap[:, 0:w_]),
                      r=[wstage], w=[wt])

    def stageA(hsrc, rsrc, wc):
        B.dma(hA.ap, hsrc, tw=[hA])
        B.dma(rinf.ap, bv(rsrc, 128, 4), tw=[rinf])
        E('act', lambda e: e.activation(out=junk.ap, in_=hA.ap, func=AF.Square, accum_out=st8[:, 0:1]),
          r=[hA], w=[junk, st8])
        E('dve', lambda e: e.tensor_scalar(out=st8[:, 1:2], in0=st8[:, 0:1], scalar1=1.0 / D, scalar2=EPS,
                                           op0=ALU.mult, op1=ALU.add), r=[st8], w=[st8])
        E('act', lambda e: e.activation(out=st8[:, 3:4], in_=st8[:, 1:2], func=AF.Sqrt), r=[st8], w=[st8])
        E('dve', lambda e: e.reciprocal(out=st8[:, 4:5], in_=st8[:, 3:4]), r=[st8], w=[st8])
        E('dve', lambda e: e.tensor_tensor(out=st8[:, 2:3], in0=st8[:, 4:5], in1=rinf[:, 0:1], op=ALU.mult), r=[st8, rinf], w=[st8])
        E('dve', lambda e: e.tensor_scalar(out=xn.ap, in0=hA.ap, scalar1=st8[:, 2:3], scalar2=None, op0=ALU.mult),
          r=[hA, st8], w=[xn])
        for half in range(2):
            for kk in range(4):
                k = half * 4 + kk
                E('pe', lambda e, k=k, kk=kk, half=half: e.transpose(Kp[2 + half][:, kk * 128:(kk + 1) * 128],
                                                                      xn[:, k * 128:(k + 1) * 128], ident),
                  r=[xn, cst], w=[Kp[2 + half]])
            E('act', lambda e, half=half: e.activation(
                out=win[:, half * 4:half * 4 + 4, wc:wc + 128],
                in_=Kp[2 + half].ap.rearrange("p (k c) -> p k c", k=4), func=AF.Copy),
              r=[Kp[2 + half]], w=[win])

    def shift_win(n=128):
        E('pool', lambda e: e.tensor_copy(out=win[:, :, 0:1], in_=win[:, :, n:n + 1]), r=[win], w=[win])
        E('pool', lambda e: e.tensor_copy(out=win[:, :, 1:n + 1], in_=win[:, :, n + 1:2 * n + 1]), r=[win], w=[win])

    def headnorm(src_ap, dst_ap, w_ap, gate_ap, rl, wl):
        E('dve', lambda e: e.tensor_reduce(out=sm8[:, 0:4], in_=src_ap, axis=AX.X, op=ALU.add), r=rl, w=[sm8])
        E('dve', lambda e: e.tensor_scalar(out=sm8[:, 4:8], in0=sm8[:, 0:4], scalar1=-1.0 / 64, scalar2=None,
                                           op0=ALU.mult), r=[sm8], w=[sm8])
        E('dve', lambda e: e.tensor_tensor(out=src_ap, in0=src_ap, in1=bc(sm8[:, 4:8].unsqueeze(2), [128, 4, 64]),
                                           op=ALU.add), r=rl + [sm8], w=rl)
        j3 = junk[:, 0:256].rearrange("p (h d) -> p h d", h=4)
        E('act', lambda e: e.activation(out=j3, in_=src_ap, func=AF.Square), r=rl, w=[junk])
        E('dve', lambda e: e.tensor_reduce(out=sm8[:, 8:12], in_=j3, axis=AX.X, op=ALU.add), r=[junk], w=[sm8])
        E('dve', lambda e: e.tensor_scalar(out=sm8[:, 12:16], in0=sm8[:, 8:12], scalar1=1.0 / 64, scalar2=EPS,
                                           op0=ALU.mult, op1=ALU.add), r=[sm8], w=[sm8])
        E('act', lambda e: e.activation(out=sm8[:, 12:16], in_=sm8[:, 12:16], func=AF.Sqrt), r=[sm8], w=[sm8])
        E('dve', lambda e: e.reciprocal(out=sm8[:, 16:20], in_=sm8[:, 12:16]), r=[sm8], w=[sm8])
        E('dve', lambda e: e.tensor_tensor(out=src_ap, in0=src_ap, in1=bc(sm8[:, 16:20].unsqueeze(2), [128, 4, 64]),
                                           op=ALU.mult), r=rl + [sm8], w=rl)
        E('dve', lambda e: e.tensor_tensor(out=src_ap, in0=src_ap, in1=w_ap, op=ALU.mult), r=rl + [par], w=rl)
        E('dve', lambda e: e.tensor_tensor(out=dst_ap, in0=src_ap, in1=gate_ap, op=ALU.mult), r=rl + [g128], w=wl)

    def layer(l):
        B.dma(par[:, 0:672], bc(psmall[l][0:672].unsqueeze(0), [128, 672]), tw=[par])
        B.dma(n1w.ap, pn1[l][0:D].rearrange("(k p) -> p k", p=128), tw=[n1w], slow=True)
        B.dma(n2w.ap, pn2[l][0:D].rearrange("(k p) -> p k", p=128), tw=[n2w], slow=True)
        for k in range(3):
            B.dma(cw[:, k, :], pcw[k][l][0:512].rearrange("(c p) -> p c", p=64), tw=[cw], slow=True)
            B.dma(cwf[:, k, :], pfc[k][l][0:2 * DFF].rearrange("(c p) -> p c", p=128), tw=[cwf], slow=True)
        E('dve', lambda e: e.tensor_scalar(out=par[:, 0:64], in0=par[:, 0:64], scalar1=0.125, scalar2=None,
                                           op0=ALU.mult), r=[par], w=[par])
        E('act', lambda e: e.activation(out=der[:, 32:40], in_=par[:, 128:136], func=AF.Exp), r=[par], w=[der])
        E('act', lambda e: e.activation(out=der[:, 0:8], in_=par[:, 136:144], func=AF.Exp, scale=-1.0), r=[par], w=[der])
        E('act', lambda e: e.activation(out=der[:, 0:8], in_=der[:, 0:8], func=AF.Ln, bias=1.0), r=[der], w=[der])
        E('dve', lambda e: e.tensor_scalar(out=der[:, 0:8], in0=der[:, 0:8], scalar1=-1.0, scalar2=None, op0=ALU.mult),
          r=[der], w=[der])
        E('act', lambda e: e.activation(out=der[:, 8:16], in_=der[:, 0:8], func=AF.Exp, scale=128.0), r=[der], w=[der])
        for h in range(4):
            for (dst, ridx, lgc) in ((16 + h, 0, h), (20 + h, 1, 4 + h), (24 + h, 2, h), (28 + h, 3, 4 + h)):
                E('act', lambda e, dst=dst, ridx=ridx, lgc=lgc: e.activation(
                    out=der[:, dst:dst + 1], in_=cst[:, C_RIDX + ridx:C_RIDX + ridx + 1], func=AF.Exp,
                    scale=der[:, lgc:lgc + 1]), r=[der, cst], w=[der])
            E('act', lambda e, h=h: e.activation(out=dtmp[:, 0, :], in_=cst[:, C_DF:C_DF + 128], func=AF.Exp,
                                                 scale=der[:, h:h + 1]), r=[der, cst], w=[dtmp])
            E('act', lambda e, h=h: e.activation(out=dtmp[:, 1, :], in_=cst[:, C_DB:C_DB + 128], func=AF.Exp,
                                                 scale=der[:, 4 + h:5 + h]), r=[der, cst], w=[dtmp])
            E('dve', lambda e: e.tensor_tensor(out=dtmp[:, 0, :], in0=dtmp[:, 0, :], in1=triF, op=ALU.mult),
              r=[dtmp, cst], w=[dtmp])
            E('dve', lambda e: e.tensor_tensor(out=dtmp[:, 1, :], in0=dtmp[:, 1, :], in1=triB, op=ALU.mult),
              r=[dtmp, cst], w=[dtmp])
            E('dve', lambda e, h=h: e.tensor_tensor(out=DT[:, h, :], in0=dtmp[:, 0, :], in1=dtmp[:, 1, :], op=ALU.add),
              r=[dtmp], w=[DT])
        load_weight(Win, [[(w_in_k[k], 0, PW)] for k in range(8)], l, PW, n1w)
        for t_ in (Rf, Sf, mF, Rb, Sb, mB, metaK, metaV):
            E('dve', lambda e, t_=t_: e.memset(t_.ap, 0.0), w=[t_])
        E('pool', lambda e: e.memset(win.ap, 0.0), w=[win])
        for c_ in range(2):
            CUR[0] = c_
            E('dve', lambda e: e.memset(vun.ap, 1.0), w=[vun])
            E('dve', lambda e: e.memset(p128.ap, 1.0), w=[p128])
        CUR[0] = 0
        stageA(H[0], RI[0], 257)
        shift_win(256)

        def stepP_main(ix, u):
            wo = 128 * u
            stageA(ix(H, 1), ix(RI, 1), 129 + wo)
            B.dma(rmr.ap, bvb(ix(RM, 0), 128, 64), tw=[rmr])
            B.dma(rinfB.ap, bv(ix(RI, 0), 128, 4), tw=[rinfB])
            rB = rinfB
            def proj(groups):
                for (bk, c0, c1, o0) in groups:
                    for k in range(8):
                        E('pe', lambda e, bk=bk, c0=c0, c1=c1, o0=o0, k=k: e.matmul(
                            Kp[bk][:, o0:o0 + c1 - c0], win[:, k, 1 + wo:129 + wo], Win[:, k, c0:c1], start=(k == 0), stop=(k == 7)),
                          r=[win, wreg], w=[Kp[bk]])
            proj([(0, 0, 512, 0), (1, 512, 1024, 0), (2, 1024, 1536, 0), (3, 1536, 1792, 0), (3, 2816, 2832, 256)])
            E('act', lambda e: e.activation(out=sq[:, 0:512], in_=Kp[0].ap, func=AF.Square), r=[Kp[0]], w=[sq])
            E('act', lambda e: e.activation(out=sq[:, 512:640], in_=Kp[1][:, 0:128], func=AF.Square), r=[Kp[1]], w=[sq])
            E('dve', lambda e: e.tensor_reduce(out=hs[:, 0:10], in_=sq.ap.rearrange("p (h d) -> p h d", d=64),
                                               axis=AX.X, op=ALU.add), r=[sq], w=[hs])
            E('dve', lambda e: e.tensor_scalar(out=hs[:, 0:10], in0=hs[:, 0:10], scalar1=1.0 / 64, scalar2=EPS,
                                               op0=ALU.mult, op1=ALU.add), r=[hs], w=[hs])
            E('act', lambda e: e.activation(out=hs[:, 0:10], in_=hs[:, 0:10], func=AF.Sqrt), r=[hs], w=[hs])
            E('dve', lambda e: e.reciprocal(out=hs[:, 0:10], in_=hs[:, 0:10]), r=[hs], w=[hs])
            E('dve', lambda e: e.tensor_tensor(out=qn[:, 0:512].rearrange("p (h d) -> p h d", d=64),
                                               in0=Kp[0].ap.rearrange("p (h d) -> p h d", d=64),
                                               in1=bc(hs[:, 0:8].unsqueeze(2), [128, 8, 64]), op=ALU.mult),
              r=[Kp[0], hs], w=[qn])
            E('dve', lambda e: e.tensor_tensor(out=qn[:, 512:640].rearrange("p (h d) -> p h d", d=64),
                                               in0=Kp[1][:, 0:128].rearrange("p (h d) -> p h d", d=64),
                                               in1=bc(hs[:, 8:10].unsqueeze(2), [128, 2, 64]), op=ALU.mult),
              r=[Kp[1], hs], w=[qn])
            E('pool', lambda e: e.tensor_tensor(out=qn[:, 0:512].rearrange("p (h d) -> p h d", d=64),
                                                in0=qn[:, 0:512].rearrange("p (h d) -> p h d", d=64),
                                                in1=bc(par[:, 0:64].unsqueeze(1), [128, 8, 64]), op=ALU.mult),
              r=[qn, par], w=[qn])
            E('pool', lambda e: e.tensor_tensor(out=qn[:, 512:640].rearrange("p (h d) -> p h d", d=64),
                                                in0=qn[:, 512:640].rearrange("p (h d) -> p h d", d=64),
                                                in1=bc(par[:, 64:128].unsqueeze(1), [128, 2, 64]), op=ALU.mult),
              r=[qn, par], w=[qn])
            E('act', lambda e: e.activation(out=vun[:, :, 0:64], in_=Kp[1][:, 128:256].rearrange("p (h d) -> p h d", d=64),
                                            func=AF.Copy), r=[Kp[1]], w=[vun])
            E('dve', lambda e: e.tensor_scalar(out=V128('AV'), in0=vun.ap.rearrange("p h d -> p (h d)"),
                                               scalar1=rB[:, 1:2], scalar2=None, op0=ALU.mult), r=[vun, rB], w=[p128])
            E('act', lambda e: e.activation(out=rqk[:, 0:256], in_=Kp[1][:, 256:512], func=AF.Copy), r=[Kp[1]], w=[rqk])
            E('act', lambda e: e.activation(out=rqk[:, 256:512], in_=Kp[2][:, 0:256], func=AF.Copy, scale=0.125),
              r=[Kp[2]], w=[rqk])
            E('act', lambda e: e.activation(out=V128('RV'), in_=Kp[2][:, 256:512], func=AF.Copy), r=[Kp[2]], w=[p128])
            E('act', lambda e: e.activation(out=VF('RG'), in_=Kp[3][:, 0:256], func=AF.Silu), r=[Kp[3]], w=[f128])
            E('dve', lambda e: e.tensor_scalar(out=f128[:, 536:537], in0=rB[:, 2:3], scalar1=-30000.0, scalar2=None, op0=ALU.mult),
              r=[rB], w=[f128])
            E('dve', lambda e: e.tensor_tensor(out=G.ap, in0=Kp[3][:, 256:272], in1=par[:, 656:672], op=ALU.add),
              r=[Kp[3], par], w=[G])
            for d_ in range(2):
                E('dve', lambda e, d_=d_: e.tensor_tensor(
                    out=rkw[:, d_, :].rearrange("p (h d) -> p h d", d=64),
                    in0=rqk[:, 256:512].rearrange("p (h d) -> p h d", d=64),
                    in1=bc(der[:, 24 + 4 * d_:28 + 4 * d_].unsqueeze(2), [128, 4, 64]), op=ALU.mult),
                  r=[rqk, der], w=[rkw])
            proj([(0, 2304, 2816, 0)])
            E('act', lambda e: e.activation(out=VF('MO'), in_=Kp[0][:, 256:512], func=AF.Sigmoid), r=[Kp[0]], w=[f128])
            E('act', lambda e: e.activation(out=V128('MVA').rearrange("p (h d) -> p h d", d=65)[:, :, 0:64],
                                            in_=Kp[0][:, 0:256].rearrange("p (h d) -> p h d", d=64), func=AF.Copy),
              r=[Kp[0]], w=[p128])
            for hc in range(8):
                xb, xo = Kp[1 + hc // 3], (hc % 3) * 130
                for k in range(8):
                    E('pe', lambda e, hc=hc, xb=xb, xo=xo, k=k: e.matmul(
                        xb[0:64, xo:xo + 130], Win[:, k, 1792 + hc * 64:1792 + (hc + 1) * 64], win[:, k, wo:wo + 130],
                        start=(k == 0), stop=(k == 7)), r=[win, wreg], w=[xb])
            for hc in range(8):
                xb, xo = Kp[1 + hc // 3], (hc % 3) * 130
                E('act', lambda e, hc=hc, xb=xb, xo=xo: e.activation(out=cacc[:, hc, :], in_=xb[0:64, xo:xo + 128],
                                                                      func=AF.Copy, scale=cw[:, 0, hc:hc + 1]),
                  r=[xb, cw], w=[cacc])
                for tap in (1, 2):
                    E('dve', lambda e, hc=hc, xb=xb, xo=xo, tap=tap: e.scalar_tensor_tensor(
                        out=cacc[:, hc, :], in0=xb[0:64, xo + tap:xo + tap + 128], scalar=cw[:, tap, hc:hc + 1],
                        in1=cacc[:, hc, :], op0=ALU.mult, op1=ALU.add), r=[xb, cw, cacc], w=[cacc])
            E('act', lambda e: e.activation(out=V64('MQT').rearrange("p (h t) -> p h t", h=4), in_=cacc[:, 0:4, :],
                                            func=AF.Silu), r=[cacc], w=[p64])
            E('act', lambda e: e.activation(out=mkf.ap, in_=cacc[:, 4:8, :], func=AF.Silu), r=[cacc], w=[mkf])
            E('dve', lambda e: e.scalar_tensor_tensor(out=mkf.ap, in0=mkf.ap, scalar=0.125,
                                                      in1=bc(rmr.ap.unsqueeze(1), [64, 4, 128]),
                                                      op0=ALU.mult, op1=ALU.mult), r=[mkf, rmr], w=[mkf])
            E('pool', lambda e: e.tensor_copy(out=V64('MKT').rearrange("p (h t) -> p h t", h=4), in_=mkf.ap),
              r=[mkf], w=[p64])
            for h in range(8):
                xb = Kp[h // 4]
                E('pe', lambda e, h=h, xb=xb: e.transpose(xb[0:64, (h % 4) * 128:(h % 4 + 1) * 128],
                                                          qn[:, h * 64:(h + 1) * 64], ident), r=[qn, cst], w=[xb])
            E('act', lambda e: e.activation(out=V64('AQT')[:, 0:512], in_=Kp[0][0:64, :], func=AF.Copy), r=[Kp[0]], w=[p64])
            E('dve', lambda e: e.tensor_copy(out=V64('AQT')[:, 512:1024], in_=Kp[1][0:64, :]), r=[Kp[1]], w=[p64])
            for h in range(2):
                E('pe', lambda e, h=h: e.transpose(Kp[2][0:64, h * 128:(h + 1) * 128], qn[:, 512 + h * 64:512 + (h + 1) * 64],
                                                   ident), r=[qn, cst], w=[Kp[2]])
            E('act', lambda e: e.activation(out=V64('AKT'), in_=Kp[2][0:64, 0:256], func=AF.Copy), r=[Kp[2]], w=[p64])
            for h in range(4):
                E('pe', lambda e, h=h: e.transpose(Kp[0][0:64, h * 128:(h + 1) * 128], rqk[:, h * 64:(h + 1) * 64], ident),
                  r=[rqk, cst], w=[Kp[0]])
            E('act', lambda e: e.activation(out=V64('RQT'), in_=Kp[0][0:64, :], func=AF.Copy), r=[Kp[0]], w=[p64])
            for h in range(4):
                E('pe', lambda e, h=h: e.transpose(Kp[1][0:64, h * 128:(h + 1) * 128],
                                                   rqk[:, 256 + h * 64:256 + (h + 1) * 64], ident),
                  r=[rqk, cst], w=[Kp[1]])
            E('dve', lambda e: e.tensor_copy(out=V64('RKT'), in_=Kp[1][0:64, :]), r=[Kp[1]], w=[p64])
            for h in range(4):
                E('pe', lambda e, h=h: e.transpose(Kp[2][:, 256 + h * 64:256 + (h + 1) * 64], mkf[:, h, :], ident[0:64, 0:64]),
                  r=[mkf, cst], w=[Kp[2]])
            E('act', lambda e: e.activation(out=mktok.ap, in_=Kp[2][:, 256:512], func=AF.Copy), r=[Kp[2]], w=[mktok])
            E('pe', lambda e: e.matmul(Kp[3][0:16, 0:130], sel16.ap, vun.ap.rearrange("p h d -> p (h d)"), start=True, stop=True),
              r=[sel16, vun], w=[Kp[3]])
            E('act', lambda e: e.activation(out=vmn.ap, in_=Kp[3][0:16, 0:130], func=AF.Copy), r=[Kp[3]], w=[vmn])
            for d_ in range(2):
                for h in range(4):
                    E('pe', lambda e, d_=d_, h=h: e.matmul(
                        Kp[0][0:64, d_ * 256 + h * 64:d_ * 256 + (h + 1) * 64], rkw[:, d_, h * 64:(h + 1) * 64],
                        V128('RV')[:, h * 64:(h + 1) * 64], start=True, stop=True), r=[rkw, p128], w=[Kp[0]])
            E('act', lambda e: e.activation(out=kvf.ap, in_=Kp[0][0:64, 0:256], func=AF.Copy), r=[Kp[0]], w=[kvf])
            E('act', lambda e: e.activation(out=b64[:, 0:256], in_=Kp[0][0:64, 256:512], func=AF.Copy), r=[Kp[0]], w=[b64])
            G4 = G.ap.rearrange("p (d k h) -> p d k h", d=2, k=2)
            gl3 = gl.ap.rearrange("p (d h) -> p d h", d=2)
            E('act', lambda e: e.activation(out=gl3, in_=G4[:, :, 1, :], func=AF.Exp, scale=-1.0), r=[G], w=[gl])
            E('act', lambda e: e.activation(out=gl.ap, in_=gl.ap, func=AF.Ln, bias=1.0), r=[gl], w=[gl])
            E('pe', lambda e: e.matmul(Kp[1][:, 0:4], triF, gl[:, 0:4], start=True, stop=True), r=[cst, gl], w=[Kp[1]])
            E('pe', lambda e: e.matmul(Kp[1][:, 4:8], triB, gl[:, 4:8], start=True, stop=True), r=[cst, gl], w=[Kp[1]])
            E('pe', lambda e: e.matmul(Kp[1][:, 8:16], ones.ap, gl.ap, start=True, stop=True), r=[ones, gl], w=[Kp[1]])
            GT = VF('GT')
            E('dve', lambda e: e.tensor_scalar(out=GT[:, 8:16], in0=Kp[1][:, 0:8], scalar1=-1.0, scalar2=None, op0=ALU.mult),
              r=[Kp[1]], w=[f128])
            E('dve', lambda e: e.tensor_tensor(out=GT[:, 0:8].rearrange("p (d h) -> p d h", d=2), in0=Kp[1][:, 0:8].rearrange("p (d h) -> p d h", d=2),
                                               in1=G4[:, :, 0, :], op=ALU.add), r=[Kp[1], G], w=[f128])
            E('dve', lambda e: e.tensor_scalar(out=tot[:, 8:16], in0=Kp[1][:, 8:16], scalar1=-1.0, scalar2=None, op0=ALU.mult),
              r=[Kp[1]], w=[tot])
            E('pe', lambda e: e.matmul(Kp[2][0:4, 0:128], GT[:, 0:4], ident, start=True, stop=True), r=[f128, cst], w=[Kp[2]])
            E('pe', lambda e: e.matmul(Kp[2][0:4, 128:256], GT[:, 4:8], ident, start=True, stop=True), r=[f128, cst], w=[Kp[2]])
            E('dve', lambda e: e.tensor_copy(out=aT.ap, in_=Kp[2][0:4, 0:256]), r=[Kp[2]], w=[aT])
            E('dve', lambda e: e.tensor_tensor_scan(out=cmT[:, 0:128], data0=ones[0:4, :], data1=aT[:, 0:128],
                                                    initial=-3.0e38, op0=ALU.mult, op1=ALU.max), r=[ones, aT], w=[cmT])
            pst = cmT.ap.ap[0][0]
            rev_o = bass.AP(cmT.ap.tensor, cmT.ap.offset + 255, [[pst, 4], [-1, 128]])
            pst2 = aT.ap.ap[0][0]
            rev_i = bass.AP(aT.ap.tensor, aT.ap.offset + 255, [[pst2, 4], [-1, 128]])
            E('dve', lambda e: e.tensor_tensor_scan(out=rev_o, data0=ones[0:4, :], data1=rev_i,
                                                    initial=-3.0e38, op0=ALU.mult, op1=ALU.max), r=[ones, aT], w=[cmT])
            for d_ in range(2):
                col = 127 if d_ == 0 else 128
                E('dve', lambda e, d_=d_, col=col: e.tensor_scalar(out=dg[:, d_ * 4:d_ * 4 + 4], in0=ident[0:4, 0:4],
                                                                    scalar1=cmT[:, col:col + 1], scalar2=None, op0=ALU.mult),
                  r=[cst, cmT], w=[dg])
            E('pe', lambda e: e.matmul(Kp[1][:, 16:24], ones[0:4, :], dg.ap, start=True, stop=True), r=[ones, dg], w=[Kp[1]])
            E('dve', lambda e: e.tensor_copy(out=tot[:, 0:8], in_=Kp[1][:, 16:24]), r=[Kp[1]], w=[tot])
            E('pe', lambda e: e.matmul(Kp[1][:, 24:28], cmT[:, 0:128], ident[0:4, 0:4], start=True, stop=True), r=[cmT, cst], w=[Kp[1]])
            E('pe', lambda e: e.matmul(Kp[1][:, 28:32], cmT[:, 128:256], ident[0:4, 0:4], start=True, stop=True), r=[cmT, cst], w=[Kp[1]])
            E('dve', lambda e: e.tensor_copy(out=GT[:, 16:24], in_=Kp[1][:, 24:32]), r=[Kp[1]], w=[f128])
            E('dve', lambda e: e.tensor_tensor(out=eend.ap, in0=GT[:, 0:8], in1=tot[:, 0:8], op=ALU.subtract), r=[f128, tot], w=[eend])
            E('act', lambda e: e.activation(out=eend.ap, in_=eend.ap, func=AF.Exp), r=[eend], w=[eend])
            for d_ in range(2):
                E('dve', lambda e, d_=d_: e.tensor_tensor(
                    out=mke[:, d_, :].rearrange("p (h d) -> p h d", d=64), in0=mktok.ap.rearrange("p (h d) -> p h d", d=64),
                    in1=bc(eend[:, d_ * 4:d_ * 4 + 4].unsqueeze(2), [128, 4, 64]), op=ALU.mult), r=[mktok, eend], w=[mke])
            for d_ in range(2):
                for h in range(4):
                    dst = Kp[1][0:64, 64 + h * 65:64 + (h + 1) * 65] if d_ == 0 else Kp[2][0:64, h * 65:(h + 1) * 65]
                    E('pe', lambda e, d_=d_, h=h, dst=dst: e.matmul(dst, mke[:, d_, h * 64:(h + 1) * 64],
                                                                     V128('MVA')[:, h * 65:(h + 1) * 65], start=True, stop=True),
                      r=[mke, p128], w=[Kp[1] if d_ == 0 else Kp[2]])
            E('act', lambda e: e.activation(out=b64[:, 256:516], in_=Kp[2][0:64, 0:260], func=AF.Copy), r=[Kp[2]], w=[b64])
            E('act', lambda e: e.activation(out=kvnf.ap, in_=Kp[1][0:64, 64:324], func=AF.Copy), r=[Kp[1]], w=[kvnf])
            B.dma(bv(ix(SAV, 0), 128, 130), p128[:, 0:130], tr=[p128])
            B.dma(bv(ix(S128b, 0), 128, 516), p128[:, 130:646], tr=[p128])
            B.dma(bv(ix(SF128, 0), 128, F128W), f128.ap, tr=[f128])
            B.dma(bv(ix(SCMT, 0), 4, 256), cmT.ap, tr=[cmT])
            B.dma(bv(ix(SB64, 0), 64, 516), b64.ap, tr=[b64])

        def stepP_state(ix, u):
            rB = rinfB
            E('dve', lambda e: e.tensor_tensor(out=mtmp[0:16, :], in0=vmn.ap, in1=metaV.ap, op=ALU.subtract),
              r=[vmn, metaV], w=[mtmp])
            E('dve', lambda e: e.scalar_tensor_tensor(out=metaV.ap, in0=mtmp[0:16, :], scalar=rB[0:16, 2:3], in1=metaV.ap,
                                                      op0=ALU.mult, op1=ALU.add), r=[mtmp, rB, metaV], w=[metaV])
            E('dve', lambda e: e.tensor_copy(out=mvb.ap, in_=metaV.ap), r=[metaV], w=[mvb])
            akm = V64('AKT').rearrange("p (h t) -> p h t", h=2)[:, :, 112:128]
            mk3 = metaK.ap.rearrange("p (h t) -> p h t", h=2)
            mt3 = mtmp[:, 0:32].rearrange("p (h t) -> p h t", h=2)
            E('dve', lambda e: e.tensor_tensor(out=mt3, in0=akm, in1=mk3, op=ALU.subtract), r=[p64, metaK], w=[mtmp])
            E('dve', lambda e: e.scalar_tensor_tensor(out=metaK.ap, in0=mtmp[:, 0:32], scalar=rB[0:64, 2:3], in1=metaK.ap,
                                                      op0=ALU.mult, op1=ALU.add), r=[mtmp, rB, metaK], w=[metaK])
            E('dve', lambda e: e.tensor_copy(out=V64('MKM'), in_=metaK.ap), r=[metaK], w=[p64])
            E('dve', lambda e: e.tensor_scalar(out=Rf.ap, in0=Rf.ap, scalar1=rB[0:64, 3:4], scalar2=None, op0=ALU.mult),
              r=[Rf, rB], w=[Rf])
            E('act', lambda e: e.activation(out=V64('RF'), in_=Rf.ap, func=AF.Copy), r=[Rf], w=[p64])
            E('dve', lambda e: e.tensor_tensor(out=Rf.ap.rearrange("p (h d) -> p h d", d=64),
                                               in0=Rf.ap.rearrange("p (h d) -> p h d", d=64),
                                               in1=bc(der[0:64, 8:12].unsqueeze(2), [64, 4, 64]), op=ALU.mult),
              r=[Rf, der], w=[Rf])
            E('dve', lambda e: e.tensor_tensor(out=Rf.ap, in0=Rf.ap, in1=kvf.ap, op=ALU.add), r=[Rf, kvf], w=[Rf])
            E('dve', lambda e: e.tensor_scalar(out=Sf.ap, in0=Sf.ap, scalar1=rB[0:64, 3:4], scalar2=None, op0=ALU.mult),
              r=[Sf, rB], w=[Sf])
            E('dve', lambda e: e.tensor_scalar(out=mF.ap, in0=mF.ap, scalar1=rB[:, 3:4], scalar2=None, op0=ALU.mult),
              r=[mF, rB], w=[mF])
            E('act', lambda e: e.activation(out=V64('SF'), in_=Sf.ap, func=AF.Copy), r=[Sf], w=[p64])
            E('dve', lambda e: e.tensor_copy(out=srow[0:1, 0:4], in_=mF[0:1, :]), r=[mF], w=[srow])
            E('dve', lambda e: e.tensor_copy(out=srow[0:1, 4:8], in_=tot[0:1, 4:8]), r=[tot], w=[srow])
            E('dve', lambda e: e.tensor_copy(out=srow[0:1, 8:12], in_=tot[0:1, 12:16]), r=[tot], w=[srow])
            mlstm_update(mF, Sf, tot[:, 0:4], tot[:, 8:12], kvnf.ap, kvnf, [tot])
            B.dma(bv(ix(S64a, 0), 64, 1024), p64[:, 0:1024], tr=[p64])
            B.dma(bv(ix(SAK, 0), 64, 256), p64[:, 1024:1280], tr=[p64])
            B.dma(bv(ix(S64b, 0), 64, 1536), p64[:, 1280:2816], tr=[p64])
            B.dma(bv(ix(S64c, 0), 64, 1060), p64[:, 2816:3876], tr=[p64])
            B.dma(bv(ix(SMV, 0), 16, 130), mvb.ap, tr=[mvb])
            B.dma(bv(ix(SROW, 0), 1, 12), srow[0:1, 0:12], tr=[srow])

        def mlstm_update(m_t, S_t, totc, blc, kvn_ap, kvn_tile, extra_r):
            E('dve', lambda e: e.tensor_tensor(out=sc4[:, 0:4], in0=m_t.ap, in1=totc, op=ALU.max), r=[m_t] + extra_r, w=[sc4])
            E('dve', lambda e: e.tensor_tensor(out=sc4[:, 4:8], in0=m_t.ap, in1=sc4[:, 0:4], op=ALU.subtract), r=[m_t, sc4], w=[sc4])
            E('dve', lambda e: e.tensor_tensor(out=sc4[:, 8:12], in0=totc, in1=sc4[:, 0:4], op=ALU.subtract), r=extra_r + [sc4], w=[sc4])
            E('act', lambda e: e.activation(out=sc4[:, 4:12], in_=sc4[:, 4:12], func=AF.Exp), r=[sc4], w=[sc4])
            E('dve', lambda e: e.tensor_tensor(out=S_t.ap.rearrange("p (h d) -> p h d", d=65),
                                               in0=S_t.ap.rearrange("p (h d) -> p h d", d=65),
                                               in1=bc(sc4[0:64, 4:8].unsqueeze(2), [64, 4, 65]), op=ALU.mult), r=[S_t, sc4], w=[S_t])
            E('dve', lambda e: e.tensor_tensor(out=stmp.ap.rearrange("p (h d) -> p h d", d=65),
                                               in0=kvn_ap.rearrange("p (h d) -> p h d", d=65),
                                               in1=bc(sc4[0:64, 8:12].unsqueeze(2), [64, 4, 65]), op=ALU.mult),
              r=[kvn_tile, sc4], w=[stmp])
            E('dve', lambda e: e.tensor_tensor(out=S_t.ap, in0=S_t.ap, in1=stmp.ap, op=ALU.add), r=[S_t, stmp], w=[S_t])
            E('dve', lambda e: e.tensor_tensor(out=m_t.ap, in0=blc, in1=sc4[:, 0:4], op=ALU.add), r=extra_r + [sc4], w=[m_t])

        def bodyP(i):
            streams = []
            ixs = [lambda T, off, u=u: pairs(T, u + off, NI)[i][0] for u in range(2)]
            for u in range(2):
                CUR[0] = u
                B.rec = []
                stepP_main(ixs[u], u)
                streams.append(B.rec)
                B.rec = None
            CUR[0] = 0
            merged = []
            for a_, b_ in zip(*streams):
                merged.append(a_)
                merged.append(b_)
            B.play(merged)
            for u in range(2):
                CUR[0] = u
                stepP_state(ixs[u], u)
            CUR[0] = 0
            shift_win(256)

        B.loop(NI, bodyP)

        def bodyB(i):
            def two(T):
                return pairs(T, 0, NI)[(NI - 1) - i]
            B.dma(b64B.ap, two(SB64)[:, 0:64 * 516].rearrange("u (p c) -> p u c", c=516), tw=[b64B])
            B.dma(scbB.ap, bc(two(SROW)[:, 0:12].unsqueeze(0), [128, 2, 12]), tw=[scbB])
            B.dma(rinfB2.ap, two(RI)[:, 0:128 * 4].rearrange("u (p c) -> p u c", c=4), tw=[rinfB2])
            for idx in (1, 0):
                E('act', lambda e, idx=idx: e.activation(out=o64B[:, idx, 0:256], in_=Rb.ap, func=AF.Copy), r=[Rb], w=[o64B])
                E('act', lambda e, idx=idx: e.activation(out=o64B[:, idx, 256:516], in_=Sb.ap, func=AF.Copy), r=[Sb], w=[o64B])
                E('dve', lambda e, idx=idx: e.tensor_copy(out=mbrowB[0:1, idx, :], in_=mB[0:1, :]), r=[mB], w=[mbrowB])
                E('dve', lambda e: e.tensor_tensor(out=Rb.ap.rearrange("p (h d) -> p h d", d=64),
                                                   in0=Rb.ap.rearrange("p (h d) -> p h d", d=64),
                                                   in1=bc(der[0:64, 12:16].unsqueeze(2), [64, 4, 64]), op=ALU.mult), r=[Rb, der], w=[Rb])
                E('dve', lambda e, idx=idx: e.tensor_tensor(out=Rb.ap, in0=Rb.ap, in1=b64B[:, idx, 0:256], op=ALU.add), r=[Rb, b64B], w=[Rb])
                mlstm_update(mB, Sb, scbB[:, idx, 4:8], scbB[:, idx, 8:12], b64B[:, idx, 256:516], b64B, [scbB])
                E('dve', lambda e, idx=idx: e.tensor_scalar(out=Rb.ap, in0=Rb.ap, scalar1=rinfB2[0:64, idx, 3:4], scalar2=None, op0=ALU.mult), r=[Rb, rinfB2], w=[Rb])
                E('dve', lambda e, idx=idx: e.tensor_scalar(out=Sb.ap, in0=Sb.ap, scalar1=rinfB2[0:64, idx, 3:4], scalar2=None, op0=ALU.mult), r=[Sb, rinfB2], w=[Sb])
                E('dve', lambda e, idx=idx: e.tensor_scalar(out=mB.ap, in0=mB.ap, scalar1=rinfB2[:, idx, 3:4], scalar2=None, op0=ALU.mult), r=[mB, rinfB2], w=[mB])
            B.dma(two(SO64)[:, 0:64 * 516].rearrange("u (p c) -> p u c", c=516), o64B.ap, tr=[o64B])
            B.dma(two(SMB)[:, 0:4].unsqueeze(0), mbrowB.ap, tr=[mbrowB])

        B.loop(NI, bodyB)

        load_weight(Wout, [[(w_out_k[k], 0, D)] for k in range(8)], l, D, None)

        def Q64(n):
            a, b = P64[n]
            return q64.ap[:, a:b]

        def Q128(n):
            a, b = P128[n]
            return q128.ap[:, a:b]

        def QF(n):
            a, b = F128[n]
            return g128.ap[:, a:b]

        def stepF(ix, u):
            B.dma(hA.ap, ix(H, 1), tw=[hA])
            B.dma(q64[:, 0:1024], bv(ix(S64a, 1), 64, 1024), tw=[q64])
            B.dma(q64[:, 1280:2816], bv(ix(S64b, 1), 64, 1536), tw=[q64])
            B.dma(q64[:, 2816:3876], bv(ix(S64c, 1), 64, 1060), tw=[q64])
            B.dma(q128[:, 130:646], bv(ix(S128b, 1), 128, 516), tw=[q128])
            B.dma(g128.ap, bv(ix(SF128, 1), 128, F128W), tw=[g128])
            B.dma(mv16.ap, bv(ix(SMV, 1), 16, 130), tw=[mv16])
            B.dma(cmb.ap, bvb(ix(SCMT, 1), 1024, 128), tw=[cmb])
            B.dma(mpv[:, 0:4], bvb(ix(SROW, 1), 4, 128), tw=[mpv])
            B.dma(mpv[:, 4:8], bvb(ix(SMB, 1), 4, 128), tw=[mpv])
            B.dma(r64.ap, bv(ix(SO64, 1), 64, 516), tw=[r64])
            KM = [K[1], K[2], K[3], K[0]]
            KW = [K[3], K[0]]
            aqt = Q64('AQT').rearrange("p (h t) -> p h t", h=8)
            for kvh in range(2):
                ksrc = [kn[:, u + nb_, kvh * 128:(kvh + 1) * 128] for nb_ in range(3)]
                ktile = [kn, kn, kn]
                vsrc = [vn[:, u + nb_, kvh * 65:(kvh + 1) * 65] for nb_ in range(3)]
                vtile = [vn, vn, vn]
                qrhs = aqt[:, 4 * kvh:4 * kvh + 4, :]
                for nb in range(3):
                    E('pe', lambda e, nb=nb, ksrc=ksrc, qrhs=qrhs: e.matmul(K[nb].ap, ksrc[nb], qrhs, start=True, stop=True), r=[ktile[nb], q64], w=[K[nb]])
                E('pe', lambda e, kvh=kvh, qrhs=qrhs: e.matmul(K[3][0:16, :], Q64('MKM')[:, kvh * 16:(kvh + 1) * 16], qrhs, start=True, stop=True),
                  r=[q64], w=[K[3]])
                for nb in range(3):
                    if nb == 0:
                        E('act', lambda e, nb=nb: e.activation(out=stmpF.ap, in_=K[nb].ap, func=AF.Exp, bias=g128[:, 536:537]),
                          r=[K[nb], g128], w=[stmpF])
                    else:
                        E('act', lambda e, nb=nb: e.activation(out=stmpF.ap, in_=K[nb].ap, func=AF.Exp), r=[K[nb]], w=[stmpF])
                    E('pool', lambda e, nb=nb, kvh=kvh: e.tensor_tensor(out=ptb[nb].ap, in0=stmpF.ap, in1=Eb[:, kvh * 3 + nb, :],
                                                                        op=ALU.mult), r=[stmpF, Eb], w=[ptb[nb]])
                E('act', lambda e: e.activation(out=pmb.ap, in_=K[3][0:16, :], func=AF.Exp), r=[K[3]], w=[pmb])
                for g in range(4):
                    for nb in range(3):
                        E('pe', lambda e, g=g, nb=nb, vsrc=vsrc: e.matmul(K[0][:, g * 65:(g + 1) * 65], ptb[nb][:, g * 128:(g + 1) * 128],
                                                               vsrc[nb], start=(nb == 0), stop=False), r=[ptb[nb], vtile[nb]], w=[K[0]])
                    E('pe', lambda e, g=g, kvh=kvh: e.matmul(K[0][:, g * 65:(g + 1) * 65], pmb[:, g * 128:(g + 1) * 128],
                                                    mv16[:, kvh * 65:(kvh + 1) * 65], start=False, stop=True), r=[pmb, mv16], w=[K[0]])
                o3 = K[0][:, 0:260].rearrange("p (h d) -> p h d", d=65)
                E('dve', lambda e, o3=o3, kvh=kvh: e.tensor_tensor(out=sm8[:, 32:36], in0=o3[:, :, 64], in1=der[:, 32 + 4 * kvh:36 + 4 * kvh], op=ALU.add),
                  r=[K[0], der], w=[sm8])
                E('dve', lambda e: e.reciprocal(out=sm8[:, 36:40], in_=sm8[:, 32:36]), r=[sm8], w=[sm8])
                E('dve', lambda e, o3=o3, kvh=kvh: e.tensor_tensor(out=mixed[:, kvh * 256:(kvh + 1) * 256].rearrange("p (h d) -> p h d", d=64),
                                                   in0=o3[:, :, 0:64], in1=bc(sm8[:, 36:40].unsqueeze(2), [128, 4, 64]), op=ALU.mult),
                  r=[K[0], sm8], w=[mixed])
            rqt = Q64('RQT').rearrange("p (h t) -> p h t", h=4)
            rkt = Q64('RKT').rearrange("p (h t) -> p h t", h=4)
            for h in range(4):
                E('pe', lambda e, h=h: e.matmul(K[1][:, h * 128:(h + 1) * 128], rkt[:, h, :], rqt[:, h, :], start=True, stop=True),
                  r=[q64], w=[K[1]])
            E('dve', lambda e: e.tensor_tensor(out=wtr.ap, in0=K[1].ap, in1=DT.ap.rearrange("p h t -> p (h t)"), op=ALU.mult),
              r=[K[1], DT], w=[wtr])
            for h in range(4):
                E('pe', lambda e, h=h: e.matmul(K[2][:, h * 64:(h + 1) * 64], wtr[:, h * 128:(h + 1) * 128],
                                                Q128('RV')[:, h * 64:(h + 1) * 64], start=True, stop=True), r=[wtr, q128], w=[K[2]])
            for h in range(4):
                E('pe', lambda e, h=h: e.matmul(K[2][:, 256 + h * 64:256 + (h + 1) * 64], rqt[:, h, :],
                                                Q64('RF')[:, h * 64:(h + 1) * 64], start=True, stop=True), r=[q64], w=[K[2]])
            for h in range(4):
                E('pe', lambda e, h=h: e.matmul(K[3][:, h * 64:(h + 1) * 64], rqt[:, h, :],
                                                r64[:, h * 64:(h + 1) * 64], start=True, stop=True), r=[q64, r64], w=[K[3]])
            ro = oacc[:, 0:256].rearrange("p (h d) -> p h d", d=64)
            E('dve', lambda e: e.tensor_tensor(out=ro, in0=K[2][:, 256:512].rearrange("p (h d) -> p h d", d=64),
                                               in1=bc(der[:, 16:20].unsqueeze(2), [128, 4, 64]), op=ALU.mult), r=[K[2], der], w=[oacc])
            E('dve', lambda e: e.tensor_tensor(out=oacc[:, 0:256], in0=oacc[:, 0:256], in1=K[2][:, 0:256], op=ALU.add), r=[oacc, K[2]], w=[oacc])
            rb3 = obuf[:, 0:256].rearrange("p (h d) -> p h d", d=64)
            E('dve', lambda e: e.tensor_tensor(out=rb3, in0=K[3][:, 0:256].rearrange("p (h d) -> p h d", d=64),
                                               in1=bc(der[:, 20:24].unsqueeze(2), [128, 4, 64]), op=ALU.mult), r=[K[3], der], w=[obuf])
            E('dve', lambda e: e.tensor_tensor(out=oacc[:, 0:256], in0=oacc[:, 0:256], in1=obuf[:, 0:256], op=ALU.add), r=[oacc, obuf], w=[oacc])
            headnorm(ro, mixed[:, 512:768].rearrange("p (h d) -> p h d", d=64),
                     par[:, 144:400].rearrange("p (h d) -> p h d", d=64), QF('RG').rearrange("p (h d) -> p h d", d=64),
                     [oacc], [mixed])
            mqt = Q64('MQT').rearrange("p (h t) -> p h t", h=4)
            mkt = Q64('MKT').rearrange("p (h t) -> p h t", h=4)
            GT = QF('GT')
            for h in range(4):
                E('pe', lambda e, h=h: e.matmul(K[0][:, h * 128:(h + 1) * 128], mkt[:, h, :], mqt[:, h, :], start=True, stop=True),
                  r=[q64], w=[K[0]])
            cm4 = cmb.ap.rearrange("p (h d t) -> p h d t", h=4, d=2)
            mp3 = mpv.ap.rearrange("p (d h) -> p h d", d=2)
            E('dve', lambda e: e.tensor_tensor(out=cm4, in0=cm4, in1=bc(mp3.unsqueeze(3), [128, 4, 2, 128]), op=ALU.max),
              r=[cmb, mpv], w=[cmb])
            for h in range(4):
                for d_ in range(2):
                    E('dve', lambda e, h=h, d_=d_: e.tensor_scalar(out=cm4[:, h, d_, :], in0=cm4[:, h, d_, :],
                                                                    scalar1=GT[:, d_ * 4 + h:d_ * 4 + h + 1], scalar2=0.0,
                                                                    op0=ALU.subtract, op1=ALU.max), r=[cmb, g128], w=[cmb])
            E('act', lambda e: e.activation(out=cmb.ap, in_=cmb.ap, func=AF.Exp, scale=-1.0), r=[cmb], w=[cmb])
            msk = cst[:, C_TF:C_TF + 256].rearrange("p (d t) -> p d t", d=2)
            E('pool', lambda e: e.tensor_tensor(out=cm4, in0=cm4, in1=bc(msk.unsqueeze(1), [128, 4, 2, 128]), op=ALU.mult),
              r=[cmb, cst], w=[cmb])
            E('dve', lambda e: e.tensor_tensor(out=wtm.ap.rearrange("p (h d t) -> p h d t", h=4, d=2), in0=cm4,
                                               in1=bc(K[0].ap.rearrange("p (h t) -> p h t", h=4).unsqueeze(2), [128, 4, 2, 128]),
                                               op=ALU.mult), r=[cmb, K[0]], w=[wtm])
            wt4 = wtm.ap.rearrange("p (h d t) -> p h d t", h=4, d=2)
            for d_ in range(2):
                for h in range(4):
                    E('pe', lambda e, d_=d_, h=h: e.matmul(KM[d_][:, h * 65:(h + 1) * 65], wt4[:, h, d_, :],
                                                           Q128('MVA')[:, h * 65:(h + 1) * 65], start=True, stop=True),
                      r=[wtm, q128], w=[KM[d_]])
                for h in range(4):
                    st_src = Q64('SF')[:, h * 65:(h + 1) * 65] if d_ == 0 else r64[:, 256 + h * 65:256 + (h + 1) * 65]
                    E('pe', lambda e, d_=d_, h=h, st_src=st_src: e.matmul(KM[2 + d_][:, h * 65:(h + 1) * 65], mqt[:, h, :], st_src,
                                                                           start=True, stop=True), r=[q64, r64], w=[KM[2 + d_]])
            E('dve', lambda e: e.tensor_tensor(out=sm8[:, 40:48], in0=GT[:, 16:24], in1=mpv.ap, op=ALU.max), r=[g128, mpv], w=[sm8])
            E('dve', lambda e: e.tensor_tensor(out=sm8[:, 48:56], in0=mpv.ap, in1=sm8[:, 40:48], op=ALU.subtract), r=[mpv, sm8], w=[sm8])
            E('dve', lambda e: e.scalar_tensor_tensor(out=sm8[:, 56:64], in0=GT[:, 8:16], scalar=-1.0, in1=sm8[:, 40:48],
                                                      op0=ALU.mult, op1=ALU.subtract), r=[g128, sm8], w=[sm8])
            E('act', lambda e: e.activation(out=sm8[:, 48:64], in_=sm8[:, 48:64], func=AF.Exp), r=[sm8], w=[sm8])
            for d_ in range(2):
                nd = obuf[:, 0:260].rearrange("p (h d) -> p h d", d=65)
                E('dve', lambda e, d_=d_: e.tensor_tensor(out=nd, in0=KM[2 + d_][:, 0:260].rearrange("p (h d) -> p h d", d=65),
                                                          in1=bc(sm8[:, 48 + 4 * d_:52 + 4 * d_].unsqueeze(2), [128, 4, 65]), op=ALU.mult),
                  r=[KM[2 + d_], sm8], w=[obuf])
                E('dve', lambda e, d_=d_: e.tensor_tensor(out=obuf[:, 0:260], in0=obuf[:, 0:260], in1=KM[d_][:, 0:260], op=ALU.add),
                  r=[obuf, KM[d_]], w=[obuf])
                E('dve', lambda e, d_=d_: e.scalar_tensor_tensor(out=sm8[:, 20:24], in0=nd[:, :, 64], scalar=-1.0, in1=nd[:, :, 64],
                                                                 op0=ALU.mult, op1=ALU.max), r=[obuf], w=[sm8])
                E('dve', lambda e, d_=d_: e.tensor_tensor(out=sm8[:, 24:28], in0=sm8[:, 20:24], in1=sm8[:, 56 + 4 * d_:60 + 4 * d_], op=ALU.max),
                  r=[sm8], w=[sm8])
                E('dve', lambda e: e.reciprocal(out=sm8[:, 28:32], in_=sm8[:, 24:28]), r=[sm8], w=[sm8])
                hm = oacc[:, 260:516].rearrange("p (h d) -> p h d", d=64)
                if d_ == 0:
                    E('dve', lambda e: e.tensor_tensor(out=hm, in0=nd[:, :, 0:64], in1=bc(sm8[:, 28:32].unsqueeze(2), [128, 4, 64]), op=ALU.mult),
                      r=[obuf, sm8], w=[oacc])
                else:
                    E('dve', lambda e: e.tensor_tensor(out=nd[:, :, 0:64], in0=nd[:, :, 0:64], in1=bc(sm8[:, 28:32].unsqueeze(2), [128, 4, 64]), op=ALU.mult),
                      r=[obuf, sm8], w=[obuf])
                    E('dve', lambda e: e.tensor_tensor(out=hm, in0=hm, in1=nd[:, :, 0:64], op=ALU.add), r=[oacc, obuf], w=[oacc])
            hm = oacc[:, 260:516].rearrange("p (h d) -> p h d", d=64)
            headnorm(hm, mixed[:, 768:1024].rearrange("p (h d) -> p h d", d=64),
                     par[:, 400:656].rearrange("p (h d) -> p h d", d=64), QF('MO').rearrange("p (h d) -> p h d", d=64),
                     [oacc], [mixed])
            for half in range(2):
                for kk in range(4):
                    k = half * 4 + kk
                    E('pe', lambda e, k=k, kk=kk, half=half: e.transpose(K[1 + half][:, kk * 128:(kk + 1) * 128],
                                                                          mixed[:, k * 128:(k + 1) * 128], ident), r=[mixed, cst], w=[K[1 + half]])
                E('act', lambda e, half=half: e.activation(out=mixT[:, half * 4:half * 4 + 4, :],
                                                           in_=K[1 + half].ap.rearrange("p (k c) -> p k c", k=4), func=AF.Copy),
                  r=[K[1 + half]], w=[mixT])
            for n in range(2):
                for k in range(8):
                    E('pe', lambda e, n=n, k=k: e.matmul(KW[n].ap, mixT[:, k, :], Wout[:, k, n * 512:(n + 1) * 512],
                                                         start=(k == 0), stop=(k == 7)), r=[mixT, wdn], w=[KW[n]])
                E('dve', lambda e, n=n: e.tensor_tensor(out=hA[:, n * 512:(n + 1) * 512], in0=hA[:, n * 512:(n + 1) * 512],
                                                        in1=KW[n].ap, op=ALU.add), r=[hA, KW[n]], w=[hA])
            B.dma(ix(H, 1), hA.ap, tr=[hA])

        for kvh in range(2):
            for nb in range(3):
                for g in range(4):
                    slope = float(2.0 ** (-8.0 * (4 * kvh + g + 1) / 8.0))
                    E('act', lambda e, kvh=kvh, nb=nb, g=g, slope=slope: e.activation(
                        out=Eb[:, kvh * 3 + nb, g * 128:(g + 1) * 128], in_=cst[:, C_DIST + nb * 128:C_DIST + (nb + 1) * 128],
                        func=AF.Exp, scale=-slope), r=[cst], w=[Eb])
        B.dma(kn[:, 2, :], bv(SAK[0], 64, 256), tw=[kn])
        B.dma(kn[:, 3, :], bv(SAK[1], 64, 256), tw=[kn])
        B.dma(vn[:, 2, :], bv(SAV[0], 128, 130), tw=[vn])
        B.dma(vn[:, 3, :], bv(SAV[1], 128, 130), tw=[vn])

        def bodyF(i):
            for sl in range(2):
                E('pool', lambda e, sl=sl: e.tensor_copy(out=kn[:, sl, :], in_=kn[:, sl + 2, :]), r=[kn], w=[kn])
                E('pool', lambda e, sl=sl: e.tensor_copy(out=vn[:, sl, :], in_=vn[:, sl + 2, :]), r=[vn], w=[vn])
            for sl in range(2):
                B.dma(kn[:, 2 + sl, :], bv(pairs(SAK, 2 + sl, NJ)[i][0], 64, 256), tw=[kn])
                B.dma(vn[:, 2 + sl, :], bv(pairs(SAV, 2 + sl, NJ)[i][0], 128, 130), tw=[vn])
            streams = []
            for u in range(2):
                CUR[0] = u
                B.rec = []
                stepF(lambda T, off, u=u: pairs(T, u + off, NJ)[i][0], u)
                streams.append(B.rec)
                B.rec = None
            CUR[0] = 0
            merged = []
            if os.environ.get("MERGE", "alt") == "seq":
                merged = streams[0] + streams[1]
            else:
                for a_, b_ in zip(*streams):
                    merged.append(a_)
                    merged.append(b_)
            B.play(merged)

        B.loop(NJ, bodyF)

        load_weight(Wup, [[(ffn_up_k[k][0], 0, DFF), (ffn_up_k[k][1], DFF, 2 * DFF)] for k in range(8)], l, 2 * DFF, n2w)
        load_weight(Wdn, [[(ffn_down_k[k], 0, D)] for k in range(22)], l, D, None)
        E('pool', lambda e: e.memset(win.ap, 0.0), w=[win])
        stageA(H[0], RI[0], 257)
        shift_win(256)

        def bodyG(i):
            cur = lambda T, u: pairs(T, 0, NJ)[i][u]
            nxt = lambda T, u: pairs(T, 1, NJ)[i][u]
            stageA(nxt(H, 0), nxt(RI, 0), 129)
            stageA(nxt(H, 1), nxt(RI, 1), 257)
            for cc in range(22):
                banks = (PP[2 * (cc % 2)], PP[2 * (cc % 2) + 1])
                for gv in range(2):
                    for k in range(8):
                        E('pe', lambda e, cc=cc, gv=gv, k=k: e.matmul(
                            banks[gv][:, 0:258], Wup[:, k, gv * DFF + cc * 128:gv * DFF + (cc + 1) * 128],
                            win[:, k, 0:258], start=(k == 0), stop=(k == 7)), r=[wreg, win], w=[banks[gv]])
                for gv in range(2):
                    ch = gv * 22 + cc
                    xb = banks[gv]
                    E('act', lambda e, gv=gv, ch=ch, xb=xb: e.activation(out=cg[:, gv, :], in_=xb[:, 0:256],
                                                                          func=AF.Copy, scale=cwf[:, 0, ch:ch + 1]), r=[xb, cwf], w=[cg])
                    for tap in (1, 2):
                        E('dve', lambda e, gv=gv, ch=ch, xb=xb, tap=tap: e.scalar_tensor_tensor(
                            out=cg[:, gv, :], in0=xb[:, tap:tap + 256], scalar=cwf[:, tap, ch:ch + 1],
                            in1=cg[:, gv, :], op0=ALU.mult, op1=ALU.add), r=[xb, cwf, cg], w=[cg])
                E('act', lambda e: e.activation(out=sg.ap, in_=cg[:, 0, :], func=AF.Silu), r=[cg], w=[sg])
                E('pool', lambda e, cc=cc: e.tensor_tensor(out=actT[:, cc, :], in0=sg.ap, in1=cg[:, 1, :], op=ALU.mult),
                  r=[sg, cg], w=[actT])
            for u in range(2):
                dbank = (X[0], X[1]) if u == 0 else (X[2], PP[4])
                for n in range(2):
                    for cc in range(22):
                        E('pe', lambda e, n=n, cc=cc, u=u: e.matmul(dbank[n].ap, actT[:, cc, u * 128:(u + 1) * 128],
                                                                    Wdn[:, cc, n * 512:(n + 1) * 512],
                                                                    start=(cc == 0), stop=(cc == 21)), r=[actT, wdn], w=[dbank[n]])
                B.dma(hA.ap, cur(H, u), tw=[hA])
                for n in range(2):
                    E('dve', lambda e, n=n: e.tensor_tensor(out=hA[:, n * 512:(n + 1) * 512], in0=hA[:, n * 512:(n + 1) * 512],
                                                            in1=dbank[n].ap, op=ALU.add), r=[hA, dbank[n]], w=[hA])
                B.dma(cur(H, u), hA.ap, tr=[hA])
            shift_win(256)

        B.loop(NJ, bodyG)

    B.seg_end()
    with nc.Fori(0, depth) as l:
        layer(l)
        B.seg_end()
    B.seg_end()
    return B


def make_consts():
    c = np.zeros((128, C_W), np.float32)
    s = np.arange(128)[:, None].astype(np.float64)
    t = np.arange(128)[None, :].astype(np.float64)
    c[:, C_ID:C_ID + 128] = np.eye(128)
    c[:, C_TF:C_TF + 128] = (s <= t)
    c[:, C_TB:C_TB + 128] = (s >= t)
    c[:, C_DF:C_DF + 128] = np.maximum(t - s, 0)
    c[:, C_DB:C_DB + 128] = np.maximum(s - t, 0)
    for nb in range(3):
        kpos = (nb - 1) * 128 + s
        dist = np.abs(t - kpos)
        c[:, C_DIST + nb * 128:C_DIST + (nb + 1) * 128] = np.where(dist <= 128, dist, 1.0e6)
    sv = np.arange(128)
    c[:, C_RIDX + 0] = sv + 1
    c[:, C_RIDX + 1] = 128 - sv
    c[:, C_RIDX + 2] = 127 - sv
    c[:, C_RIDX + 3] = sv
    for m in range(16):
        c[112 + m, C_SEL + m] = 1.0
    return c


def layout_core(seqs, meta, NB):
    NBA = NB + 6
    xin = np.zeros((NBA, 128, D), np.float32)
    ri = np.zeros((NBA, 128, 4), np.float32)
    ri[:, :, 2] = 1.0
    rowmap = []
    b = 1
    for x in seqs:
        nb = x.shape[0] // 128
        xin[b, 112:128] = meta
        ri[b, 112:128, 0] = 1.0
        ri[b, :, 2] = 1.0
        xin[b + 1:b + 1 + nb] = x.reshape(nb, 128, D)
        ri[b + 1:b + 1 + nb, :, 0] = 1.0
        ri[b + 1:b + 1 + nb, :, 1] = 1.0
        ri[b + 1:b + 1 + nb, :, 2] = 0.0
        rowmap.append((b + 1, nb))
        b += 1 + nb
    assert b <= NB + 1
    ri[:, :, 3] = 1.0 - ri[:, :, 2]
    rmrow = np.ascontiguousarray(ri[:, :, 0].reshape(NBA, 1, 128))
    return xin, ri, rmrow, rowmap


_CACHE = {}


def get_program(NB, depth):
    key = (NB, depth)
    if key not in _CACHE:
        _CACHE[key] = build(NB, depth).nc
    return _CACHE[key]


def kernel(x_prompt, x_sample, meta_tokens, norm1_w, w_in, attn_q_norm_w, attn_k_norm_w, attn_sink,
           ret_decay_logit, ret_norm_w, mlstm_conv_w, mlstm_gate_b, mlstm_norm_w, w_out, norm2_w,
           ffn_up, ffn_conv_w, ffn_down):
    f = lambda a: np.ascontiguousarray(np.asarray(a, dtype=np.float32))
    x_prompt, x_sample, meta = f(x_prompt), f(x_sample), f(meta_tokens)
    depth = int(np.asarray(norm1_w).shape[0])
    Bp, Sp, _ = x_prompt.shape
    Bs, Ss, _ = x_sample.shape
    assert Sp == 2 * Ss and Bs % 2 == 0
    NB = 2 * (1 + Ss // 128)
    cores = [[x_prompt[i]] for i in range(Bp)] + [[x_sample[2 * i], x_sample[2 * i + 1]] for i in range(Bs // 2)]
    ncore = len(cores)
    assert ncore <= 8
    nc = get_program(NB, depth)
    consts = make_consts()
    LP = 16896
    LW1 = 128 * PW
    SBk = 128 * D

    def padrow(a, L):
        a = f(a).reshape(depth, -1)
        o = np.zeros((depth, L), np.float32)
        o[:, :a.shape[1]] = a
        return o
    shared = {"consts": consts}
    shared["psmall"] = padrow(np.concatenate([f(attn_q_norm_w), f(attn_k_norm_w), f(attn_sink),
                                              f(ret_decay_logit).reshape(depth, 8), f(ret_norm_w), f(mlstm_norm_w),
                                              f(mlstm_gate_b)], axis=1), LP)
    shared["pn1"] = padrow(norm1_w, LP)
    shared["pn2"] = padrow(norm2_w, LP)
    mc, fc = f(mlstm_conv_w), f(ffn_conv_w)
    for k in range(3):
        shared["pcw%d" % k] = padrow(mc[:, k], LP)
        shared["pfc%d" % k] = padrow(fc[:, k], LP)
    wi, wo, fu, fd = f(w_in), f(w_out), f(ffn_up), f(ffn_down)
    for k in range(8):
        shared["w_in_k%d" % k] = padrow(wi[:, k * 128:(k + 1) * 128, :], LW1)
        shared["w_out_k%d" % k] = padrow(wo[:, k * 128:(k + 1) * 128, :], SBk)
        for hf in range(2):
            shared["ffn_up_k%d_%d" % (k, hf)] = padrow(fu[:, k * 128:(k + 1) * 128, hf * DFF:(hf + 1) * DFF], LW1)
    for k in range(22):
        shared["ffn_down_k%d" % k] = padrow(fd[:, k * 128:(k + 1) * 128, :], SBk)
    in_maps, maps = [], []
    for seqs in cores:
        xin, ri, rmrow, rowmap = layout_core(seqs, meta, NB)
        m = dict(shared)
        m.update({"xin": xin, "rowinfo": ri, "rmrow": rmrow})
        in_maps.append(m)
        maps.append(rowmap)
    while len(in_maps) < 8:
        in_maps.append(in_maps[-1])
    res = run_bass_kernel_spmd(nc, in_maps, core_ids=list(range(len(in_maps))))
    y_p = np.zeros_like(x_prompt)
    y_s = np.zeros_like(x_sample)
    for ci in range(ncore):
        h = res.results[ci]["hout"]
        if ci < Bp:
            b0, nb = maps[ci][0]
            y_p[ci] = h[b0:b0 + nb].reshape(nb * 128, D)
        else:
            for j, (b0, nb) in enumerate(maps[ci]):
                y_s[2 * (ci - Bp) + j] = h[b0:b0 + nb].reshape(nb * 128, D)
    return (y_p, y_s)
```

```python
import numpy as np
import concourse.bass as bass
import concourse.mybir as mybir
from concourse.bass_utils import run_bass_kernel_spmd

F32 = mybir.dt.float32
BF16 = mybir.dt.bfloat16
AF = mybir.ActivationFunctionType
ALU = mybir.AluOpType
AX = mybir.AxisListType

D = 1024
PW = 2832
DFF = 2816
EPS = 1e-6
OFF = 1 << 16
import os
SES_SET = set(os.environ.get("SES", "act,dve,pool").split(","))

P64 = dict(AQT=(0, 1024), AKT=(1024, 1280), RQT=(1280, 1792), RKT=(1792, 2304), MQT=(2304, 2816),
           MKT=(2816, 3328), RF=(3328, 3584), SF=(3584, 3844), MKM=(3844, 3876))
P64W = 3876
P128 = dict(AV=(0, 130), RV=(130, 386), MVA=(386, 646))
P128W = 646
F128 = dict(RG=(0, 256), MO=(256, 512), GT=(512, 536))
F128W = 537
C_ID, C_TF, C_TB, C_DF, C_DB, C_DIST, C_RIDX, C_SEL, C_W = 0, 128, 256, 384, 512, 640, 1024, 1028, 1044


class T:
    def __init__(self, name, ap):
        self.name = name
        self.ap = ap
        self.w = None
        self.r = {}
        self.dsem = None
        self.dloc = 0

    def __getitem__(self, k):
        return self.ap[k]


CUR = [0]


class Dual:
    def __init__(self, a, b):
        self.s = (a, b)

    def cur(self):
        return self.s[CUR[0]]

    @property
    def ap(self):
        return self.cur().ap

    def __getitem__(self, k):
        return self.cur().ap[k]


def _res(ts):
    return [t.cur() if isinstance(t, Dual) else t for t in ts]


class Bld:
    def __init__(self, tables=None):
        self.nc = bass.Bass("TRN2", target_bir_lowering=False)
        nc = self.nc
        self.eng = {'pe': nc.tensor, 'act': nc.scalar, 'dve': nc.vector, 'pool': nc.gpsimd, 'sp': nc.sync}
        self.sem = {e: nc.alloc_semaphore('s_' + e) for e in self.eng}
        self.loc = {e: 0 for e in self.eng}
        self.tiles = {}
        self.waited = {e: {} for e in self.eng}
        self.common = [0, 0]
        self.nwait = 0
        self.rec = None

    def sb(self, name, shape, dt=F32, dma=False, reg=None):
        reg = reg or self.common
        n = 1
        for x in shape[1:]:
            n *= x
        nbytes = n * (4 if dt == F32 else 2)
        off = reg[0]
        reg[0] = (off + nbytes + 63) // 64 * 64
        assert reg[0] <= reg[1], (name, reg)
        t = T(name, self.nc.alloc_sbuf_tensor_at(name, list(shape), dt, offset=off).ap())
        self.tiles[name] = t
        if dma:
            self.mkdma(t)
        return t

    def mkdma(self, t):
        t.dsem = self.nc.alloc_semaphore('d_' + t.name)

    def ps(self, name, shape=(128, 512)):
        t = T(name, self.nc.alloc_psum_tensor(name, list(shape), F32).ap())
        self.tiles[name] = t
        return t

    def _wait(self, e, k, pos):
        if k == e and e not in SES_SET:
            return
        if self.waited[e].get(k, -1) >= pos:
            return
        self.waited[e][k] = pos
        sem = self.tiles[k[1]].dsem if isinstance(k, tuple) else self.sem[k]
        self.eng[e].wait_ge(sem, pos)
        self.nwait += 1

    def _deps(self, e, r, w):
        for t in r:
            if t.w is not None:
                self._wait(e, *t.w)
        for t in w:
            if t.w is not None:
                self._wait(e, *t.w)
            for k, pos in list(t.r.items()):
                self._wait(e, k, pos)

    def sb2(self, name, shape, dt=F32, dma=False, reg=None, reg2=None):
        return Dual(self.sb(name + "_0", shape, dt, dma, reg), self.sb(name + "_1", shape, dt, dma, reg2 or reg))

    def play(self, items):
        for it in items:
            CUR[0] = it[1]
            if it[0] == 'op':
                self.op(*it[2:])
            else:
                self.dma(*it[2:])
        CUR[0] = 0

    def op(self, e, fn, r=(), w=()):
        r, w = _res(r), _res(w)
        if self.rec is not None:
            self.rec.append(('op', CUR[0], e, fn, r, w))
            return None
        self._deps(e, r, w)
        ins = fn(self.eng[e])
        ins.then_inc(self.sem[e], 1)
        self.loc[e] += 1
        pos = self.loc[e]
        for t in r:
            if t not in w:
                t.r[e] = pos
        for t in w:
            t.w = (e, pos)
            t.r = {}
        return ins

    def dma(self, out, in_, tw=(), tr=(), q='sp', slow=False):
        tw = _res(tw)
        tr = _res(tr)
        if self.rec is not None:
            self.rec.append(('dma', CUR[0], out, in_, tw, tr, q, slow))
            return
        self._deps(q, tr, tw)
        if slow:
            ins = self.eng[q].dma_start(out=out, in_=in_, allow_slow_non_contiguous=True)
        else:
            ins = self.eng[q].dma_start(out=out, in_=in_)
        tt = (tw + tr)[0]
        ins.then_inc(tt.dsem, 16)
        tt.dloc += 16
        pos = tt.dloc
        key = ('D', tt.name)
        for t in tw:
            t.w = (key, pos)
            t.r = {}
        for t in tr:
            t.r[key] = pos

    def seg_end(self):
        nc = self.nc
        used = []
        for e in self.eng:
            if self.loc[e]:
                if e != 'sp':
                    self.eng[e].wait_ge(self.sem[e], self.loc[e])
                used.append(self.sem[e])
        for t in self.tiles.values():
            if t.dsem is not None and t.dloc:
                nc.sync.wait_ge(t.dsem, t.dloc)
                used.append(t.dsem)
        if used:
            nc.all_engine_barrier()
            nums = [sm.num for sm in list(self.sem.values()) + [t.dsem for t in self.tiles.values() if t.dsem is not None]]
            nc.sync.sem_clear(range(min(nums), max(nums) + 1))
            nc.all_engine_barrier()
        for e in self.eng:
            self.loc[e] = 0
        for t in self.tiles.values():
            t.w = None
            t.r = {}
            t.dloc = 0
        self.waited = {e: {} for e in self.eng}

    def flush(self):
        self.seg_end()

    def loop(self, n, body):
        self.seg_end()
        with self.nc.Fori(0, n) as i:
            body(i)
            self.seg_end()


def bc(ap, shape):
    return ap.broadcast_to(list(shape))


def build(NB, depth):
    NBT = NB + 2
    NBA = NB + 6
    NI = (NBT + 2) // 2
    NJ = NBT // 2
    assert NBT % 2 == 0
    B = Bld()
    nc = B.nc
    dt_ = nc.dram_tensor

    SB = 128 * D
    LP = 16896
    LW1 = 128 * PW
    xin = dt_("xin", [NBA, 128, D], F32, kind="ExternalInput").ap()
    rowinfo = dt_("rowinfo", [NBA, 128, 4], F32, kind="ExternalInput").ap()
    rmrow = dt_("rmrow", [NBA, 1, 128], F32, kind="ExternalInput").ap()
    consts = dt_("consts", [128, C_W], F32, kind="ExternalInput").ap()
    psmall = dt_("psmall", [depth, LP], F32, kind="ExternalInput").ap()
    pn1 = dt_("pn1", [depth, LP], F32, kind="ExternalInput").ap()
    pn2 = dt_("pn2", [depth, LP], F32, kind="ExternalInput").ap()
    pcw = [dt_("pcw%d" % k, [depth, LP], F32, kind="ExternalInput").ap() for k in range(3)]
    pfc = [dt_("pfc%d" % k, [depth, LP], F32, kind="ExternalInput").ap() for k in range(3)]
    w_in_k = [dt_("w_in_k%d" % k, [depth, LW1], F32, kind="ExternalInput").ap() for k in range(8)]
    ffn_up_k = [[dt_("ffn_up_k%d_%d" % (k, hf), [depth, LW1], F32, kind="ExternalInput").ap() for hf in range(2)]
                for k in range(8)]
    w_out_k = [dt_("w_out_k%d" % k, [depth, SB], F32, kind="ExternalInput").ap() for k in range(8)]
    ffn_down_k = [dt_("ffn_down_k%d" % k, [depth, SB], F32, kind="ExternalInput").ap() for k in range(22)]
    Hh = dt_("hout", [NBA, 128, D], F32, kind="ExternalOutput")
    H = Hh.ap()
    Hf = H.rearrange("n p d -> n (p d)")

    def scratch(name, dt):
        return dt_(name, [NBA, SB], dt).ap()
    S64a, SAK, S64b, S64c = (scratch(n, BF16) for n in ("s64a", "sak", "s64b", "s64c"))
    SAV, S128b, SMV, SO64 = (scratch(n, BF16) for n in ("sav", "s128b", "smv", "so64"))
    SF128, SCMT, SROW, SB64, SMB, RI, RM = (scratch(n, F32) for n in ("sf128", "scmt", "srow", "sb64", "smb", "ri", "rm"))

    def bv(tb, rows, cols):
        return tb[0:rows * cols].rearrange("(p c) -> p c", c=cols)

    def bvb(tb, cols, parts):
        return bc(tb[0:cols].unsqueeze(0), [parts, cols])

    def pairs(T, off, n):
        v = T[off:off + 2 * n]
        if len(T.shape) == 2:
            return v.rearrange("(n u) s -> n u s", u=2)
        return v.rearrange("(n u) p d -> n u p d", u=2)

    BASE = 16512
    W1, W2, WST, GA, CM, TOP = (BASE + x for x in (0, 90112, 135168, 146496, 158784, 196608))
    B.common = [CM, TOP]
    rW1, rW2, rWS = [W1, W2], [W2, WST], [WST, GA]
    SH2 = W1 + 45312
    rS2 = [SH2, SH2 + 8768]
    rP = [SH2 + 8768, GA]
    rF1 = [W1, SH2]
    rF2 = [SH2 + 8768, W2]
    rG = [WST, CM]
    cst = B.sb("cst", [128, C_W], F32, dma=True)
    ident = cst[:, C_ID:C_ID + 128]
    triF = cst[:, C_TF:C_TF + 128]
    triB = cst[:, C_TB:C_TB + 128]
    ones = B.sb("ones", [128, 128])
    sel16 = B.sb("sel16", [128, 16], BF16)
    par = B.sb("par", [128, 1200], F32, dma=True)
    n1w = B.sb("n1w", [128, 8], F32, dma=True)
    n2w = B.sb("n2w", [128, 8], F32, dma=True)
    cw = B.sb("cw", [64, 3, 8], F32, dma=True)
    cwf = B.sb("cwf", [128, 3, 44], F32, dma=True)
    der = B.sb("der", [128, 64])
    DT = B.sb("DT", [128, 4, 128])
    dtmp = B.sb("dtmp", [128, 2, 128])
    wreg = B.sb("wreg", [128, 8 * 2 * DFF], BF16, reg=rW1)
    wdn = B.sb("wdn", [128, 22 * D], BF16, reg=rW2)
    wstage = B.sb("wstage", [128, PW], F32, dma=True, reg=rWS)
    wstage2 = B.sb("wstage2", [128, PW], F32, dma=True, reg=[GA, CM])
    Win = wreg.ap[:, 0:8 * PW].rearrange("p (k c) -> p k c", k=8)
    Wup = wreg.ap.rearrange("p (k c) -> p k c", k=8)
    Wdn = wdn.ap.rearrange("p (k c) -> p k c", k=22)
    Wout = wdn.ap[:, 0:8 * D].rearrange("p (k c) -> p k c", k=8)

    hA = B.sb2("hA", [128, D], F32, dma=True, reg=None, reg2=rS2)
    rinf = B.sb2("rinf", [128, 4], F32, dma=True, reg=None, reg2=rS2)
    rinfB = B.sb2("rinfB", [128, 4], F32, dma=True, reg=None, reg2=rS2)
    xn = B.sb2("xn", [128, D], F32, reg=None, reg2=rS2)
    junk = xn
    st8 = B.sb2("st8", [128, 8], reg=None, reg2=rS2)
    win = B.sb("win", [128, 8, 513], BF16)
    sm8 = B.sb2("sm8", [128, 64])
    Rf = B.sb("Rf", [64, 256])
    Sf = B.sb("Sf", [64, 260])
    mF = B.sb("mF", [128, 4])
    Rb = B.sb("Rb", [64, 256])
    Sb = B.sb("Sb", [64, 260])
    mB = B.sb("mB", [128, 4])
    metaK = B.sb("metaK", [64, 32])
    metaV = B.sb("metaV", [16, 130])
    PP = [B.ps("pp%d" % i) for i in range(5)]
    X = [B.ps("x%d" % i) for i in range(3)]
    Kp = [Dual(PP[j], (PP[4], X[0], X[1], X[2])[j]) for j in range(4)]
    p64 = B.sb2("p64", [64, P64W], BF16, dma=True, reg=rP)
    p128 = B.sb2("p128", [128, P128W], BF16, dma=True, reg=rP)
    f128 = B.sb2("f128", [128, F128W], F32, dma=True, reg=rP)

    def V64(n):
        a, b = P64[n]
        return p64.ap[:, a:b]

    def V128(n):
        a, b = P128[n]
        return p128.ap[:, a:b]

    def VF(n):
        a, b = F128[n]
        return f128.ap[:, a:b]
    sq = B.sb2("sq", [128, 640], reg=rP)
    hs = B.sb2("hs", [128, 16], reg=rP)
    qn = B.sb2("qn", [128, 640], reg=rP)
    vun = B.sb2("vun", [128, 2, 65], BF16, reg=rP)
    rqk = B.sb2("rqk", [128, 512], reg=rP)
    rkw = B.sb2("rkw", [128, 2, 256], BF16, reg=rP)
    mtmp = B.sb("mtmp", [64, 130], reg=rP)
    mvb = B.sb2("mvb", [16, 130], BF16, dma=True, reg=rP)
    rmr = B.sb2("rmr", [64, 128], F32, dma=True, reg=rP)
    cacc = B.sb2("cacc", [64, 8, 128], reg=rP)
    mkf = B.sb2("mkf", [64, 4, 128], reg=rP)
    mktok = B.sb2("mktok", [128, 256], reg=rP)
    mke = B.sb2("mke", [128, 2, 256], BF16, reg=rP)
    G = B.sb2("G", [128, 16], reg=rP)
    gl = B.sb2("gl", [128, 8], reg=rP)
    aT = B.sb2("aT", [4, 256], reg=rP)
    cmT = B.sb2("cmT", [4, 256], F32, dma=True, reg=rP)
    dg = B.sb2("dg", [4, 8], reg=rP)
    tot = B.sb2("tot", [128, 16], reg=rP)
    eend = B.sb2("eend", [128, 8], reg=rP)
    sc4 = B.sb("sc4", [128, 16], reg=rP)
    b64 = B.sb2("b64", [64, 516], F32, dma=True, reg=rP)
    srow = B.sb2("srow", [1, 16], F32, dma=True, reg=rP)
    stmp = B.sb("stmp", [64, 260], reg=rP)
    vmn = B.sb2("vmn", [16, 130], reg=rP)
    kvf = B.sb2("kvf", [64, 256], reg=rP)
    kvnf = B.sb2("kvnf", [64, 260], reg=rP)
    scb = B.sb("scb", [128, 16], F32, dma=True, reg=rP)
    b64B = B.sb("b64B", [64, 2, 516], F32, dma=True, reg=rP)
    scbB = B.sb("scbB", [128, 2, 12], F32, dma=True, reg=rP)
    rinfB2 = B.sb("rinfB2", [128, 2, 4], F32, dma=True, reg=rP)
    o64B = B.sb("o64B", [64, 2, 516], BF16, dma=True, reg=rP)
    mbrowB = B.sb("mbrowB", [1, 2, 4], F32, dma=True, reg=rP)
    o64 = B.sb("o64", [64, 516], BF16, dma=True, reg=rP)
    mbrow = B.sb("mbrow", [1, 4], F32, dma=True, reg=rP)
    kn = B.sb("kn", [64, 4, 256], BF16, dma=True, reg=rF1)
    vn = B.sb("vn", [128, 4, 130], BF16, dma=True, reg=rF1)
    q64 = B.sb2("q64", [64, P64W], BF16, dma=True, reg=rF1, reg2=rF2)
    q128 = B.sb2("q128", [128, P128W], BF16, dma=True, reg=rF1, reg2=rF2)
    g128 = B.sb2("g128", [128, F128W], F32, dma=True, reg=rF1, reg2=rF2)
    mv16 = B.sb2("mv16", [16, 130], BF16, dma=True, reg=rF1, reg2=rF2)
    cmb = B.sb2("cmb", [128, 1024], F32, dma=True, reg=rF1, reg2=rF2)
    mpv = B.sb2("mpv", [128, 8], F32, dma=True, reg=rF1, reg2=rF2)
    r64 = B.sb2("r64", [64, 516], BF16, dma=True, reg=rF1, reg2=rF2)
    mixed = B.sb2("mixed", [128, D], reg=rF1, reg2=rF2)
    stmpF = B.sb2("stmpF", [128, 512], reg=rF1, reg2=rF2)
    ptb = [B.sb2("ptb%d" % i, [128, 512], BF16, reg=rF1, reg2=rF2) for i in range(3)]
    pmb = B.sb2("pmb", [16, 512], BF16, reg=rF1, reg2=rF2)
    wtr = B.sb2("wtr", [128, 512], BF16, reg=rF1, reg2=rF2)
    wtm = B.sb2("wtm", [128, 1024], BF16, reg=rF1, reg2=rF2)
    oacc = B.sb2("oacc", [128, 520], reg=rF1, reg2=rF2)
    obuf = B.sb2("obuf", [128, 520], reg=rF1, reg2=rF2)
    mixT = B.sb2("mixT", [128, 8, 128], BF16, reg=rF1, reg2=rF1)
    K = [Dual(PP[j], (PP[4], X[0], X[1], X[2])[j]) for j in range(4)]
    rF3 = [W2 + 16384, WST]
    Eb = B.sb("Eb", [128, 6, 512], BF16, reg=rF3)
    actT = B.sb("actT", [128, 22, 256], BF16, reg=rG)
    cg = B.sb("cg", [128, 2, 256], reg=rG)
    sg = B.sb("sg", [128, 256], reg=rG)

    def E(e, fn, r=(), w=()):
        return B.op(e, fn, r, w)

    B.dma(cst.ap, consts, tw=[cst])
    E('dve', lambda e: e.memset(ones.ap, 1.0), w=[ones])
    E('dve', lambda e: e.tensor_copy(out=sel16.ap, in_=cst[:, C_SEL:C_SEL + 16]), r=[cst], w=[sel16])
    for c0 in range(0, NBA, 8):
        c1 = min(NBA, c0 + 8)
        B.dma(H[c0:c1], xin[c0:c1], tr=[cst])
    for c0 in range(NBA):
        B.dma(bv(RI[c0], 128, 4), rowinfo[c0], tr=[cst])
        B.dma(bv(RM[c0], 1, 128), rmrow[c0], tr=[cst])
        if c0 % 16 == 15:
            B.seg_end()
    B.flush()

    def load_weight(dst3, chunks, l, ncols, scale_t=None):
        wt = wdn if (dst3 is Wdn or dst3 is Wout) else wreg
        it = 0
        for k, pieces in enumerate(chunks):
            for (tk, c0, c1) in pieces:
                w_ = c1 - c0
                ws = wstage if it % 2 == 0 else wstage2
                B.dma(ws.ap[:, 0:w_], tk[l][0:128 * w_].rearrange("(p c) -> p c", c=w_), tw=[ws])
                eng = 'dve' if it % 2 == 0 else 'pool'
                it += 1
                if scale_t is not None:
                    E(eng, lambda e, k=k, c0=c0, c1=c1, w_=w_, ws=ws: e.tensor_scalar(
                        out=dst3[:, k, c0:c1], in0=ws.ap[:, 0:w_], scalar1=scale_t[:, k:k + 1], scalar2=None,
                        op0=ALU.mult), r=[ws, scale_t], w=[wt])
                else:
                    E(eng, lambda e, k=k, c0=c0, c1=c1, w_=w_, ws=ws: e.tensor_copy(out=dst3[:, k, c0:c1], in_=ws.ap[:, 0:w_]),
                      r=[ws], w=[wt])

    def stageA(hsrc, rsrc, wc):
        B.dma(hA.ap, hsrc, tw=[hA])
        B.dma(rinf.ap, bv(rsrc, 128, 4), tw=[rinf])
        E('act', lambda e: e.activation(out=junk.ap, in_=hA.ap, func=AF.Square, accum_out=st8[:, 0:1]),
          r=[hA], w=[junk, st8])
        E('dve', lambda e: e.tensor_scalar(out=st8[:, 1:2], in0=st8[:, 0:1], scalar1=1.0 / D, scalar2=EPS,
                                           op0=ALU.mult, op1=ALU.add), r=[st8], w=[st8])
        E('act', lambda e: e.activation(out=st8[:, 3:4], in_=st8[:, 1:2], func=AF.Sqrt), r=[st8], w=[st8])
        E('dve', lambda e: e.reciprocal(out=st8[:, 4:5], in_=st8[:, 3:4]), r=[st8], w=[st8])
        E('dve', lambda e: e.tensor_tensor(out=st8[:, 2:3], in0=st8[:, 4:5], in1=rinf[:, 0:1], op=ALU.mult), r=[st8, rinf], w=[st8])
        E('dve', lambda e: e.tensor_scalar(out=xn.ap, in0=hA.ap, scalar1=st8[:, 2:3], scalar2=None, op0=ALU.mult),
          r=[hA, st8], w=[xn])
        for half in range(2):
            for kk in range(4):
                k = half * 4 + kk
                E('pe', lambda e, k=k, kk=kk, half=half: e.transpose(Kp[2 + half][:, kk * 128:(kk + 1) * 128],
                                                                      xn[:, k * 128:(k + 1) * 128], ident),
                  r=[xn, cst], w=[Kp[2 + half]])
            E('act', lambda e, half=half: e.activation(
                out=win[:, half * 4:half * 4 + 4, wc:wc + 128],
                in_=Kp[2 + half].ap.rearrange("p (k c) -> p k c", k=4), func=AF.Copy),
              r=[Kp[2 + half]], w=[win])

    def shift_win(n=128):
        E('pool', lambda e: e.tensor_copy(out=win[:, :, 0:1], in_=win[:, :, n:n + 1]), r=[win], w=[win])
        E('pool', lambda e: e.tensor_copy(out=win[:, :, 1:n + 1], in_=win[:, :, n + 1:2 * n + 1]), r=[win], w=[win])

    def headnorm(src_ap, dst_ap, w_ap, gate_ap, rl, wl):
        E('dve', lambda e: e.tensor_reduce(out=sm8[:, 0:4], in_=src_ap, axis=AX.X, op=ALU.add), r=rl, w=[sm8])
        E('dve', lambda e: e.tensor_scalar(out=sm8[:, 4:8], in0=sm8[:, 0:4], scalar1=-1.0 / 64, scalar2=None,
                                           op0=ALU.mult), r=[sm8], w=[sm8])
        E('dve', lambda e: e.tensor_tensor(out=src_ap, in0=src_ap, in1=bc(sm8[:, 4:8].unsqueeze(2), [128, 4, 64]),
                                           op=ALU.add), r=rl + [sm8], w=rl)
        j3 = junk[:, 0:256].rearrange("p (h d) -> p h d", h=4)
        E('act', lambda e: e.activation(out=j3, in_=src_ap, func=AF.Square), r=rl, w=[junk])
        E('dve', lambda e: e.tensor_reduce(out=sm8[:, 8:12], in_=j3, axis=AX.X, op=ALU.add), r=[junk], w=[sm8])
        E('dve', lambda e: e.tensor_scalar(out=sm8[:, 12:16], in0=sm8[:, 8:12], scalar1=1.0 / 64, scalar2=EPS,
                                           op0=ALU.mult, op1=ALU.add), r=[sm8], w=[sm8])
        E('act', lambda e: e.activation(out=sm8[:, 12:16], in_=sm8[:, 12:16], func=AF.Sqrt), r=[sm8], w=[sm8])
        E('dve', lambda e: e.reciprocal(out=sm8[:, 16:20], in_=sm8[:, 12:16]), r=[sm8], w=[sm8])
        E('dve', lambda e: e.tensor_tensor(out=src_ap, in0=src_ap, in1=bc(sm8[:, 16:20].unsqueeze(2), [128, 4, 64]),
                                           op=ALU.mult), r=rl + [sm8], w=rl)
        E('dve', lambda e: e.tensor_tensor(out=src_ap, in0=src_ap, in1=w_ap, op=ALU.mult), r=rl + [par], w=rl)
        E('dve', lambda e: e.tensor_tensor(out=dst_ap, in0=src_ap, in1=gate_ap, op=ALU.mult), r=rl + [g128], w=wl)

    def layer(l):
        B.dma(par[:, 0:672], bc(psmall[l][0:672].unsqueeze(0), [128, 672]), tw=[par])
        B.dma(n1w.ap, pn1[l][0:D].rearrange("(k p) -> p k", p=128), tw=[n1w], slow=True)
        B.dma(n2w.ap, pn2[l][0:D].rearrange("(k p) -> p k", p=128), tw=[n2w], slow=True)
        for k in range(3):
            B.dma(cw[:, k, :], pcw[k][l][0:512].rearrange("(c p) -> p c", p=64), tw=[cw], slow=True)
            B.dma(cwf[:, k, :], pfc[k][l][0:2 * DFF].rearrange("(c p) -> p c", p=128), tw=[cwf], slow=True)
        E('dve', lambda e: e.tensor_scalar(out=par[:, 0:64], in0=par[:, 0:64], scalar1=0.125, scalar2=None,
                                           op0=ALU.mult), r=[par], w=[par])
        E('act', lambda e: e.activation(out=der[:, 32:40], in_=par[:, 128:136], func=AF.Exp), r=[par], w=[der])
        E('act', lambda e: e.activation(out=der[:, 0:8], in_=par[:, 136:144], func=AF.Exp, scale=-1.0), r=[par], w=[der])
        E('act', lambda e: e.activation(out=der[:, 0:8], in_=der[:, 0:8], func=AF.Ln, bias=1.0), r=[der], w=[der])
        E('dve', lambda e: e.tensor_scalar(out=der[:, 0:8], in0=der[:, 0:8], scalar1=-1.0, scalar2=None, op0=ALU.mult),
          r=[der], w=[der])
        E('act', lambda e: e.activation(out=der[:, 8:16], in_=der[:, 0:8], func=AF.Exp, scale=128.0), r=[der], w=[der])
        for h in range(4):
            for (dst, ridx, lgc) in ((16 + h, 0, h), (20 + h, 1, 4 + h), (24 + h, 2, h), (28 + h, 3, 4 + h)):
                E('act', lambda e, dst=dst, ridx=ridx, lgc=lgc: e.activation(
                    out=der[:, dst:dst + 1], in_=cst[:, C_RIDX + ridx:C_RIDX + ridx + 1], func=AF.Exp,
                    scale=der[:, lgc:lgc + 1]), r=[der, cst], w=[der])
            E('act', lambda e, h=h: e.activation(out=dtmp[:, 0, :], in_=cst[:, C_DF:C_DF + 128], func=AF.Exp,
                                                 scale=der[:, h:h + 1]), r=[der, cst], w=[dtmp])
            E('act', lambda e, h=h: e.activation(out=dtmp[:, 1, :], in_=cst[:, C_DB:C_DB + 128], func=AF.Exp,
                                                 scale=der[:, 4 + h:5 + h]), r=[der, cst], w=[dtmp])
            E('dve', lambda e: e.tensor_tensor(out=dtmp[:, 0, :], in0=dtmp[:, 0, :], in1=triF, op=ALU.mult),
              r=[dtmp, cst], w=[dtmp])
            E('dve', lambda e: e.tensor_tensor(out=dtmp[:, 1, :], in0=dtmp[:, 1, :], in1=triB, op=ALU.mult),
              r=[dtmp, cst], w=[dtmp])
            E('dve', lambda e, h=h: e.tensor_tensor(out=DT[:, h, :], in0=dtmp[:, 0, :], in1=dtmp[:, 1, :], op=ALU.add),
              r=[dtmp], w=[DT])
        load_weight(Win, [[(w_in_k[k], 0, PW)] for k in range(8)], l, PW, n1w)
        for t_ in (Rf, Sf, mF, Rb, Sb, mB, metaK, metaV):
            E('dve', lambda e, t_=t_: e.memset(t_.ap, 0.0), w=[t_])
        E('pool', lambda e: e.memset(win.ap, 0.0), w=[win])
        for c_ in range(2):
            CUR[0] = c_
            E('dve', lambda e: e.memset(vun.ap, 1.0), w=[vun])
            E('dve', lambda e: e.memset(p128.ap, 1.0), w=[p128])
        CUR[0] = 0
        stageA(H[0], RI[0], 257)
        shift_win(256)

        def stepP_main(ix, u):
            wo = 128 * u
            stageA(ix(H, 1), ix(RI, 1), 129 + wo)
            B.dma(rmr.ap, bvb(ix(RM, 0), 128, 64), tw=[rmr])
            B.dma(rinfB.ap, bv(ix(RI, 0), 128, 4), tw=[rinfB])
            rB = rinfB
            def proj(groups):
                for (bk, c0, c1, o0) in groups:
                    for k in range(8):
                        E('pe', lambda e, bk=bk, c0=c0, c1=c1, o0=o0, k=k: e.matmul(
                            Kp[bk][:, o0:o0 + c1 - c0], win[:, k, 1 + wo:129 + wo], Win[:, k, c0:c1], start=(k == 0), stop=(k == 7)),
                          r=[win, wreg], w=[Kp[bk]])
            proj([(0, 0, 512, 0), (1, 512, 1024, 0), (2, 1024, 1536, 0), (3, 1536, 1792, 0), (3, 2816, 2832, 256)])
            E('act', lambda e: e.activation(out=sq[:, 0:512], in_=Kp[0].ap, func=AF.Square), r=[Kp[0]], w=[sq])
            E('act', lambda e: e.activation(out=sq[:, 512:640], in_=Kp[1][:, 0:128], func=AF.Square), r=[Kp[1]], w=[sq])
            E('dve', lambda e: e.tensor_reduce(out=hs[:, 0:10], in_=sq.ap.rearrange("p (h d) -> p h d", d=64),
                                               axis=AX.X, op=ALU.add), r=[sq], w=[hs])
            E('dve', lambda e: e.tensor_scalar(out=hs[:, 0:10], in0=hs[:, 0:10], scalar1=1.0 / 64, scalar2=EPS,
                                               op0=ALU.mult, op1=ALU.add), r=[hs], w=[hs])
            E('act', lambda e: e.activation(out=hs[:, 0:10], in_=hs[:, 0:10], func=AF.Sqrt), r=[hs], w=[hs])
            E('dve', lambda e: e.reciprocal(out=hs[:, 0:10], in_=hs[:, 0:10]), r=[hs], w=[hs])
            E('dve', lambda e: e.tensor_tensor(out=qn[:, 0:512].rearrange("p (h d) -> p h d", d=64),
                                               in0=Kp[0].ap.rearrange("p (h d) -> p h d", d=64),
                                               in1=bc(hs[:, 0:8].unsqueeze(2), [128, 8, 64]), op=ALU.mult),
              r=[Kp[0], hs], w=[qn])
            E('dve', lambda e: e.tensor_tensor(out=qn[:, 512:640].rearrange("p (h d) -> p h d", d=64),
                                               in0=Kp[1][:, 0:128].rearrange("p (h d) -> p h d", d=64),
                                               in1=bc(hs[:, 8:10].unsqueeze(2), [128, 2, 64]), op=ALU.mult),
              r=[Kp[1], hs], w=[qn])
            E('pool', lambda e: e.tensor_tensor(out=qn[:, 0:512].rearrange("p (h d) -> p h d", d=64),
                                                in0=qn[:, 0:512].rearrange("p (h d) -> p h d", d=64),
                                                in1=bc(par[:, 0:64].unsqueeze(1), [128, 8, 64]), op=ALU.mult),
              r=[qn, par], w=[qn])
            E('pool', lambda e: e.tensor_tensor(out=qn[:, 512:640].rearrange("p (h d) -> p h d", d=64),
                                                in0=qn[:, 512:640].rearrange("p (h d) -> p h d", d=64),
                                                in1=bc(par[:, 64:128].unsqueeze(1), [128, 2, 64]), op=ALU.mult),
              r=[qn, par], w=[qn])
            E('act', lambda e: e.activation(out=vun[:, :, 0:64], in_=Kp[1][:, 128:256].rearrange("p (h d) -> p h d", d=64),
                                            func=AF.Copy), r=[Kp[1]], w=[vun])
            E('dve', lambda e: e.tensor_scalar(out=V128('AV'), in0=vun.ap.rearrange("p h d -> p (h d)"),
                                               scalar1=rB[:, 1:2], scalar2=None, op0=ALU.mult), r=[vun, rB], w=[p128])
            E('act', lambda e: e.activation(out=rqk[:, 0:256], in_=Kp[1][:, 256:512], func=AF.Copy), r=[Kp[1]], w=[rqk])
            E('act', lambda e: e.activation(out=rqk[:, 256:512], in_=Kp[2][:, 0:256], func=AF.Copy, scale=0.125),
              r=[Kp[2]], w=[rqk])
            E('act', lambda e: e.activation(out=V128('RV'), in_=Kp[2][:, 256:512], func=AF.Copy), r=[Kp[2]], w=[p128])
            E('act', lambda e: e.activation(out=VF('RG'), in_=Kp[3][:, 0:256], func=AF.Silu), r=[Kp[3]], w=[f128])
            E('dve', lambda e: e.tensor_scalar(out=f128[:, 536:537], in0=rB[:, 2:3], scalar1=-30000.0, scalar2=None, op0=ALU.mult),
              r=[rB], w=[f128])
            E('dve', lambda e: e.tensor_tensor(out=G.ap, in0=Kp[3][:, 256:272], in1=par[:, 656:672], op=ALU.add),
              r=[Kp[3], par], w=[G])
            for d_ in range(2):
                E('dve', lambda e, d_=d_: e.tensor_tensor(
                    out=rkw[:, d_, :].rearrange("p (h d) -> p h d", d=64),
                    in0=rqk[:, 256:512].rearrange("p (h d) -> p h d", d=64),
                    in1=bc(der[:, 24 + 4 * d_:28 + 4 * d_].unsqueeze(2), [128, 4, 64]), op=ALU.mult),
                  r=[rqk, der], w=[rkw])
            proj([(0, 2304, 2816, 0)])
            E('act', lambda e: e.activation(out=VF('MO'), in_=Kp[0][:, 256:512], func=AF.Sigmoid), r=[Kp[0]], w=[f128])
            E('act', lambda e: e.activation(out=V128('MVA').rearrange("p (h d) -> p h d", d=65)[:, :, 0:64],
                                            in_=Kp[0][:, 0:256].rearrange("p (h d) -> p h d", d=64), func=AF.Copy),
              r=[Kp[0]], w=[p128])
            for hc in range(8):
                xb, xo = Kp[1 + hc // 3], (hc % 3) * 130
                for k in range(8):
                    E('pe', lambda e, hc=hc, xb=xb, xo=xo, k=k: e.matmul(
                        xb[0:64, xo:xo + 130], Win[:, k, 1792 + hc * 64:1792 + (hc + 1) * 64], win[:, k, wo:wo + 130],
                        start=(k == 0), stop=(k == 7)), r=[win, wreg], w=[xb])
            for hc in range(8):
                xb, xo = Kp[1 + hc // 3], (hc % 3) * 130
                E('act', lambda e, hc=hc, xb=xb, xo=xo: e.activation(out=cacc[:, hc, :], in_=xb[0:64, xo:xo + 128],
                                                                      func=AF.Copy, scale=cw[:, 0, hc:hc + 1]),
                  r=[xb, cw], w=[cacc])
                for tap in (1, 2):
                    E('dve', lambda e, hc=hc, xb=xb, xo=xo, tap=tap: e.scalar_tensor_tensor(
                        out=cacc[:, hc, :], in0=xb[0:64, xo + tap:xo + tap + 128], scalar=cw[:, tap, hc:hc + 1],
                        in1=cacc[:, hc, :], op0=ALU.mult, op1=ALU.add), r=[xb, cw, cacc], w=[cacc])
            E('act', lambda e: e.activation(out=V64('MQT').rearrange("p (h t) -> p h t", h=4), in_=cacc[:, 0:4, :],
                                            func=AF.Silu), r=[cacc], w=[p64])
            E('act', lambda e: e.activation(out=mkf.ap, in_=cacc[:, 4:8, :], func=AF.Silu), r=[cacc], w=[mkf])
            E('dve', lambda e: e.scalar_tensor_tensor(out=mkf.ap, in0=mkf.ap, scalar=0.125,
                                                      in1=bc(rmr.ap.unsqueeze(1), [64, 4, 128]),
                                                      op0=ALU.mult, op1=ALU.mult), r=[mkf, rmr], w=[mkf])
            E('pool', lambda e: e.tensor_copy(out=V64('MKT').rearrange("p (h t) -> p h t", h=4), in_=mkf.ap),
              r=[mkf], w=[p64])
            for h in range(8):
                xb = Kp[h // 4]
                E('pe', lambda e, h=h, xb=xb: e.transpose(xb[0:64, (h % 4) * 128:(h % 4 + 1) * 128],
                                                          qn[:, h * 64:(h + 1) * 64], ident), r=[qn, cst], w=[xb])
            E('act', lambda e: e.activation(out=V64('AQT')[:, 0:512], in_=Kp[0][0:64, :], func=AF.Copy), r=[Kp[0]], w=[p64])
            E('dve', lambda e: e.tensor_copy(out=V64('AQT')[:, 512:1024], in_=Kp[1][0:64, :]), r=[Kp[1]], w=[p64])
            for h in range(2):
                E('pe', lambda e, h=h: e.transpose(Kp[2][0:64, h * 128:(h + 1) * 128], qn[:, 512 + h * 64:512 + (h + 1) * 64],
                                                   ident), r=[qn, cst], w=[Kp[2]])
            E('act', lambda e: e.activation(out=V64('AKT'), in_=Kp[2][0:64, 0:256], func=AF.Copy), r=[Kp[2]], w=[p64])
            for h in range(4):
                E('pe', lambda e, h=h: e.transpose(Kp[0][0:64, h * 128:(h + 1) * 128], rqk[:, h * 64:(h + 1) * 64], ident),
                  r=[rqk, cst], w=[Kp[0]])
            E('act', lambda e: e.activation(out=V64('RQT'), in_=Kp[0][0:64, :], func=AF.Copy), r=[Kp[0]], w=[p64])
            for h in range(4):
                E('pe', lambda e, h=h: e.transpose(Kp[1][0:64, h * 128:(h + 1) * 128],
                                                   rqk[:, 256 + h * 64:256 + (h + 1) * 64], ident),
                  r=[rqk, cst], w=[Kp[1]])
            E('dve', lambda e: e.tensor_copy(out=V64('RKT'), in_=Kp[1][0:64, :]), r=[Kp[1]], w=[p64])
            for h in range(4):
                E('pe', lambda e, h=h: e.transpose(Kp[2][:, 256 + h * 64:256 + (h + 1) * 64], mkf[:, h, :], ident[0:64, 0:64]),
                  r=[mkf, cst], w=[Kp[2]])
            E('act', lambda e: e.activation(out=mktok.ap, in_=Kp[2][:, 256:512], func=AF.Copy), r=[Kp[2]], w=[mktok])
            E('pe', lambda e: e.matmul(Kp[3][0:16, 0:130], sel16.ap, vun.ap.rearrange("p h d -> p (h d)"), start=True, stop=True),
              r=[sel16, vun], w=[Kp[3]])
            E('act', lambda e: e.activation(out=vmn.ap, in_=Kp[3][0:16, 0:130], func=AF.Copy), r=[Kp[3]], w=[vmn])
            for d_ in range(2):
                for h in range(4):
                    E('pe', lambda e, d_=d_, h=h: e.matmul(
                        Kp[0][0:64, d_ * 256 + h * 64:d_ * 256 + (h + 1) * 64], rkw[:, d_, h * 64:(h + 1) * 64],
                        V128('RV')[:, h * 64:(h + 1) * 64], start=True, stop=True), r=[rkw, p128], w=[Kp[0]])
            E('act', lambda e: e.activation(out=kvf.ap, in_=Kp[0][0:64, 0:256], func=AF.Copy), r=[Kp[0]], w=[kvf])
            E('act', lambda e: e.activation(out=b64[:, 0:256], in_=Kp[0][0:64, 256:512], func=AF.Copy), r=[Kp[0]], w=[b64])
            G4 = G.ap.rearrange("p (d k h) -> p d k h", d=2, k=2)
            gl3 = gl.ap.rearrange("p (d h) -> p d h", d=2)
            E('act', lambda e: e.activation(out=gl3, in_=G4[:, :, 1, :], func=AF.Exp, scale=-1.0), r=[G], w=[gl])
            E('act', lambda e: e.activation(out=gl.ap, in_=gl.ap, func=AF.Ln, bias=1.0), r=[gl], w=[gl])
            E('pe', lambda e: e.matmul(Kp[1][:, 0:4], triF, gl[:, 0:4], start=True, stop=True), r=[cst, gl], w=[Kp[1]])
            E('pe', lambda e: e.matmul(Kp[1][:, 4:8], triB, gl[:, 4:8], start=True, stop=True), r=[cst, gl], w=[Kp[1]])
            E('pe', lambda e: e.matmul(Kp[1][:, 8:16], ones.ap, gl.ap, start=True, stop=True), r=[ones, gl], w=[Kp[1]])
            GT = VF('GT')
            E('dve', lambda e: e.tensor_scalar(out=GT[:, 8:16], in0=Kp[1][:, 0:8], scalar1=-1.0, scalar2=None, op0=ALU.mult),
              r=[Kp[1]], w=[f128])
            E('dve', lambda e: e.tensor_tensor(out=GT[:, 0:8].rearrange("p (d h) -> p d h", d=2), in0=Kp[1][:, 0:8].rearrange("p (d h) -> p d h", d=2),
                                               in1=G4[:, :, 0, :], op=ALU.add), r=[Kp[1], G], w=[f128])
            E('dve', lambda e: e.tensor_scalar(out=tot[:, 8:16], in0=Kp[1][:, 8:16], scalar1=-1.0, scalar2=None, op0=ALU.mult),
              r=[Kp[1]], w=[tot])
            E('pe', lambda e: e.matmul(Kp[2][0:4, 0:128], GT[:, 0:4], ident, start=True, stop=True), r=[f128, cst], w=[Kp[2]])
            E('pe', lambda e: e.matmul(Kp[2][0:4, 128:256], GT[:, 4:8], ident, start=True, stop=True), r=[f128, cst], w=[Kp[2]])
            E('dve', lambda e: e.tensor_copy(out=aT.ap, in_=Kp[2][0:4, 0:256]), r=[Kp[2]], w=[aT])
            E('dve', lambda e: e.tensor_tensor_scan(out=cmT[:, 0:128], data0=ones[0:4, :], data1=aT[:, 0:128],
                                                    initial=-3.0e38, op0=ALU.mult, op1=ALU.max), r=[ones, aT], w=[cmT])
            pst = cmT.ap.ap[0][0]
            rev_o = bass.AP(cmT.ap.tensor, cmT.ap.offset + 255, [[pst, 4], [-1, 128]])
            pst2 = aT.ap.ap[0][0]
            rev_i = bass.AP(aT.ap.tensor, aT.ap.offset + 255, [[pst2, 4], [-1, 128]])
            E('dve', lambda e: e.tensor_tensor_scan(out=rev_o, data0=ones[0:4, :], data1=rev_i,
                                                    initial=-3.0e38, op0=ALU.mult, op1=ALU.max), r=[ones, aT], w=[cmT])
            for d_ in range(2):
                col = 127 if d_ == 0 else 128
                E('dve', lambda e, d_=d_, col=col: e.tensor_scalar(out=dg[:, d_ * 4:d_ * 4 + 4], in0=ident[0:4, 0:4],
                                                                    scalar1=cmT[:, col:col + 1], scalar2=None, op0=ALU.mult),
                  r=[cst, cmT], w=[dg])
            E('pe', lambda e: e.matmul(Kp[1][:, 16:24], ones[0:4, :], dg.ap, start=True, stop=True), r=[ones, dg], w=[Kp[1]])
            E('dve', lambda e: e.tensor_copy(out=tot[:, 0:8], in_=Kp[1][:, 16:24]), r=[Kp[1]], w=[tot])
            E('pe', lambda e: e.matmul(Kp[1][:, 24:28], cmT[:, 0:128], ident[0:4, 0:4], start=True, stop=True), r=[cmT, cst], w=[Kp[1]])
            E('pe', lambda e: e.matmul(Kp[1][:, 28:32], cmT[:, 128:256], ident[0:4, 0:4], start=True, stop=True), r=[cmT, cst], w=[Kp[1]])
            E('dve', lambda e: e.tensor_copy(out=GT[:, 16:24], in_=Kp[1][:, 24:32]), r=[Kp[1]], w=[f128])
            E('dve', lambda e: e.tensor_tensor(out=eend.ap, in0=GT[:, 0:8], in1=tot[:, 0:8], op=ALU.subtract), r=[f128, tot], w=[eend])
            E('act', lambda e: e.activation(out=eend.ap, in_=eend.ap, func=AF.Exp), r=[eend], w=[eend])
            for d_ in range(2):
                E('dve', lambda e, d_=d_: e.tensor_tensor(
                    out=mke[:, d_, :].rearrange("p (h d) -> p h d", d=64), in0=mktok.ap.rearrange("p (h d) -> p h d", d=64),
                    in1=bc(eend[:, d_ * 4:d_ * 4 + 4].unsqueeze(2), [128, 4, 64]), op=ALU.mult), r=[mktok, eend], w=[mke])
            for d_ in range(2):
                for h in range(4):
                    dst = Kp[1][0:64, 64 + h * 65:64 + (h + 1) * 65] if d_ == 0 else Kp[2][0:64, h * 65:(h + 1) * 65]
                    E('pe', lambda e, d_=d_, h=h, dst=dst: e.matmul(dst, mke[:, d_, h * 64:(h + 1) * 64],
                                                                     V128('MVA')[:, h * 65:(h + 1) * 65], start=True, stop=True),
                      r=[mke, p128], w=[Kp[1] if d_ == 0 else Kp[2]])
            E('act', lambda e: e.activation(out=b64[:, 256:516], in_=Kp[2][0:64, 0:260], func=AF.Copy), r=[Kp[2]], w=[b64])
            E('act', lambda e: e.activation(out=kvnf.ap, in_=Kp[1][0:64, 64:324], func=AF.Copy), r=[Kp[1]], w=[kvnf])
            B.dma(bv(ix(SAV, 0), 128, 130), p128[:, 0:130], tr=[p128])
            B.dma(bv(ix(S128b, 0), 128, 516), p128[:, 130:646], tr=[p128])
            B.dma(bv(ix(SF128, 0), 128, F128W), f128.ap, tr=[f128])
            B.dma(bv(ix(SCMT, 0), 4, 256), cmT.ap, tr=[cmT])
            B.dma(bv(ix(SB64, 0), 64, 516), b64.ap, tr=[b64])

        def stepP_state(ix, u):
            rB = rinfB
            E('dve', lambda e: e.tensor_tensor(out=mtmp[0:16, :], in0=vmn.ap, in1=metaV.ap, op=ALU.subtract),
              r=[vmn, metaV], w=[mtmp])
            E('dve', lambda e: e.scalar_tensor_tensor(out=metaV.ap, in0=mtmp[0:16, :], scalar=rB[0:16, 2:3], in1=metaV.ap,
                                                      op0=ALU.mult, op1=ALU.add), r=[mtmp, rB, metaV], w=[metaV])
            E('dve', lambda e: e.tensor_copy(out=mvb.ap, in_=metaV.ap), r=[metaV], w=[mvb])
            akm = V64('AKT').rearrange("p (h t) -> p h t", h=2)[:, :, 112:128]
            mk3 = metaK.ap.rearrange("p (h t) -> p h t", h=2)
            mt3 = mtmp[:, 0:32].rearrange("p (h t) -> p h t", h=2)
            E('dve', lambda e: e.tensor_tensor(out=mt3, in0=akm, in1=mk3, op=ALU.subtract), r=[p64, metaK], w=[mtmp])
            E('dve', lambda e: e.scalar_tensor_tensor(out=metaK.ap, in0=mtmp[:, 0:32], scalar=rB[0:64, 2:3], in1=metaK.ap,
                                                      op0=ALU.mult, op1=ALU.add), r=[mtmp, rB, metaK], w=[metaK])
            E('dve', lambda e: e.tensor_copy(out=V64('MKM'), in_=metaK.ap), r=[metaK], w=[p64])
            E('dve', lambda e: e.tensor_scalar(out=Rf.ap, in0=Rf.ap, scalar1=rB[0:64, 3:4], scalar2=None, op0=ALU.mult),
              r=[Rf, rB], w=[Rf])
            E('act', lambda e: e.activation(out=V64('RF'), in_=Rf.ap, func=AF.Copy), r=[Rf], w=[p64])
            E('dve', lambda e: e.tensor_tensor(out=Rf.ap.rearrange("p (h d) -> p h d", d=64),
                                               in0=Rf.ap.rearrange("p (h d) -> p h d", d=64),
                                               in1=bc(der[0:64, 8:12].unsqueeze(2), [64, 4, 64]), op=ALU.mult),
              r=[Rf, der], w=[Rf])
            E('dve', lambda e: e.tensor_tensor(out=Rf.ap, in0=Rf.ap, in1=kvf.ap, op=ALU.add), r=[Rf, kvf], w=[Rf])
            E('dve', lambda e: e.tensor_scalar(out=Sf.ap, in0=Sf.ap, scalar1=rB[0:64, 3:4], scalar2=None, op0=ALU.mult),
              r=[Sf, rB], w=[Sf])
            E('dve', lambda e: e.tensor_scalar(out=mF.ap, in0=mF.ap, scalar1=rB[:, 3:4], scalar2=None, op0=ALU.mult),
              r=[mF, rB], w=[mF])
            E('act', lambda e: e.activation(out=V64('SF'), in_=Sf.ap, func=AF.Copy), r=[Sf], w=[p64])
            E('dve', lambda e: e.tensor_copy(out=srow[0:1, 0:4], in_=mF[0:1, :]), r=[mF], w=[srow])
            E('dve', lambda e: e.tensor_copy(out=srow[0:1, 4:8], in_=tot[0:1, 4:8]), r=[tot], w=[srow])
            E('dve', lambda e: e.tensor_copy(out=srow[0:1, 8:12], in_=tot[0:1, 12:16]), r=[tot], w=[srow])
            mlstm_update(mF, Sf, tot[:, 0:4], tot[:, 8:12], kvnf.ap, kvnf, [tot])
            B.dma(bv(ix(S64a, 0), 64, 1024), p64[:, 0:1024], tr=[p64])
            B.dma(bv(ix(SAK, 0), 64, 256), p64[:, 1024:1280], tr=[p64])
            B.dma(bv(ix(S64b, 0), 64, 1536), p64[:, 1280:2816], tr=[p64])
            B.dma(bv(ix(S64c, 0), 64, 1060), p64[:, 2816:3876], tr=[p64])
            B.dma(bv(ix(SMV, 0), 16, 130), mvb.ap, tr=[mvb])
            B.dma(bv(ix(SROW, 0), 1, 12), srow[0:1, 0:12], tr=[srow])

        def mlstm_update(m_t, S_t, totc, blc, kvn_ap, kvn_tile, extra_r):
            E('dve', lambda e: e.tensor_tensor(out=sc4[:, 0:4], in0=m_t.ap, in1=totc, op=ALU.max), r=[m_t] + extra_r, w=[sc4])
            E('dve', lambda e: e.tensor_tensor(out=sc4[:, 4:8], in0=m_t.ap, in1=sc4[:, 0:4], op=ALU.subtract), r=[m_t, sc4], w=[sc4])
            E('dve', lambda e: e.tensor_tensor(out=sc4[:, 8:12], in0=totc, in1=sc4[:, 0:4], op=ALU.subtract), r=extra_r + [sc4], w=[sc4])
            E('act', lambda e: e.activation(out=sc4[:, 4:12], in_=sc4[:, 4:12], func=AF.Exp), r=[sc4], w=[sc4])
            E('dve', lambda e: e.tensor_tensor(out=S_t.ap.rearrange("p (h d) -> p h d", d=65),
                                               in0=S_t.ap.rearrange("p (h d) -> p h d", d=65),
                                               in1=bc(sc4[0:64, 4:8].unsqueeze(2), [64, 4, 65]), op=ALU.mult), r=[S_t, sc4], w=[S_t])
            E('dve', lambda e: e.tensor_tensor(out=stmp.ap.rearrange("p (h d) -> p h d", d=65),
                                               in0=kvn_ap.rearrange("p (h d) -> p h d", d=65),
                                               in1=bc(sc4[0:64, 8:12].unsqueeze(2), [64, 4, 65]), op=ALU.mult),
              r=[kvn_tile, sc4], w=[stmp])
            E('dve', lambda e: e.tensor_tensor(out=S_t.ap, in0=S_t.ap, in1=stmp.ap, op=ALU.add), r=[S_t, stmp], w=[S_t])
            E('dve', lambda e: e.tensor_tensor(out=m_t.ap, in0=blc, in1=sc4[:, 0:4], op=ALU.add), r=extra_r + [sc4], w=[m_t])

        def bodyP(i):
            streams = []
            ixs = [lambda T, off, u=u: pairs(T, u + off, NI)[i][0] for u in range(2)]
            for u in range(2):
                CUR[0] = u
                B.rec = []
                stepP_main(ixs[u], u)
                streams.append(B.rec)
                B.rec = None
            CUR[0] = 0
            merged = []
            for a_, b_ in zip(*streams):
                merged.append(a_)
                merged.append(b_)
            B.play(merged)
            for u in range(2):
                CUR[0] = u
                stepP_state(ixs[u], u)
            CUR[0] = 0
            shift_win(256)

        B.loop(NI, bodyP)

        def bodyB(i):
            def two(T):
                return pairs(T, 0, NI)[(NI - 1) - i]
            B.dma(b64B.ap, two(SB64)[:, 0:64 * 516].rearrange("u (p c) -> p u c", c=516), tw=[b64B])
            B.dma(scbB.ap, bc(two(SROW)[:, 0:12].unsqueeze(0), [128, 2, 12]), tw=[scbB])
            B.dma(rinfB2.ap, two(RI)[:, 0:128 * 4].rearrange("u (p c) -> p u c", c=4), tw=[rinfB2])
            for idx in (1, 0):
                E('act', lambda e, idx=idx: e.activation(out=o64B[:, idx, 0:256], in_=Rb.ap, func=AF.Copy), r=[Rb], w=[o64B])
                E('act', lambda e, idx=idx: e.activation(out=o64B[:, idx, 256:516], in_=Sb.ap, func=AF.Copy), r=[Sb], w=[o64B])
                E('dve', lambda e, idx=idx: e.tensor_copy(out=mbrowB[0:1, idx, :], in_=mB[0:1, :]), r=[mB], w=[mbrowB])
                E('dve', lambda e: e.tensor_tensor(out=Rb.ap.rearrange("p (h d) -> p h d", d=64),
                                                   in0=Rb.ap.rearrange("p (h d) -> p h d", d=64),
                                                   in1=bc(der[0:64, 12:16].unsqueeze(2), [64, 4, 64]), op=ALU.mult), r=[Rb, der], w=[Rb])
                E('dve', lambda e, idx=idx: e.tensor_tensor(out=Rb.ap, in0=Rb.ap, in1=b64B[:, idx, 0:256], op=ALU.add), r=[Rb, b64B], w=[Rb])
                mlstm_update(mB, Sb, scbB[:, idx, 4:8], scbB[:, idx, 8:12], b64B[:, idx, 256:516], b64B, [scbB])
                E('dve', lambda e, idx=idx: e.tensor_scalar(out=Rb.ap, in0=Rb.ap, scalar1=rinfB2[0:64, idx, 3:4], scalar2=None, op0=ALU.mult), r=[Rb, rinfB2], w=[Rb])
                E('dve', lambda e, idx=idx: e.tensor_scalar(out=Sb.ap, in0=Sb.ap, scalar1=rinfB2[0:64, idx, 3:4], scalar2=None, op0=ALU.mult), r=[Sb, rinfB2], w=[Sb])
                E('dve', lambda e, idx=idx: e.tensor_scalar(out=mB.ap, in0=mB.ap, scalar1=rinfB2[:, idx, 3:4], scalar2=None, op0=ALU.mult), r=[mB, rinfB2], w=[mB])
            B.dma(two(SO64)[:, 0:64 * 516].rearrange("u (p c) -> p u c", c=516), o64B.ap, tr=[o64B])
            B.dma(two(SMB)[:, 0:4].unsqueeze(0), mbrowB.ap, tr=[mbrowB])

        B.loop(NI, bodyB)

        load_weight(Wout, [[(w_out_k[k], 0, D)] for k in range(8)], l, D, None)

        def Q64(n):
            a, b = P64[n]
            return q64.ap[:, a:b]

        def Q128(n):
            a, b = P128[n]
            return q128.ap[:, a:b]

        def QF(n):
            a, b = F128[n]
            return g128.ap[:, a:b]

        def stepF(ix, u):
            B.dma(hA.ap, ix(H, 1), tw=[hA])
            B.dma(q64[:, 0:1024], bv(ix(S64a, 1), 64, 1024), tw=[q64])
            B.dma(q64[:, 1280:2816], bv(ix(S64b, 1), 64, 1536), tw=[q64])
            B.dma(q64[:, 2816:3876], bv(ix(S64c, 1), 64, 1060), tw=[q64])
            B.dma(q128[:, 130:646], bv(ix(S128b, 1), 128, 516), tw=[q128])
            B.dma(g128.ap, bv(ix(SF128, 1), 128, F128W), tw=[g128])
            B.dma(mv16.ap, bv(ix(SMV, 1), 16, 130), tw=[mv16])
            B.dma(cmb.ap, bvb(ix(SCMT, 1), 1024, 128), tw=[cmb])
            B.dma(mpv[:, 0:4], bvb(ix(SROW, 1), 4, 128), tw=[mpv])
            B.dma(mpv[:, 4:8], bvb(ix(SMB, 1), 4, 128), tw=[mpv])
            B.dma(r64.ap, bv(ix(SO64, 1), 64, 516), tw=[r64])
            KM = [K[1], K[2], K[3], K[0]]
            KW = [K[3], K[0]]
            aqt = Q64('AQT').rearrange("p (h t) -> p h t", h=8)
            for kvh in range(2):
                ksrc = [kn[:, u + nb_, kvh * 128:(kvh + 1) * 128] for nb_ in range(3)]
                ktile = [kn, kn, kn]
                vsrc = [vn[:, u + nb_, kvh * 65:(kvh + 1) * 65] for nb_ in range(3)]
                vtile = [vn, vn, vn]
                qrhs = aqt[:, 4 * kvh:4 * kvh + 4, :]
                for nb in range(3):
                    E('pe', lambda e, nb=nb, ksrc=ksrc, qrhs=qrhs: e.matmul(K[nb].ap, ksrc[nb], qrhs, start=True, stop=True), r=[ktile[nb], q64], w=[K[nb]])
                E('pe', lambda e, kvh=kvh, qrhs=qrhs: e.matmul(K[3][0:16, :], Q64('MKM')[:, kvh * 16:(kvh + 1) * 16], qrhs, start=True, stop=True),
                  r=[q64], w=[K[3]])
                for nb in range(3):
                    if nb == 0:
                        E('act', lambda e, nb=nb: e.activation(out=stmpF.ap, in_=K[nb].ap, func=AF.Exp, bias=g128[:, 536:537]),
                          r=[K[nb], g128], w=[stmpF])
                    else:
                        E('act', lambda e, nb=nb: e.activation(out=stmpF.ap, in_=K[nb].ap, func=AF.Exp), r=[K[nb]], w=[stmpF])
                    E('pool', lambda e, nb=nb, kvh=kvh: e.tensor_tensor(out=ptb[nb].ap, in0=stmpF.ap, in1=Eb[:, kvh * 3 + nb, :],
                                                                        op=ALU.mult), r=[stmpF, Eb], w=[ptb[nb]])
                E('act', lambda e: e.activation(out=pmb.ap, in_=K[3][0:16, :], func=AF.Exp), r=[K[3]], w=[pmb])
                for g in range(4):
                    for nb in range(3):
                        E('pe', lambda e, g=g, nb=nb, vsrc=vsrc: e.matmul(K[0][:, g * 65:(g + 1) * 65], ptb[nb][:, g * 128:(g + 1) * 128],
                                                               vsrc[nb], start=(nb == 0), stop=False), r=[ptb[nb], vtile[nb]], w=[K[0]])
                    E('pe', lambda e, g=g, kvh=kvh: e.matmul(K[0][:, g * 65:(g + 1) * 65], pmb[:, g * 128:(g + 1) * 128],
                                                    mv16[:, kvh * 65:(kvh + 1) * 65], start=False, stop=True), r=[pmb, mv16], w=[K[0]])
                o3 = K[0][:, 0:260].rearrange("p (h d) -> p h d", d=65)
                E('dve', lambda e, o3=o3, kvh=kvh: e.tensor_tensor(out=sm8[:, 32:36], in0=o3[:, :, 64], in1=der[:, 32 + 4 * kvh:36 + 4 * kvh], op=ALU.add),
                  r=[K[0], der], w=[sm8])
                E('dve', lambda e: e.reciprocal(out=sm8[:, 36:40], in_=sm8[:, 32:36]), r=[sm8], w=[sm8])
                E('dve', lambda e, o3=o3, kvh=kvh: e.tensor_tensor(out=mixed[:, kvh * 256:(kvh + 1) * 256].rearrange("p (h d) -> p h d", d=64),
                                                   in0=o3[:, :, 0:64], in1=bc(sm8[:, 36:40].unsqueeze(2), [128, 4, 64]), op=ALU.mult),
                  r=[K[0], sm8], w=[mixed])
            rqt = Q64('RQT').rearrange("p (h t) -> p h t", h=4)
            rkt = Q64('RKT').rearrange("p (h t) -> p h t", h=4)
            for h in range(4):
                E('pe', lambda e, h=h: e.matmul(K[1][:, h * 128:(h + 1) * 128], rkt[:, h, :], rqt[:, h, :], start=True, stop=True),
                  r=[q64], w=[K[1]])
            E('dve', lambda e: e.tensor_tensor(out=wtr.ap, in0=K[1].ap, in1=DT.ap.rearrange("p h t -> p (h t)"), op=ALU.mult),
              r=[K[1], DT], w=[wtr])
            for h in range(4):
                E('pe', lambda e, h=h: e.matmul(K[2][:, h * 64:(h + 1) * 64], wtr[:, h * 128:(h + 1) * 128],
                                                Q128('RV')[:, h * 64:(h + 1) * 64], start=True, stop=True), r=[wtr, q128], w=[K[2]])
            for h in range(4):
                E('pe', lambda e, h=h: e.matmul(K[2][:, 256 + h * 64:256 + (h + 1) * 64], rqt[:, h, :],
                                                Q64('RF')[:, h * 64:(h + 1) * 64], start=True, stop=True), r=[q64], w=[K[2]])
            for h in range(4):
                E('pe', lambda e, h=h: e.matmul(K[3][:, h * 64:(h + 1) * 64], rqt[:, h, :],
                                                r64[:, h * 64:(h + 1) * 64], start=True, stop=True), r=[q64, r64], w=[K[3]])
            ro = oacc[:, 0:256].rearrange("p (h d) -> p h d", d=64)
            E('dve', lambda e: e.tensor_tensor(out=ro, in0=K[2][:, 256:512].rearrange("p (h d) -> p h d", d=64),
                                               in1=bc(der[:, 16:20].unsqueeze(2), [128, 4, 64]), op=ALU.mult), r=[K[2], der], w=[oacc])
            E('dve', lambda e: e.tensor_tensor(out=oacc[:, 0:256], in0=oacc[:, 0:256], in1=K[2][:, 0:256], op=ALU.add), r=[oacc, K[2]], w=[oacc])
            rb3 = obuf[:, 0:256].rearrange("p (h d) -> p h d", d=64)
            E('dve', lambda e: e.tensor_tensor(out=rb3, in0=K[3][:, 0:256].rearrange("p (h d) -> p h d", d=64),
                                               in1=bc(der[:, 20:24].unsqueeze(2), [128, 4, 64]), op=ALU.mult), r=[K[3], der], w=[obuf])
            E('dve', lambda e: e.tensor_tensor(out=oacc[:, 0:256], in0=oacc[:, 0:256], in1=obuf[:, 0:256], op=ALU.add), r=[oacc, obuf], w=[oacc])
            headnorm(ro, mixed[:, 512:768].rearrange("p (h d) -> p h d", d=64),
                     par[:, 144:400].rearrange("p (h d) -> p h d", d=64), QF('RG').rearrange("p (h d) -> p h d", d=64),
                     [oacc], [mixed])
            mqt = Q64('MQT').rearrange("p (h t) -> p h t", h=4)
            mkt = Q64('MKT').rearrange("p (h t) -> p h t", h=4)
            GT = QF('GT')
            for h in range(4):
                E('pe', lambda e, h=h: e.matmul(K[0][:, h * 128:(h + 1) * 128], mkt[:, h, :], mqt[:, h, :], start=True, stop=True),
                  r=[q64], w=[K[0]])
            cm4 = cmb.ap.rearrange("p (h d t) -> p h d t", h=4, d=2)
            mp3 = mpv.ap.rearrange("p (d h) -> p h d", d=2)
            E('dve', lambda e: e.tensor_tensor(out=cm4, in0=cm4, in1=bc(mp3.unsqueeze(3), [128, 4, 2, 128]), op=ALU.max),
              r=[cmb, mpv], w=[cmb])
            for h in range(4):
                for d_ in range(2):
                    E('dve', lambda e, h=h, d_=d_: e.tensor_scalar(out=cm4[:, h, d_, :], in0=cm4[:, h, d_, :],
                                                                    scalar1=GT[:, d_ * 4 + h:d_ * 4 + h + 1], scalar2=0.0,
                                                                    op0=ALU.subtract, op1=ALU.max), r=[cmb, g128], w=[cmb])
            E('act', lambda e: e.activation(out=cmb.ap, in_=cmb.ap, func=AF.Exp, scale=-1.0), r=[cmb], w=[cmb])
            msk = cst[:, C_TF:C_TF + 256].rearrange("p (d t) -> p d t", d=2)
            E('pool', lambda e: e.tensor_tensor(out=cm4, in0=cm4, in1=bc(msk.unsqueeze(1), [128, 4, 2, 128]), op=ALU.mult),
              r=[cmb, cst], w=[cmb])
            E('dve', lambda e: e.tensor_tensor(out=wtm.ap.rearrange("p (h d t) -> p h d t", h=4, d=2), in0=cm4,
                                               in1=bc(K[0].ap.rearrange("p (h t) -> p h t", h=4).unsqueeze(2), [128, 4, 2, 128]),
                                               op=ALU.mult), r=[cmb, K[0]], w=[wtm])
            wt4 = wtm.ap.rearrange("p (h d t) -> p h d t", h=4, d=2)
            for d_ in range(2):
                for h in range(4):
                    E('pe', lambda e, d_=d_, h=h: e.matmul(KM[d_][:, h * 65:(h + 1) * 65], wt4[:, h, d_, :],
                                                           Q128('MVA')[:, h * 65:(h + 1) * 65], start=True, stop=True),
                      r=[wtm, q128], w=[KM[d_]])
                for h in range(4):
                    st_src = Q64('SF')[:, h * 65:(h + 1) * 65] if d_ == 0 else r64[:, 256 + h * 65:256 + (h + 1) * 65]
                    E('pe', lambda e, d_=d_, h=h, st_src=st_src: e.matmul(KM[2 + d_][:, h * 65:(h + 1) * 65], mqt[:, h, :], st_src,
                                                                           start=True, stop=True), r=[q64, r64], w=[KM[2 + d_]])
            E('dve', lambda e: e.tensor_tensor(out=sm8[:, 40:48], in0=GT[:, 16:24], in1=mpv.ap, op=ALU.max), r=[g128, mpv], w=[sm8])
            E('dve', lambda e: e.tensor_tensor(out=sm8[:, 48:56], in0=mpv.ap, in1=sm8[:, 40:48], op=ALU.subtract), r=[mpv, sm8], w=[sm8])
            E('dve', lambda e: e.scalar_tensor_tensor(out=sm8[:, 56:64], in0=GT[:, 8:16], scalar=-1.0, in1=sm8[:, 40:48],
                                                      op0=ALU.mult, op1=ALU.subtract), r=[g128, sm8], w=[sm8])
            E('act', lambda e: e.activation(out=sm8[:, 48:64], in_=sm8[:, 48:64], func=AF.Exp), r=[sm8], w=[sm8])
            for d_ in range(2):
                nd = obuf[:, 0:260].rearrange("p (h d) -> p h d", d=65)
                E('dve', lambda e, d_=d_: e.tensor_tensor(out=nd, in0=KM[2 + d_][:, 0:260].rearrange("p (h d) -> p h d", d=65),
                                                          in1=bc(sm8[:, 48 + 4 * d_:52 + 4 * d_].unsqueeze(2), [128, 4, 65]), op=ALU.mult),
                  r=[KM[2 + d_], sm8], w=[obuf])
                E('dve', lambda e, d_=d_: e.tensor_tensor(out=obuf[:, 0:260], in0=obuf[:, 0:260], in1=KM[d_][:, 0:260], op=ALU.add),
                  r=[obuf, KM[d_]], w=[obuf])
                E('dve', lambda e, d_=d_: e.scalar_tensor_tensor(out=sm8[:, 20:24], in0=nd[:, :, 64], scalar=-1.0, in1=nd[:, :, 64],
                                                                 op0=ALU.mult, op1=ALU.max), r=[obuf], w=[sm8])
                E('dve', lambda e, d_=d_: e.tensor_tensor(out=sm8[:, 24:28], in0=sm8[:, 20:24], in1=sm8[:, 56 + 4 * d_:60 + 4 * d_], op=ALU.max),
                  r=[sm8], w=[sm8])
                E('dve', lambda e: e.reciprocal(out=sm8[:, 28:32], in_=sm8[:, 24:28]), r=[sm8], w=[sm8])
                hm = oacc[:, 260:516].rearrange("p (h d) -> p h d", d=64)
                if d_ == 0:
                    E('dve', lambda e: e.tensor_tensor(out=hm, in0=nd[:, :, 0:64], in1=bc(sm8[:, 28:32].unsqueeze(2), [128, 4, 64]), op=ALU.mult),
                      r=[obuf, sm8], w=[oacc])
                else:
                    E('dve', lambda e: e.tensor_tensor(out=nd[:, :, 0:64], in0=nd[:, :, 0:64], in1=bc(sm8[:, 28:32].unsqueeze(2), [128, 4, 64]), op=ALU.mult),
                      r=[obuf, sm8], w=[obuf])
                    E('dve', lambda e: e.tensor_tensor(out=hm, in0=hm, in1=nd[:, :, 0:64], op=ALU.add), r=[oacc, obuf], w=[oacc])
            hm = oacc[:, 260:516].rearrange("p (h d) -> p h d", d=64)
            headnorm(hm, mixed[:, 768:1024].rearrange("p (h d) -> p h d", d=64),
                     par[:, 400:656].rearrange("p (h d) -> p h d", d=64), QF('MO').rearrange("p (h d) -> p h d", d=64),
                     [oacc], [mixed])
            for half in range(2):
                for kk in range(4):
                    k = half * 4 + kk
                    E('pe', lambda e, k=k, kk=kk, half=half: e.transpose(K[1 + half][:, kk * 128:(kk + 1) * 128],
                                                                          mixed[:, k * 128:(k + 1) * 128], ident), r=[mixed, cst], w=[K[1 + half]])
                E('act', lambda e, half=half: e.activation(out=mixT[:, half * 4:half * 4 + 4, :],
                                                           in_=K[1 + half].ap.rearrange("p (k c) -> p k c", k=4), func=AF.Copy),
                  r=[K[1 + half]], w=[mixT])
            for n in range(2):
                for k in range(8):
                    E('pe', lambda e, n=n, k=k: e.matmul(KW[n].ap, mixT[:, k, :], Wout[:, k, n * 512:(n + 1) * 512],
                                                         start=(k == 0), stop=(k == 7)), r=[mixT, wdn], w=[KW[n]])
                E('dve', lambda e, n=n: e.tensor_tensor(out=hA[:, n * 512:(n + 1) * 512], in0=hA[:, n * 512:(n + 1) * 512],
                                                        in1=KW[n].ap, op=ALU.add), r=[hA, KW[n]], w=[hA])
            B.dma(ix(H, 1), hA.ap, tr=[hA])

        for kvh in range(2):
            for nb in range(3):
                for g in range(4):
                    slope = float(2.0 ** (-8.0 * (4 * kvh + g + 1) / 8.0))
                    E('act', lambda e, kvh=kvh, nb=nb, g=g, slope=slope: e.activation(
                        out=Eb[:, kvh * 3 + nb, g * 128:(g + 1) * 128], in_=cst[:, C_DIST + nb * 128:C_DIST + (nb + 1) * 128],
                        func=AF.Exp, scale=-slope), r=[cst], w=[Eb])
        B.dma(kn[:, 2, :], bv(SAK[0], 64, 256), tw=[kn])
        B.dma(kn[:, 3, :], bv(SAK[1], 64, 256), tw=[kn])
        B.dma(vn[:, 2, :], bv(SAV[0], 128, 130), tw=[vn])
        B.dma(vn[:, 3, :], bv(SAV[1], 128, 130), tw=[vn])

        def bodyF(i):
            for sl in range(2):
                E('pool', lambda e, sl=sl: e.tensor_copy(out=kn[:, sl, :], in_=kn[:, sl + 2, :]), r=[kn], w=[kn])
                E('pool', lambda e, sl=sl: e.tensor_copy(out=vn[:, sl, :], in_=vn[:, sl + 2, :]), r=[vn], w=[vn])
            for sl in range(2):
                B.dma(kn[:, 2 + sl, :], bv(pairs(SAK, 2 + sl, NJ)[i][0], 64, 256), tw=[kn])
                B.dma(vn[:, 2 + sl, :], bv(pairs(SAV, 2 + sl, NJ)[i][0], 128, 130), tw=[vn])
            streams = []
            for u in range(2):
                CUR[0] = u
                B.rec = []
                stepF(lambda T, off, u=u: pairs(T, u + off, NJ)[i][0], u)
                streams.append(B.rec)
                B.rec = None
            CUR[0] = 0
            merged = []
            if os.environ.get("MERGE", "alt") == "seq":
                merged = streams[0] + streams[1]
            else:
                for a_, b_ in zip(*streams):
                    merged.append(a_)
                    merged.append(b_)
            B.play(merged)

        B.loop(NJ, bodyF)

        load_weight(Wup, [[(ffn_up_k[k][0], 0, DFF), (ffn_up_k[k][1], DFF, 2 * DFF)] for k in range(8)], l, 2 * DFF, n2w)
        load_weight(Wdn, [[(ffn_down_k[k], 0, D)] for k in range(22)], l, D, None)
        E('pool', lambda e: e.memset(win.ap, 0.0), w=[win])
        stageA(H[0], RI[0], 257)
        shift_win(256)

        def bodyG(i):
            cur = lambda T, u: pairs(T, 0, NJ)[i][u]
            nxt = lambda T, u: pairs(T, 1, NJ)[i][u]
            stageA(nxt(H, 0), nxt(RI, 0), 129)
            stageA(nxt(H, 1), nxt(RI, 1), 257)
            for cc in range(22):
                banks = (PP[2 * (cc % 2)], PP[2 * (cc % 2) + 1])
                for gv in range(2):
                    for k in range(8):
                        E('pe', lambda e, cc=cc, gv=gv, k=k: e.matmul(
                            banks[gv][:, 0:258], Wup[:, k, gv * DFF + cc * 128:gv * DFF + (cc + 1) * 128],
                            win[:, k, 0:258], start=(k == 0), stop=(k == 7)), r=[wreg, win], w=[banks[gv]])
                for gv in range(2):
                    ch = gv * 22 + cc
                    xb = banks[gv]
                    E('act', lambda e, gv=gv, ch=ch, xb=xb: e.activation(out=cg[:, gv, :], in_=xb[:, 0:256],
                                                                          func=AF.Copy, scale=cwf[:, 0, ch:ch + 1]), r=[xb, cwf], w=[cg])
                    for tap in (1, 2):
                        E('dve', lambda e, gv=gv, ch=ch, xb=xb, tap=tap: e.scalar_tensor_tensor(
                            out=cg[:, gv, :], in0=xb[:, tap:tap + 256], scalar=cwf[:, tap, ch:ch + 1],
                            in1=cg[:, gv, :], op0=ALU.mult, op1=ALU.add), r=[xb, cwf, cg], w=[cg])
                E('act', lambda e: e.activation(out=sg.ap, in_=cg[:, 0, :], func=AF.Silu), r=[cg], w=[sg])
                E('pool', lambda e, cc=cc: e.tensor_tensor(out=actT[:, cc, :], in0=sg.ap, in1=cg[:, 1, :], op=ALU.mult),
                  r=[sg, cg], w=[actT])
            for u in range(2):
                dbank = (X[0], X[1]) if u == 0 else (X[2], PP[4])
                for n in range(2):
                    for cc in range(22):
                        E('pe', lambda e, n=n, cc=cc, u=u: e.matmul(dbank[n].ap, actT[:, cc, u * 128:(u + 1) * 128],
                                                                    Wdn[:, cc, n * 512:(n + 1) * 512],
                                                                    start=(cc == 0), stop=(cc == 21)), r=[actT, wdn], w=[dbank[n]])
                B.dma(hA.ap, cur(H, u), tw=[hA])
                for n in range(2):
                    E('dve', lambda e, n=n: e.tensor_tensor(out=hA[:, n * 512:(n + 1) * 512], in0=hA[:, n * 512:(n + 1) * 512],
                                                            in1=dbank[n].ap, op=ALU.add), r=[hA, dbank[n]], w=[hA])
                B.dma(cur(H, u), hA.ap, tr=[hA])
            shift_win(256)

        B.loop(NJ, bodyG)

    B.seg_end()
    with nc.Fori(0, depth) as l:
        layer(l)
        B.seg_end()
    B.seg_end()
    return B


def make_consts():
    c = np.zeros((128, C_W), np.float32)
    s = np.arange(128)[:, None].astype(np.float64)
    t = np.arange(128)[None, :].astype(np.float64)
    c[:, C_ID:C_ID + 128] = np.eye(128)
    c[:, C_TF:C_TF + 128] = (s <= t)
    c[:, C_TB:C_TB + 128] = (s >= t)
    c[:, C_DF:C_DF + 128] = np.maximum(t - s, 0)
    c[:, C_DB:C_DB + 128] = np.maximum(s - t, 0)
    for nb in range(3):
        kpos = (nb - 1) * 128 + s
        dist = np.abs(t - kpos)
        c[:, C_DIST + nb * 128:C_DIST + (nb + 1) * 128] = np.where(dist <= 128, dist, 1.0e6)
    sv = np.arange(128)
    c[:, C_RIDX + 0] = sv + 1
    c[:, C_RIDX + 1] = 128 - sv
    c[:, C_RIDX + 2] = 127 - sv
    c[:, C_RIDX + 3] = sv
    for m in range(16):
        c[112 + m, C_SEL + m] = 1.0
    return c


def layout_core(seqs, meta, NB):
    NBA = NB + 6
    xin = np.zeros((NBA, 128, D), np.float32)
    ri = np.zeros((NBA, 128, 4), np.float32)
    ri[:, :, 2] = 1.0
    rowmap = []
    b = 1
    for x in seqs:
        nb = x.shape[0] // 128
        xin[b, 112:128] = meta
        ri[b, 112:128, 0] = 1.0
        ri[b, :, 2] = 1.0
        xin[b + 1:b + 1 + nb] = x.reshape(nb, 128, D)
        ri[b + 1:b + 1 + nb, :, 0] = 1.0
        ri[b + 1:b + 1 + nb, :, 1] = 1.0
        ri[b + 1:b + 1 + nb, :, 2] = 0.0
        rowmap.append((b + 1, nb))
        b += 1 + nb
    assert b <= NB + 1
    ri[:, :, 3] = 1.0 - ri[:, :, 2]
    rmrow = np.ascontiguousarray(ri[:, :, 0].reshape(NBA, 1, 128))
    return xin, ri, rmrow, rowmap


_CACHE = {}


def get_program(NB, depth):
    key = (NB, depth)
    if key not in _CACHE:
        _CACHE[key] = build(NB, depth).nc
    return _CACHE[key]


def kernel(x_prompt, x_sample, meta_tokens, norm1_w, w_in, attn_q_norm_w, attn_k_norm_w, attn_sink,
           ret_decay_logit, ret_norm_w, mlstm_conv_w, mlstm_gate_b, mlstm_norm_w, w_out, norm2_w,
           ffn_up, ffn_conv_w, ffn_down):
    f = lambda a: np.ascontiguousarray(np.asarray(a, dtype=np.float32))
    x_prompt, x_sample, meta = f(x_prompt), f(x_sample), f(meta_tokens)
    depth = int(np.asarray(norm1_w).shape[0])
    Bp, Sp, _ = x_prompt.shape
    Bs, Ss, _ = x_sample.shape
    assert Sp == 2 * Ss and Bs % 2 == 0
    NB = 2 * (1 + Ss // 128)
    cores = [[x_prompt[i]] for i in range(Bp)] + [[x_sample[2 * i], x_sample[2 * i + 1]] for i in range(Bs // 2)]
    ncore = len(cores)
    assert ncore <= 8
    nc = get_program(NB, depth)
    consts = make_consts()
    LP = 16896
    LW1 = 128 * PW
    SBk = 128 * D

    def padrow(a, L):
        a = f(a).reshape(depth, -1)
        o = np.zeros((depth, L), np.float32)
        o[:, :a.shape[1]] = a
        return o
    shared = {"consts": consts}
    shared["psmall"] = padrow(np.concatenate([f(attn_q_norm_w), f(attn_k_norm_w), f(attn_sink),
                                              f(ret_decay_logit).reshape(depth, 8), f(ret_norm_w), f(mlstm_norm_w),
                                              f(mlstm_gate_b)], axis=1), LP)
    shared["pn1"] = padrow(norm1_w, LP)
    shared["pn2"] = padrow(norm2_w, LP)
    mc, fc = f(mlstm_conv_w), f(ffn_conv_w)
    for k in range(3):
        shared["pcw%d" % k] = padrow(mc[:, k], LP)
        shared["pfc%d" % k] = padrow(fc[:, k], LP)
    wi, wo, fu, fd = f(w_in), f(w_out), f(ffn_up), f(ffn_down)
    for k in range(8):
        shared["w_in_k%d" % k] = padrow(wi[:, k * 128:(k + 1) * 128, :], LW1)
        shared["w_out_k%d" % k] = padrow(wo[:, k * 128:(k + 1) * 128, :], SBk)
        for hf in range(2):
            shared["ffn_up_k%d_%d" % (k, hf)] = padrow(fu[:, k * 128:(k + 1) * 128, hf * DFF:(hf + 1) * DFF], LW1)
    for k in range(22):
        shared["ffn_down_k%d" % k] = padrow(fd[:, k * 128:(k + 1) * 128, :], SBk)
    in_maps, maps = [], []
    for seqs in cores:
        xin, ri, rmrow, rowmap = layout_core(seqs, meta, NB)
        m = dict(shared)
        m.update({"xin": xin, "rowinfo": ri, "rmrow": rmrow})
        in_maps.append(m)
        maps.append(rowmap)
    while len(in_maps) < 8:
        in_maps.append(in_maps[-1])
    res = run_bass_kernel_spmd(nc, in_maps, core_ids=list(range(len(in_maps))))
    y_p = np.zeros_like(x_prompt)
    y_s = np.zeros_like(x_sample)
    for ci in range(ncore):
        h = res.results[ci]["hout"]
        if ci < Bp:
            b0, nb = maps[ci][0]
            y_p[ci] = h[b0:b0 + nb].reshape(nb * 128, D)
        else:
            for j, (b0, nb) in enumerate(maps[ci]):
                y_s[2 * (ci - Bp) + j] = h[b0:b0 + nb].reshape(nb * 128, D)
    return (y_p, y_s)
```
